# Optimizing a Trainium2 kernel written in Bass

```python
import jax
import jax.numpy as jnp
from jax import lax
import numpy as np

D_MODEL = 1024
BATCH = 8
SEQ = 4096
DEPTH = 2
DEC_BATCH = 8
DEC_SEQ = 8192
PAST_LEN = 128

N_EVEN = (DEPTH + 1) // 2
N_ODD = DEPTH // 2
D_FF = 2816
MIX_A = D_MODEL // 2
GM_GROUPS = 4
GM_CH = MIX_A // GM_GROUPS
GM_CHUNK = 128
MIX_B = D_MODEL // 2
RG_BLOCKS = 8
RG_BS = MIX_B // RG_BLOCKS
CONV_W = 4
RG_C = 8.0
ATT_HEADS = 16
ATT_HD = D_MODEL // ATT_HEADS
ROT_DIM = ATT_HD // 4
ROPE_THETA = 500000.0
DIL_PATTERNS = ((128, 1), (512, 4), (2048, 16))
N_MEM = 256
XA_HEADS = 4
XA_HD = D_MODEL // XA_HEADS
EPS = 1e-6
NEG_INF = -1e30

kernel_name = 'hybrid_bidir_encoder_two_batch'

F32 = jnp.float32


def rms_norm(x, g):
    xf = x.astype(F32)
    y = xf * lax.rsqrt(jnp.mean(xf * xf, axis=-1, keepdims=True) + EPS)
    return (y * g.astype(F32)).astype(x.dtype)


def layer_norm(x, g, b):
    xf = x.astype(F32)
    mu = jnp.mean(xf, axis=-1, keepdims=True)
    var = jnp.mean(jnp.square(xf - mu), axis=-1, keepdims=True)
    y = (xf - mu) * lax.rsqrt(var + EPS)
    return (y * g.astype(F32) + b.astype(F32)).astype(x.dtype)


def swiglu(h, w_in, w_out):
    gate, up = jnp.split(h @ w_in, 2, axis=-1)
    return (jax.nn.silu(gate) * up) @ w_out


def rope_partial(x, pos):
    half = ROT_DIM // 2
    inv = jnp.power(jnp.float32(ROPE_THETA), -jnp.arange(half, dtype=F32) * (2.0 / ROT_DIM))
    ang = pos[:, None] * inv[None, :]
    cos = jnp.cos(ang)[None, :, None, :]
    sin = jnp.sin(ang)[None, :, None, :]
    xr = x[..., :ROT_DIM].astype(F32)
    x1, x2 = xr[..., :half], xr[..., half:]
    rot = jnp.concatenate([x1 * cos - x2 * sin, x2 * cos + x1 * sin], axis=-1).astype(x.dtype)
    return jnp.concatenate([rot, x[..., ROT_DIM:]], axis=-1)


def gmlp_spatial(z_u, z_v, ln_g, ln_b, w_s, b_s):
    u = jax.nn.gelu(z_u)
    v = layer_norm(jax.nn.gelu(z_v), ln_g, ln_b)
    B, S, _ = v.shape
    nc = S // GM_CHUNK
    vc = v.reshape(B, nc, GM_CHUNK, GM_GROUPS, GM_CH)
    s = jnp.einsum('gpq,bnqgc->bnpgc', w_s, vc) + b_s.T[:, :, None]
    return u * s.reshape(B, S, MIX_A)


def centered_dwconv(x, w, b):
    S = x.shape[1]
    lpad = CONV_W // 2
    xp = jnp.pad(x, ((0, 0), (lpad, CONV_W - 1 - lpad), (0, 0)))
    y = sum(xp[:, k:k + S] * w[k] for k in range(CONV_W))
    return y + b


def _lin_combine(c1, c2):
    a1, b1 = c1
    a2, b2 = c2
    return a1 * a2, a2 * b1 + b2


def rg_lru_bidir(x, w_a, b_a, w_i, b_i, lam):
    B, S, _ = x.shape
    xb = x.reshape(B, S, RG_BLOCKS, RG_BS)
    total = jnp.zeros((B, S, MIX_B), F32)
    for d in range(2):
        r = jax.nn.sigmoid(jnp.einsum('bshi,hij->bshj', xb, w_a[d]).reshape(B, S, MIX_B) + b_a[d])
        i = jax.nn.sigmoid(jnp.einsum('bshi,hij->bshj', xb, w_i[d]).reshape(B, S, MIX_B) + b_i[d])
        log_a = (-RG_C * r.astype(F32)) * jax.nn.softplus(-lam[d].astype(F32))
        a = jnp.exp(log_a)
        u = x.astype(F32) * i.astype(F32) * jnp.sqrt(-jnp.expm1(2.0 * log_a))
        _, h = lax.associative_scan(_lin_combine, (a, u), reverse=(d == 1), axis=1)
        total = total + h
    return total.astype(x.dtype)


def even_mixer(h, w_in, w_out, ln_g, ln_b, w_s, b_s, conv_w, conv_b, w_a, b_a, w_i, b_i, lam):
    z = h @ w_in
    z_u, z_v, z_x, z_g = jnp.split(z, 4, axis=-1)
    a_out = gmlp_spatial(z_u, z_v, ln_g, ln_b, w_s, b_s)
    xc = centered_dwconv(z_x, conv_w, conv_b)
    b_out = rg_lru_bidir(xc, w_a, b_a, w_i, b_i, lam) * jax.nn.gelu(z_g)
    return jnp.concatenate([a_out, b_out], axis=-1) @ w_out


def dilated_window_attn(q, k, v, window, dil):
    B, S, H, E = q.shape
    half = window // (2 * dil)
    blk = half
    L = -(-S // (dil * blk)) * blk
    S_pad = L * dil
    nb = L // blk

    def to_blocks(t):
        t = jnp.pad(t, ((0, 0), (0, S_pad - S), (0, 0), (0, 0)))
        return t.reshape(B, nb, blk, dil, H, E)

    def neighbours(t):
        tb = jnp.pad(to_blocks(t), ((0, 0), (1, 1), (0, 0), (0, 0), (0, 0), (0, 0)))
        return jnp.concatenate([tb[:, :-2], tb[:, 1:-1], tb[:, 2:]], axis=2)

    qb = to_blocks(q)
    kn = neighbours(k)
    vn = neighbours(v)
    s_idx = jnp.arange(blk)
    t_idx = jnp.arange(3 * blk)
    band = jnp.abs((t_idx[None, :] - blk) - s_idx[:, None]) <= half
    m_k = (jnp.arange(nb)[:, None] - 1) * blk + t_idx[None, :]
    pos_k = m_k[:, :, None] * dil + jnp.arange(dil)[None, None, :]
    valid = (m_k[:, :, None] >= 0) & (pos_k < S)
    mask = band[None, None, :, :] & jnp.transpose(valid, (0, 2, 1))[:, :, None, :]

    scores = jnp.einsum('bjsrhe,bjtrhe->bjrhst', qb, kn).astype(F32)
    scores = jnp.where(mask[None, :, :, None, :, :], scores, NEG_INF)
    m = jnp.max(scores, axis=-1, keepdims=True)
    p = jnp.exp(scores - m)
    den = jnp.sum(p, axis=-1)
    den_t = jnp.transpose(den, (0, 1, 4, 2, 3))
    o = jnp.einsum('bjrhst,bjtrhe->bjsrhe', p.astype(v.dtype), vn).astype(F32) / den_t[..., None]
    lse = jnp.transpose(m[..., 0], (0, 1, 4, 2, 3)) + jnp.log(den_t)
    o = o.reshape(B, S_pad, H, E)[:, :S]
    lse = lse.reshape(B, S_pad, H)[:, :S]
    return o, lse


def odd_mixer(h, w_in, w_out, q_g, k_g):
    B, S, _ = h.shape
    q, k, v = jnp.split(h @ w_in, 3, axis=-1)
    q = q.reshape(B, S, ATT_HEADS, ATT_HD)
    k = k.reshape(B, S, ATT_HEADS, ATT_HD)
    v = v.reshape(B, S, ATT_HEADS, ATT_HD)
    pos = jnp.arange(S, dtype=F32)
    q = rope_partial(rms_norm(q, q_g), pos) * (ATT_HD ** -0.5)
    k = rope_partial(rms_norm(k, k_g), pos)
    outs = []
    lses = []
    for window, dil in DIL_PATTERNS:
        o_i, l_i = dilated_window_attn(q, k, v, window, dil)
        outs.append(o_i)
        lses.append(l_i)
    wts = jax.nn.softmax(jnp.stack(lses, axis=0), axis=0)
    o = sum(wts[i][..., None] * outs[i] for i in range(len(DIL_PATTERNS)))
    return o.astype(h.dtype).reshape(B, S, D_MODEL) @ w_out


def cross_attn(h, memn, w_q, w_kv, w_o, q_g, k_g):
    B, S, _ = h.shape
    M = memn.shape[1]
    q = rms_norm((h @ w_q).reshape(B, S, XA_HEADS, XA_HD), q_g)
    k, v = jnp.split(memn @ w_kv, 2, axis=-1)
    k = rms_norm(k.reshape(B, M, XA_HEADS, XA_HD), k_g)
    v = v.reshape(B, M, XA_HEADS, XA_HD)
    s = jnp.einsum('bshe,bmhe->bhsm', q, k).astype(F32) * (XA_HD ** -0.5)
    p = jax.nn.softmax(s, axis=-1).astype(v.dtype)
    o = jnp.einsum('bhsm,bmhe->bshe', p, v).reshape(B, S, D_MODEL)
    return o @ w_o


def trunk(x, mem, P):
    for l in range(DEPTH):
        x = x + 0.5 * swiglu(rms_norm(x, P['ffn1_norm'][l]), P['ffn1_w_in'][l], P['ffn1_w_out'][l])
        h = rms_norm(x, P['mix_norm'][l])
        e = l // 2
        if l % 2 == 0:
            x = x + even_mixer(h, P['ev_w_in'][e], P['ev_w_out'][e], P['gm_ln_g'][e], P['gm_ln_b'][e],
                               P['gm_w_s'][e], P['gm_b_s'][e], P['rg_conv_w'][e], P['rg_conv_b'][e],
                               P['rg_w_a'][e], P['rg_b_a'][e], P['rg_w_i'][e], P['rg_b_i'][e], P['rg_lam'][e])
        else:
            x = x + odd_mixer(h, P['od_w_in'][e], P['od_w_out'][e], P['od_q_norm'][e], P['od_k_norm'][e])
        x = x + cross_attn(rms_norm(x, P['xa_norm'][l]), rms_norm(mem, P['xa_mem_norm'][l]),
                           P['xa_w_q'][l], P['xa_w_kv'][l], P['xa_w_o'][l], P['xa_q_norm'][l], P['xa_k_norm'][l])
        x = x + 0.5 * swiglu(rms_norm(x, P['ffn2_norm'][l]), P['ffn2_w_in'][l], P['ffn2_w_out'][l])
    return x


def setup_inputs(seed: int = 0) -> dict:
    key = jax.random.key(seed)
    ks = iter(jax.random.split(key, 48))
    D = D_MODEL

    def w(shape, fan_in):
        return jax.random.normal(next(ks), shape, F32) * (fan_in ** -0.5)

    def gain(shape):
        return 1.0 + 0.02 * jax.random.normal(next(ks), shape, F32)

    def bias(shape):
        return 0.02 * jax.random.normal(next(ks), shape, F32)

    out = {}
    out['x_prompt'] = jax.random.normal(next(ks), (BATCH, SEQ, D), F32)
    out['x_sample'] = jax.random.normal(next(ks), (DEC_BATCH, DEC_SEQ, D), F32)
    out['mem_prompt'] = jax.random.normal(next(ks), (BATCH, N_MEM, D), F32)
    out['mem_sample'] = jax.random.normal(next(ks), (DEC_BATCH, N_MEM, D), F32)
    out['ffn1_norm'] = gain((DEPTH, D))
    out['ffn1_w_in'] = w((DEPTH, D, 2 * D_FF), D)
    out['ffn1_w_out'] = w((DEPTH, D_FF, D), D_FF)
    out['mix_norm'] = gain((DEPTH, D))
    out['ev_w_in'] = w((N_EVEN, D, 2 * MIX_A + 2 * MIX_B), D)
    out['ev_w_out'] = w((N_EVEN, MIX_A + MIX_B, D), MIX_A + MIX_B)
    out['gm_ln_g'] = gain((N_EVEN, MIX_A))
    out['gm_ln_b'] = bias((N_EVEN, MIX_A))
    out['gm_w_s'] = w((N_EVEN, GM_GROUPS, GM_CHUNK, GM_CHUNK), GM_CHUNK)
    out['gm_b_s'] = gain((N_EVEN, GM_GROUPS, GM_CHUNK))
    out['rg_conv_w'] = w((N_EVEN, CONV_W, MIX_B), CONV_W)
    out['rg_conv_b'] = bias((N_EVEN, MIX_B))
    out['rg_w_a'] = w((N_EVEN, 2, RG_BLOCKS, RG_BS, RG_BS), RG_BS)
    out['rg_b_a'] = bias((N_EVEN, 2, MIX_B))
    out['rg_w_i'] = w((N_EVEN, 2, RG_BLOCKS, RG_BS, RG_BS), RG_BS)
    out['rg_b_i'] = bias((N_EVEN, 2, MIX_B))
    lam_u = jax.random.uniform(next(ks), (N_EVEN, 2, MIX_B), F32, 0.9, 0.999)
    base = lam_u ** (1.0 / RG_C)
    out['rg_lam'] = jnp.log(base) - jnp.log1p(-base)
    out['od_w_in'] = w((N_ODD, D, 3 * D), D)
    out['od_w_out'] = w((N_ODD, D, D), D)
    out['od_q_norm'] = gain((N_ODD, ATT_HD))
    out['od_k_norm'] = gain((N_ODD, ATT_HD))
    out['xa_norm'] = gain((DEPTH, D))
    out['xa_mem_norm'] = gain((DEPTH, D))
    out['xa_w_q'] = w((DEPTH, D, D), D)
    out['xa_w_kv'] = w((DEPTH, D, 2 * D), D)
    out['xa_w_o'] = w((DEPTH, D, D), D)
    out['xa_q_norm'] = gain((DEPTH, XA_HD))
    out['xa_k_norm'] = gain((DEPTH, XA_HD))
    out['ffn2_norm'] = gain((DEPTH, D))
    out['ffn2_w_in'] = w((DEPTH, D, 2 * D_FF), D)
    out['ffn2_w_out'] = w((DEPTH, D_FF, D), D_FF)
    return out


def reference(x_prompt, x_sample, mem_prompt, mem_sample,
              ffn1_norm, ffn1_w_in, ffn1_w_out, mix_norm,
              ev_w_in, ev_w_out, gm_ln_g, gm_ln_b, gm_w_s, gm_b_s,
              rg_conv_w, rg_conv_b, rg_w_a, rg_b_a, rg_w_i, rg_b_i, rg_lam,
              od_w_in, od_w_out, od_q_norm, od_k_norm,
              xa_norm, xa_mem_norm, xa_w_q, xa_w_kv, xa_w_o, xa_q_norm, xa_k_norm,
              ffn2_norm, ffn2_w_in, ffn2_w_out):
    P = dict(ffn1_norm=ffn1_norm, ffn1_w_in=ffn1_w_in, ffn1_w_out=ffn1_w_out, mix_norm=mix_norm,
             ev_w_in=ev_w_in, ev_w_out=ev_w_out, gm_ln_g=gm_ln_g, gm_ln_b=gm_ln_b,
             gm_w_s=gm_w_s, gm_b_s=gm_b_s, rg_conv_w=rg_conv_w, rg_conv_b=rg_conv_b,
             rg_w_a=rg_w_a, rg_b_a=rg_b_a, rg_w_i=rg_w_i, rg_b_i=rg_b_i, rg_lam=rg_lam,
             od_w_in=od_w_in, od_w_out=od_w_out, od_q_norm=od_q_norm, od_k_norm=od_k_norm,
             xa_norm=xa_norm, xa_mem_norm=xa_mem_norm, xa_w_q=xa_w_q, xa_w_kv=xa_w_kv,
             xa_w_o=xa_w_o, xa_q_norm=xa_q_norm, xa_k_norm=xa_k_norm,
             ffn2_norm=ffn2_norm, ffn2_w_in=ffn2_w_in, ffn2_w_out=ffn2_w_out)
    y_prompt = trunk(x_prompt, mem_prompt, P)
    y_sample = trunk(x_sample, mem_sample, P)
    return (y_prompt, y_sample)
```

```python
import contextlib
import math
import numpy as np
import concourse.bass as bass
import concourse.mybir as mybir
from concourse.bass_utils import run_bass_kernel_spmd

F32 = mybir.dt.float32
BF16 = mybir.dt.bfloat16
ALU = mybir.AluOpType
AF = mybir.ActivationFunctionType

D = 1024
KC = 8
DFF = 2816
NJ = DFF // 128
N = 512
NMEM = 256
EPS = 1e-6
ROPE_THETA = 500000.0
GELU_C = 1.5957691216057308
AB = 2048
DILS = (1, 4, 16)


class Buf:
    __slots__ = ("w", "r")

    def __init__(self):
        self.w = None
        self.r = {}


class Eng:
    def __init__(self, name, e, sem):
        self.name = name
        self.e = e
        self.sem = sem
        self.cnt = 0
        self.known = {}
        self.pending = False


class K:
    def __init__(self, nc, es):
        self.nc = nc
        self.es = es
        self.sems = []
        self.E = {}
        for name, e in (("pe", nc.tensor), ("act", nc.scalar), ("dve", nc.vector),
                        ("pool", nc.gpsimd), ("sp", nc.sync)):
            self.E[name] = Eng(name, e, self.mksem("c_" + name))
        self.rings = {}
        self.ring_pos = {}
        self.ring_val = {}
        for q, n in (("sp", 10), ("pool", 8), ("act", 6)):
            self.rings[q] = [self.mksem("d_%s%d" % (q, i)) for i in range(n)]
            self.ring_pos[q] = 0
            for s in self.rings[q]:
                self.ring_val[s] = 0
        self.banks = []
        self.bank_bufs = []
        for i in range(8):
            t = es.enter_context(nc.psum_tensor("ps%d" % i, [128, 512], F32))
            self.banks.append(t)
            self.bank_bufs.append(Buf())
        self.bank_i = 0
        self.nalloc = 0

    def mksem(self, name):
        h = self.es.enter_context(self.nc.semaphore(name))
        self.sems.append(h)
        return len(self.sems) - 1

    def sb(self, shape, dtype, es=None):
        self.nalloc += 1
        nb = int(np.prod(shape[1:])) * (4 if dtype == F32 else 2)
        nb = (nb + 31) // 32 * 32
        self.cur = getattr(self, "cur", 0) + nb
        self.peak = max(getattr(self, "peak", 0), self.cur)
        self.rem = min(getattr(self, "rem", 1 << 30), self.nc.sbuf_bytes_remaining)
        t = (es or self.es).enter_context(self.nc.sbuf_tensor("t%d" % self.nalloc, list(shape), dtype))

        def rel():
            self.cur -= nb
        (es or self.es).callback(rel)
        return t

    def bank(self):
        i = self.bank_i
        self.bank_i = (i + 1) % 8
        return self.banks[i], self.bank_bufs[i]

    def need(self, E, s, v):
        if E.known.get(s, 0) >= v:
            return
        E.e.wait_ge(self.sems[s], v)
        E.known[s] = v

    def _deps(self, E, reads, writes):
        for b in reads:
            if b.w is not None:
                self._dep(E, b.w)
        for b in writes:
            if b.w is not None:
                self._dep(E, b.w)
            for s, v in b.r.items():
                self._dep(E, (s, v))

    def _dep(self, E, ev):
        s, v = ev
        if s == E.sem and E.name == "pe":
            return
        self.need(E, s, v)

    def _record(self, ev, reads, writes):
        s, v = ev
        for b in reads:
            if b.r.get(s, 0) < v:
                b.r[s] = v
        for b in writes:
            b.w = ev
            b.r = {}

    def op(self, eng, fn, reads=(), writes=(), inc=True):
        E = self.E[eng]
        self._deps(E, reads, writes)
        ins = fn(E.e)
        if inc:
            E.cnt += 1
            ins.then_inc(self.sems[E.sem], 1)
            ev = (E.sem, E.cnt)
            E.pending = False
        else:
            assert eng == "pe"
            ev = (E.sem, E.cnt + 1)
            E.pending = True
        self._record(ev, reads, writes)

    def dma(self, q, out, in_, reads=(), writes=(), **kw):
        E = self.E[q]
        self._deps(E, reads, writes)
        ring = self.rings[q]
        s = ring[self.ring_pos[q] % len(ring)]
        self.ring_pos[q] += 1
        prev = self.ring_val[s]
        self.need(E, s, prev)
        E.e.dma_start(out=out, in_=in_, **kw).then_inc(self.sems[s], 16)
        self.ring_val[s] = prev + 16
        self._record((s, prev + 16), reads, writes)

    def barrier(self):
        for E in self.E.values():
            assert not E.pending
        for E in self.E.values():
            for O in self.E.values():
                if O is not E and O.cnt > 0:
                    self.need(E, O.sem, O.cnt)
            for s, v in self.ring_val.items():
                if v > 0:
                    self.need(E, s, v)

    def finish(self):
        E = self.E["sp"]
        for O in self.E.values():
            assert not O.pending
            if O is not E and O.cnt > 0:
                self.need(E, O.sem, O.cnt)
        for s, v in self.ring_val.items():
            if v > 0:
                self.need(E, s, v)


class Ring:
    def __init__(self, k, n, shape, dtype, es=None):
        self.t = [k.sb(shape, dtype, es) for _ in range(n)]
        self.b = [Buf() for _ in range(n)]
        self.i = 0

    def next(self):
        i = self.i
        self.i = (i + 1) % len(self.t)
        return self.t[i], self.b[i]


def build(S_list, debug=False):
    nc = bass.Bass("TRN2", target_bir_lowering=False)
    SMAX = max(S_list)

    def din(name, shape):
        return nc.dram_tensor(name, list(shape), F32, kind="ExternalInput").ap()

    def dscr(name, shape, dtype):
        kind = "ExternalOutput" if (debug and name.startswith("a_")) else "Internal"
        return nc.dram_tensor(name, list(shape), dtype, kind=kind).ap()

    x_in = [din("x%d" % i, [S, D]) for i, S in enumerate(S_list)]
    m_in = [din("m%d" % i, [NMEM, D]) for i in range(len(S_list))]
    y_out = [nc.dram_tensor("y%d" % i, [S, D], F32, kind="ExternalOutput").ap() for i, S in enumerate(S_list)]
    W = {}
    for name, shape in (
        ("ffn1_norm", [2, D]), ("ffn1_w_in", [2, D, 2 * DFF]), ("ffn1_w_out", [2, DFF, D]), ("mix_norm", [2, D]),
        ("ev_w_in", [1, D, 2048]), ("ev_w_out", [1, D, D]), ("gm_ln_g", [1, 512]), ("gm_ln_b", [1, 512]),
        ("gm_w_s", [1, 4, 128, 128]), ("gm_b_s", [1, 4, 128]), ("rg_conv_w", [1, 4, 512]), ("rg_conv_b", [1, 512]),
        ("rg_w_a", [1, 2, 8, 64, 64]), ("rg_b_a", [1, 2, 512]), ("rg_w_i", [1, 2, 8, 64, 64]), ("rg_b_i", [1, 2, 512]),
        ("rg_lam", [1, 2, 512]), ("od_w_in", [1, D, 3 * D]), ("od_w_out", [1, D, D]), ("od_q_norm", [1, 64]),
        ("od_k_norm", [1, 64]), ("xa_norm", [2, D]), ("xa_mem_norm", [2, D]), ("xa_w_q", [2, D, D]),
        ("xa_w_kv", [2, D, 2 * D]), ("xa_w_o", [2, D, D]), ("xa_q_norm", [2, 256]), ("xa_k_norm", [2, 256]),
        ("ffn2_norm", [2, D]), ("ffn2_w_in", [2, D, 2 * DFF]), ("ffn2_w_out", [2, DFF, D]),
    ):
        W[name] = din(name, shape)
    c_ident = din("c_ident", [128, 128])
    c_rot = din("c_rot", [128, 128])
    c_cos = din("c_cos", [128, SMAX])
    c_sin = din("c_sin", [128, SMAX])
    c_mask = din("c_mask", [128, 4, 2, 128])

    ffn_names = [("ffn1", 0), ("ffn2", 0), ("ffn1", 1), ("ffn2", 1)]
    s_win = [dscr("s_win%d" % i, [NJ, 128, KC, 256], BF16) for i in range(4)]
    s_wout = [dscr("s_wout%d" % i, [KC, 128, NJ, 128], BF16) for i in range(4)]
    s_ev_in = dscr("s_ev_in", [6, 128, KC, 256], BF16)
    s_ev_v = dscr("s_ev_v", [128, KC, 512], BF16)
    s_ev_out = dscr("s_ev_out", [4, 128, KC, 256], BF16)
    s_od_qk = dscr("s_od_qk", [8, 128, KC, 256], BF16)
    s_od_v = dscr("s_od_v", [128, KC, 1024], BF16)
    s_od_out = dscr("s_od_out", [4, 128, KC, 256], BF16)
    s_xq = [dscr("s_xq%d" % l, [4, 128, KC, 256], BF16) for l in range(2)]
    s_xk = [dscr("s_xk%d" % l, [4, 128, KC, 256], BF16) for l in range(2)]
    s_xv = [dscr("s_xv%d" % l, [128, KC, 1024], BF16) for l in range(2)]
    s_xo = [dscr("s_xo%d" % l, [4, 128, KC, 256], BF16) for l in range(2)]
    a_XS = dscr("a_XS", [KC, 128, SMAX], F32)
    a_AO = dscr("a_AO", [4, 128, SMAX], BF16)
    a_BO = dscr("a_BO", [4, 128, SMAX], BF16)
    a_ZX = dscr("a_ZX", [4, 128, SMAX], F32)
    a_GZ = dscr("a_GZ", [4, 128, SMAX], BF16)
    a_QT = dscr("a_QT", [KC, 128, SMAX], BF16)
    a_KT = dscr("a_KT", [KC, 128, SMAX], BF16)
    a_V = dscr("a_V", [SMAX, D], BF16)
    a_OT = dscr("a_OT", [KC, 128, SMAX], BF16)
    NTMAX = SMAX // N
    tb = {nm: [Buf() for _ in range(NTMAX)] for nm in ("XS", "AO", "BO", "ZX", "GZ", "QT", "KT", "V", "OT")}
    dbg = {}

    with contextlib.ExitStack() as es:
        k = K(nc, es)
        op, dma = k.op, k.dma

        IDENT = k.sb([128, 128], F32)
        ROTM = k.sb([128, 128], F32)
        ONESB = k.sb([128, 128], BF16)
        BLK1 = k.sb([128, 128], BF16)
        MASK = k.sb([128, 4, 2, 128], BF16)
        CST = k.sb([128, 4], F32)
        bC = Buf()
        G = {}
        for nm in ("ffn1_norm", "ffn2_norm", "mix_norm", "xa_norm", "xa_mem_norm"):
            G[nm] = k.sb([128, 2, KC], F32)
        XQG = k.sb([128, 2, 2], F32)
        XKG = k.sb([128, 2, 2], F32)
        OQG = k.sb([128, 2], F32)
        CW = k.sb([128, 4, 4], F32)
        CB = k.sb([128, 4], F32)
        BA = k.sb([128, 2, 4], F32)
        BI = k.sb([128, 2, 4], F32)
        NSP = k.sb([128, 2, 4], F32)
        WBD = k.sb([128, 16, 128], BF16)
        WST = k.sb([128, 4, 128], BF16)
        LNG = k.sb([128, 512], F32)
        LNB = k.sb([128, 512], F32)
        BSB = k.sb([128, 4, 4, 128], F32)
        KX = [k.sb([128, KC, NMEM], BF16) for _ in range(2)]
        VX = [k.sb([128, 2, D], BF16) for _ in range(2)]
        bKX = [Buf(), Buf()]
        bVX = [Buf(), Buf()]
        WG = Ring(k, 4, [128, KC, 256], BF16)

        nsq = lambda e: None
        with nc.allow_non_contiguous_dma(reason="tiny param loads"):
            dma("sp", IDENT[:], c_ident[:, :], writes=[bC])
            dma("sp", ROTM[:], c_rot[:, :], writes=[bC])
            for nm in G:
                for l in range(2):
                    dma("sp", G[nm][:, l, :], W[nm][l].rearrange("(c p) -> p c", p=128), writes=[bC])
            for l in range(2):
                dma("sp", XQG[:, l, :], W["xa_q_norm"][l].rearrange("(c p) -> p c", p=128), writes=[bC])
                dma("sp", XKG[:, l, :], W["xa_k_norm"][l].rearrange("(c p) -> p c", p=128), writes=[bC])
            for hp in range(2):
                dma("sp", OQG[hp * 64:(hp + 1) * 64, 0:1], W["od_q_norm"][0].rearrange("(p o) -> p o", o=1), writes=[bC])
                dma("sp", OQG[hp * 64:(hp + 1) * 64, 1:2], W["od_k_norm"][0].rearrange("(p o) -> p o", o=1), writes=[bC])
            for t in range(4):
                dma("sp", CW[:, t, :], W["rg_conv_w"][0, t].rearrange("(c p) -> p c", p=128), writes=[bC])
            dma("sp", CB[:], W["rg_conv_b"][0].rearrange("(c p) -> p c", p=128), writes=[bC])
            for d in range(2):
                dma("sp", BA[:, d, :], W["rg_b_a"][0, d].rearrange("(c p) -> p c", p=128), writes=[bC])
                dma("sp", BI[:, d, :], W["rg_b_i"][0, d].rearrange("(c p) -> p c", p=128), writes=[bC])
                dma("sp", NSP[:, d, :], W["rg_lam"][0, d].rearrange("(c p) -> p c", p=128), writes=[bC])
            dma("sp", LNG[:], W["gm_ln_g"][0].partition_broadcast(128), writes=[bC])
            dma("sp", LNB[:], W["gm_ln_b"][0].partition_broadcast(128), writes=[bC])
            for g in range(4):
                for tc in range(4):
                    dma("sp", BSB[:, g, tc, :], W["gm_b_s"][0, g].partition_broadcast(128), writes=[bC])
        with contextlib.ExitStack() as es0:
            STG = k.sb([128, 16, 128], F32, es0)
            MSTG = k.sb([128, 4, 2, 128], F32, es0)
            WSS = k.sb([128, 4, 128], F32, es0)
            bS = Buf()
            op("pool", lambda e: e.memset(STG[:], 0.0), writes=[bS])
            for d in range(2):
                for ai, nm in enumerate(("rg_w_a", "rg_w_i")):
                    for c in range(4):
                        for hp in range(2):
                            dma("sp", STG[hp * 64:(hp + 1) * 64, (d * 2 + ai) * 4 + c, hp * 64:(hp + 1) * 64],
                                W[nm][0, d, 2 * c + hp], writes=[bS])
            dma("sp", MSTG[:], c_mask[:, :, :, :], writes=[bS])
            for g in range(4):
                dma("sp", WSS[:, g, :], W["gm_w_s"][0, g], writes=[bS])
            op("dve", lambda e: e.tensor_copy(out=WBD[:], in_=STG[:]), reads=[bS], writes=[bC])
            op("dve", lambda e: e.tensor_copy(out=MASK[:], in_=MSTG[:]), reads=[bS], writes=[bC])
            op("dve", lambda e: e.memset(ONESB[:], 1.0), writes=[bC])
            op("dve", lambda e: e.memset(BLK1[:], 0.0), writes=[bC])
            op("dve", lambda e: e.memset(BLK1[0:64, 0:64], 1.0), writes=[bC])
            op("dve", lambda e: e.memset(BLK1[64:128, 64:128], 1.0), writes=[bC])
            op("dve", lambda e: e.memset(CST[:, 0:1], EPS), writes=[bC])
            op("dve", lambda e: e.memset(CST[:, 1:2], 1.0), writes=[bC])
            op("dve", lambda e: e.tensor_scalar(out=XKG[:], in0=XKG[:], scalar1=1.0 / 16.0, scalar2=None, op0=ALU.mult), reads=[bC], writes=[bC])
            op("dve", lambda e: e.tensor_scalar(out=OQG[:, 0:1], in0=OQG[:, 0:1], scalar1=0.125, scalar2=None, op0=ALU.mult), reads=[bC], writes=[bC])
            op("act", lambda e: e.activation(out=NSP[:], in_=NSP[:], func=AF.Exp, scale=-1.0), reads=[bC], writes=[bC])
            op("act", lambda e: e.activation(out=NSP[:], in_=NSP[:], func=AF.Ln, bias=CST[:, 1:2]), reads=[bC], writes=[bC])
            op("dve", lambda e: e.tensor_scalar(out=NSP[:], in0=NSP[:], scalar1=-8.0, scalar2=None, op0=ALU.mult), reads=[bC], writes=[bC])
            for g in range(4):
                pt, pb = k.bank()
                op("pe", lambda e: e.transpose(out=pt[:, 0:128], in_=WSS[:, g, :], identity=IDENT[:]), reads=[bS, bC], writes=[pb])
                op("act", lambda e: e.copy(out=WST[:, g, :], in_=pt[:, 0:128]), reads=[pb], writes=[bC])
            k.barrier()

        wb = {}
        DQ = []
        defer = [False]

        def cdma(out, in_, b):
            if defer[0]:
                DQ.append((out, in_, b))
            else:
                dma("pool", out, in_, writes=[b])

        def drain_dq(n):
            for _ in range(min(n, len(DQ))):
                out, in_, b = DQ.pop(0)
                dma("pool", out, in_, writes=[b])

        def conv_fm(key, src, dst, col_pairs):
            bl = []
            for g, cols in enumerate(col_pairs):
                bg = []
                for h, c0 in enumerate(cols):
                    b = Buf()
                    cdma(dst[g, :, :, h * 128:(h + 1) * 128], src[:, c0:c0 + 128].rearrange("(kc p) c -> p kc c", p=128), b)
                    bg.append(b)
                bl.append(bg)
            wb[key] = bl

        def conv_wout(key, src, dst):
            bl = []
            for c in range(KC):
                b = Buf()
                cdma(dst[c], src[:, c * 128:(c + 1) * 128].rearrange("(j p) c -> p j c", p=128), b)
                bl.append([b])
            wb[key] = bl

        def conv_tm(key, src, dst, c0, ncols):
            bl = []
            for kc in range(KC):
                b = Buf()
                cdma(dst[:, kc, :], src[kc * 128:(kc + 1) * 128, c0:c0 + ncols], b)
                bl.append(b)
            wb[key] = [bl]

        def seqpairs(c0, n):
            return [(c0 + 256 * g, c0 + 256 * g + 128) for g in range(n)]

        def conv_ffn(i):
            nm, l = ffn_names[i]
            conv_fm("win%d" % i, W[nm + "_w_in"][l], s_win[i], [(j * 128, DFF + j * 128) for j in range(NJ)])
            conv_wout("wout%d" % i, W[nm + "_w_out"][l], s_wout[i])

        for l in range(2):
            conv_fm("xk%d" % l, W["xa_w_kv"][l], s_xk[l], seqpairs(0, 4))
            conv_tm("xv%d" % l, W["xa_w_kv"][l], s_xv[l], 1024, 1024)
        conv_ffn(0)
        conv_fm("ev_in", W["ev_w_in"][0], s_ev_in, seqpairs(0, 2) + seqpairs(1024, 2) + seqpairs(1536, 2))
        conv_tm("ev_v", W["ev_w_in"][0], s_ev_v, 512, 512)
        defer[0] = True
        conv_fm("ev_out", W["ev_w_out"][0], s_ev_out, seqpairs(0, 4))
        conv_fm("xq0", W["xa_w_q"][0], s_xq[0], seqpairs(0, 4))
        conv_fm("xo0", W["xa_w_o"][0], s_xo[0], seqpairs(0, 4))
        conv_ffn(1)
        conv_ffn(2)
        conv_fm("od_qk", W["od_w_in"][0], s_od_qk, seqpairs(0, 8))
        conv_tm("od_v", W["od_w_in"][0], s_od_v, 2048, 1024)
        conv_fm("od_out", W["od_w_out"][0], s_od_out, seqpairs(0, 4))
        conv_fm("xq1", W["xa_w_q"][1], s_xq[1], seqpairs(0, 4))
        conv_fm("xo1", W["xa_w_o"][1], s_xo[1], seqpairs(0, 4))
        conv_ffn(3)
        defer[0] = False
        dq_per_tile = -(-len(DQ) // (S_list[0] // N))

        def proj_fm(key, scr, ngroups, SRC, bSRC, ncols, consumer, hook=None, delay=False):
            pend_a, pend_b = [], []
            for g in range(ngroups):
                wt, wbuf = WG.next()
                dma("sp", wt[:], scr[g], reads=wb[key][g], writes=[wbuf])
                for h in range(2):
                    pt, pb = k.bank()
                    for kc in range(KC):
                        op("pe", lambda e: e.matmul(pt[:, 0:ncols], lhsT=wt[:, kc, h * 128:(h + 1) * 128], rhs=SRC[:, kc, 0:ncols],
                                                    start=(kc == 0), stop=(kc == KC - 1)),
                           reads=[wbuf, bSRC], writes=[pb], inc=(kc == KC - 1))
                    if hook is not None and g == 0 and h == 0:
                        hook()
                    if not delay:
                        r = consumer(2 * g + h, pt, pb)
                        if r is not None:
                            r()
                    else:
                        if len(pend_b) > 1:
                            pend_b.pop(0)()
                        if pend_a:
                            r = consumer(*pend_a.pop(0))
                            if r is not None:
                                pend_b.append(r)
                        pend_a.append((2 * g + h, pt, pb))
            while pend_a or pend_b:
                if pend_b:
                    pend_b.pop(0)()
                if pend_a:
                    r = consumer(*pend_a.pop(0))
                    if r is not None:
                        pend_b.append(r)

        def rms_stats(SQ, bSQ, nch, ncols, lhs, inv_n, RS, bRS):
            pt, pb = k.bank()
            for c in range(nch):
                op("pe", lambda e: e.matmul(pt[:, 0:ncols], lhsT=lhs[:], rhs=SQ[c], start=(c == 0), stop=(c == nch - 1)),
                   reads=[bSQ, bC], writes=[pb], inc=(c == nch - 1))
            op("act", lambda e: e.activation(out=RS[:, 0:ncols], in_=pt[:, 0:ncols], func=AF.Ln, scale=inv_n, bias=CST[:, 0:1]),
               reads=[pb, bC], writes=[bRS])
            op("act", lambda e: e.activation(out=RS[:, 0:ncols], in_=RS[:, 0:ncols], func=AF.Exp, scale=-0.5), reads=[bRS], writes=[bRS])

        def gelu_from_psum(pt, pb, ncols, OUT, bOUT, T1, bT1):
            op("act", lambda e: e.activation(out=T1, in_=pt[:, 0:ncols], func=AF.Square), reads=[pb], writes=[bT1])
            op("dve", lambda e: e.tensor_scalar(out=T1, in0=T1, scalar1=0.044715, scalar2=1.0, op0=ALU.mult, op1=ALU.add),
               reads=[bT1], writes=[bT1])
            op("dve", lambda e: e.tensor_tensor(out=T1, in0=T1, in1=pt[:, 0:ncols], op=ALU.mult), reads=[bT1, pb], writes=[bT1])
            op("act", lambda e: e.activation(out=T1, in_=T1, func=AF.Sigmoid, scale=GELU_C), reads=[bT1], writes=[bT1])
            op("dve", lambda e: e.tensor_tensor(out=OUT, in0=T1, in1=pt[:, 0:ncols], op=ALU.mult), reads=[bT1, pb], writes=[bOUT])

        for si, S in enumerate(S_list):
            NT = S // N
            x_d, m_d, y_d = x_in[si], m_in[si], y_out[si]

            def ffn_stage(which):
                with contextlib.ExitStack() as es1:
                    XT = Ring(k, 2, [128, KC, N], F32, es1)
                    HB = k.sb([128, KC, N], BF16, es1); bHB = Buf()
                    SQ = k.sb([128, KC, N], BF16, es1); bSQ = Buf()
                    ACTB = k.sb([128, NJ, N], BF16, es1); bACT = [Buf() for _ in range(NJ)]
                    WO = Ring(k, 2, [128, NJ, 128], BF16, es1)
                    TMP = Ring(k, 10, [128, N], F32, es1)
                    RS = k.sb([128, N], F32, es1); bRS = Buf()
                    XIN = Ring(k, 4, [128, D], F32, es1) if which == "A" else None
                    if which != "A":
                        MIX = k.sb([128, KC, N], BF16, es1); bMIX = Buf()
                        QB = k.sb([128, KC, N], BF16, es1); bQB = Buf()
                        PT = Ring(k, 4, [128, 2, N], BF16, es1)
                        MIXIN = Ring(k, 1, [128, KC, N], BF16, es1)
                    WT = ACTB[:, 0:16, :].rearrange("p (kc a) n -> p kc (a n)", a=2)
                    bWTl = bACT[0:16]

                    def rmsnorm(X, bX, gname, l, ncols=N):
                        op("act", lambda e: e.activation(out=SQ[:, :, 0:ncols], in_=X[:, :, 0:ncols], func=AF.Square), reads=[bX], writes=[bSQ])
                        rms_stats([SQ[:, c, 0:ncols] for c in range(KC)], bSQ, KC, ncols, ONESB, 1.0 / D, RS, bRS)
                        for c in range(KC):
                            op("dve", lambda e: e.scalar_tensor_tensor(out=HB[:, c, 0:ncols], in0=X[:, c, 0:ncols], scalar=G[gname][:, l, c:c + 1],
                                                                       in1=RS[:, 0:ncols], op0=ALU.mult, op1=ALU.mult),
                               reads=[bX, bRS, bC], writes=[bHB])

                    def hb_chunk(X, bX, c, post):
                        gname, l = post
                        op("dve", lambda e: e.tensor_scalar(out=HB[:, c, :], in0=X[:, c, :], scalar1=G[gname][:, l, c:c + 1], scalar2=None, op0=ALU.mult),
                           reads=[bX, bC], writes=[bHB])

                    def rmsnorm_def(X, bX, gname, l, hoisted=False):
                        for c in (range(KC) if not hoisted else []):
                            op("dve", lambda e: e.tensor_scalar(out=HB[:, c, :], in0=X[:, c, :], scalar1=G[gname][:, l, c:c + 1], scalar2=None, op0=ALU.mult),
                               reads=[bX, bC], writes=[bHB])
                        op("act", lambda e: e.activation(out=SQ[:, :, :], in_=X[:, :, :], func=AF.Square), reads=[bX], writes=[bSQ])

                        def stats():
                            rms_stats([SQ[:, c, :] for c in range(KC)], bSQ, KC, N, ONESB, 1.0 / D, RS, bRS)
                        return stats

                    def ffn(X, bX, i, hoisted=False, post=None):
                        nm, l = ffn_names[i]
                        stats = rmsnorm_def(X, bX, nm + "_norm", l, hoisted)
                        pend = []
                        for j in range(NJ):
                            wt, wbuf = WG.next()
                            dma("sp", wt[:], s_win[i][j], reads=wb["win%d" % i][j], writes=[wbuf])
                            pg, pgb = k.bank()
                            pu, pub = k.bank()
                            for h, (pt, pb) in enumerate(((pg, pgb), (pu, pub))):
                                for kc in range(KC):
                                    op("pe", lambda e: e.matmul(pt[:, :], lhsT=wt[:, kc, h * 128:(h + 1) * 128], rhs=HB[:, kc, :],
                                                                start=(kc == 0), stop=(kc == KC - 1)),
                                       reads=[wbuf, bHB], writes=[pb], inc=(kc == KC - 1))
                            pend.append((j, pg, pgb, pu, pub))
                            if j == 0:
                                continue
                            if j == 1:
                                stats()
                            while pend:
                                jj, pg_, pgb_, pu_, pub_ = pend.pop(0)
                                sg, bsg = TMP.next()
                                op("dve", lambda e: e.tensor_tensor(out=sg[:], in0=pg_[:, :], in1=RS[:], op=ALU.mult), reads=[pgb_, bRS], writes=[bsg])
                                op("act", lambda e: e.activation(out=sg[:], in_=sg[:], func=AF.Silu), reads=[bsg], writes=[bsg])
                                tu, btu = TMP.next()
                                op("dve", lambda e: e.tensor_tensor(out=tu[:], in0=pu_[:, :], in1=RS[:], op=ALU.mult), reads=[pub_, bRS], writes=[btu])
                                op("dve", lambda e: e.tensor_tensor(out=ACTB[:, jj, :], in0=tu[:], in1=sg[:], op=ALU.mult),
                                   reads=[btu, bsg], writes=[bACT[jj]])
                        for c in range(KC):
                            wt, wbuf = WO.next()
                            dma("sp", wt[:], s_wout[i][c], reads=wb["wout%d" % i][c], writes=[wbuf])
                            pt, pb = k.bank()
                            for j in range(NJ):
                                op("pe", lambda e: e.matmul(pt[:, :], lhsT=wt[:, j, :], rhs=ACTB[:, j, :], start=(j == 0), stop=(j == NJ - 1)),
                                   reads=[wbuf, bACT[j]], writes=[pb], inc=(j == NJ - 1))
                            op("dve", lambda e: e.scalar_tensor_tensor(out=X[:, c, :], in0=pt[:, :], scalar=0.5, in1=X[:, c, :],
                                                                       op0=ALU.mult, op1=ALU.add), reads=[pb, bX], writes=[bX])
                            if post is not None:
                                hb_chunk(X, bX, c, post)

                    def add_proj(X, bX, key, scr, SRC, bSRC, post=None):
                        def cons(oc, pt, pb):
                            op("dve", lambda e: e.tensor_tensor(out=X[:, oc, :], in0=pt[:, :], in1=X[:, oc, :], op=ALU.add),
                               reads=[pb, bX], writes=[bX])
                            if post is not None:
                                hb_chunk(X, bX, oc, post)
                        proj_fm(key, scr, 4, SRC, bSRC, N, cons)

                    def headnorm_proj(key, scr, ngroups, SRCH, bSRCH, ncols, lhs, inv_n, per, gain_fn, finish, rs_tok=False, hook=None):
                        hold = []

                        def cons(oc, pt, pb):
                            qf, bqf = TMP.next()
                            if rs_tok:
                                op("dve", lambda e: e.tensor_tensor(out=qf[:, 0:ncols], in0=pt[:, 0:ncols], in1=RS[:, 0:ncols], op=ALU.mult),
                                   reads=[pb, bRS], writes=[bqf])
                                op("act", lambda e: e.activation(out=SQ[:, oc % KC, 0:ncols], in_=qf[:, 0:ncols], func=AF.Square), reads=[bqf], writes=[bSQ])
                            else:
                                op("act", lambda e: e.copy(out=qf[:, 0:ncols], in_=pt[:, 0:ncols]), reads=[pb], writes=[bqf])
                                op("act", lambda e: e.activation(out=SQ[:, oc % KC, 0:ncols], in_=pt[:, 0:ncols], func=AF.Square), reads=[pb], writes=[bSQ])
                            hold.append((oc, qf, bqf))
                            if len(hold) == per:
                                rs, brs = TMP.next()
                                rms_stats([SQ[:, o % KC, 0:ncols] for o, _, _ in hold], bSQ, per, ncols, lhs, inv_n, rs, brs)
                                items = list(hold)
                                hold.clear()

                                def cont():
                                    for o, q, bq in items:
                                        finish(o, q, bq, rs, brs)
                                return cont
                            return None
                        proj_fm(key, scr, ngroups, SRCH, bSRCH, ncols, cons, hook=hook, delay=True)

                    def xattn(X, bX, l, hoisted=False, post=None):
                        stats = rmsnorm_def(X, bX, "xa_norm", l, hoisted)

                        def fin(o, q, bq, rs, brs):
                            op("dve", lambda e: e.scalar_tensor_tensor(out=QB[:, o, :], in0=q[:], scalar=XQG[:, l, (o % 2):(o % 2) + 1], in1=rs[:],
                                                                       op0=ALU.mult, op1=ALU.mult), reads=[bq, brs, bC], writes=[bQB])
                        headnorm_proj("xq%d" % l, s_xq[l], 4, HB, bHB, N, ONESB, 1.0 / 256, 2, None, fin, rs_tok=True, hook=stats)
                        pts = []
                        for h in range(4):
                            p_t, bp_t = PT.next()
                            pts.append((p_t, bp_t))
                            for mc in range(2):
                                pt, pb = k.bank()
                                for ec in range(2):
                                    op("pe", lambda e: e.matmul(pt[:, :], lhsT=KX[l][:, 2 * h + ec, mc * 128:(mc + 1) * 128], rhs=QB[:, 2 * h + ec, :],
                                                                start=(ec == 0), stop=(ec == 1)), reads=[bKX[l], bQB], writes=[pb], inc=(ec == 1))
                                op("act", lambda e: e.activation(out=p_t[:, mc, :], in_=pt[:, :], func=AF.Exp), reads=[pb], writes=[bp_t])
                        rds = []
                        for h in range(4):
                            p_t, bp_t = pts[h]
                            pd, pdb = k.bank()
                            for mc in range(2):
                                op("pe", lambda e: e.matmul(pd[:, :], lhsT=ONESB[:], rhs=p_t[:, mc, :], start=(mc == 0), stop=(mc == 1)),
                                   reads=[bC, bp_t], writes=[pdb], inc=(mc == 1))
                            rd, brd = TMP.next()
                            op("act", lambda e: e.activation(out=rd[:], in_=pd[:, :], func=AF.Ln), reads=[pdb], writes=[brd])
                            op("act", lambda e: e.activation(out=rd[:], in_=rd[:], func=AF.Exp, scale=-1.0), reads=[brd], writes=[brd])
                            rds.append((rd, brd))
                        for h in range(4):
                            p_t, bp_t = pts[h]
                            rd, brd = rds[h]
                            for ec in range(2):
                                po, pob = k.bank()
                                for mc in range(2):
                                    op("pe", lambda e: e.matmul(po[:, :], lhsT=VX[l][:, mc, (2 * h + ec) * 128:(2 * h + ec + 1) * 128], rhs=p_t[:, mc, :],
                                                                start=(mc == 0), stop=(mc == 1)), reads=[bVX[l], bp_t], writes=[pob], inc=(mc == 1))
                                op("dve", lambda e: e.tensor_tensor(out=MIX[:, 2 * h + ec, :], in0=po[:, :], in1=rd[:], op=ALU.mult),
                                   reads=[pob, brd], writes=[bMIX])
                        add_proj(X, bX, "xo%d" % l, s_xo[l], MIX, bMIX, post=post)

                    def load_fm(X, bX, t, scr, bufs, q="sp"):
                        dma(q, X[:], scr[:, :, t * N:(t + 1) * N].rearrange("c p n -> p c n"), reads=[bufs[t]], writes=[bX])

                    def store_fm(X, bX, t, scr, bufs, nch=KC, q="pool"):
                        dma(q, scr[:, :, t * N:(t + 1) * N].rearrange("c p n -> p c n"), X[:, 0:nch, :], reads=[bX], writes=[bufs[t]])

                    if which == "A":
                        MT, bMT = XT.next()
                        for mc in range(2):
                            xi, bxi = XIN.next()
                            dma("sp", xi[:], m_d[mc * 128:(mc + 1) * 128, :], writes=[bxi])
                            for half in range(2):
                                pt, pb = k.bank()
                                for c4 in range(4):
                                    c = half * 4 + c4
                                    op("pe", lambda e: e.transpose(out=pt[:, c4 * 128:(c4 + 1) * 128], in_=xi[:, c * 128:(c + 1) * 128], identity=IDENT[:]),
                                       reads=[bxi, bC], writes=[pb], inc=(c4 == 3))
                                op("dve", lambda e: e.tensor_copy(out=MT[:, half * 4:(half + 1) * 4, mc * 128:(mc + 1) * 128],
                                                                  in_=pt[:, :].rearrange("p (c n) -> p c n", c=4)), reads=[pb], writes=[bMT])
                        for l in range(2):
                            rmsnorm(MT, bMT, "xa_mem_norm", l, ncols=NMEM)

                            def fink(o, q, bq, rs, brs, l=l):
                                op("dve", lambda e: e.scalar_tensor_tensor(out=KX[l][:, o, :], in0=q[:, 0:NMEM], scalar=XKG[:, l, (o % 2):(o % 2) + 1],
                                                                           in1=rs[:, 0:NMEM], op0=ALU.mult, op1=ALU.mult), reads=[bq, brs, bC], writes=[bKX[l]])
                            headnorm_proj("xk%d" % l, s_xk[l], 4, HB, bHB, NMEM, ONESB, 1.0 / 256, 2, None, fink)
                            dma("sp", WT[:], s_xv[l], reads=wb["xv%d" % l][0], writes=bWTl)
                            for mc in range(2):
                                for half in range(2):
                                    pt, pb = k.bank()
                                    for kc in range(KC):
                                        op("pe", lambda e: e.matmul(pt[:, :], lhsT=HB[:, kc, mc * 128:(mc + 1) * 128], rhs=WT[:, kc, half * 512:(half + 1) * 512],
                                                                    start=(kc == 0), stop=(kc == KC - 1)), reads=[bHB] + bWTl, writes=[pb], inc=(kc == KC - 1))
                                    op("act", lambda e: e.copy(out=VX[l][:, mc, half * 512:(half + 1) * 512], in_=pt[:, :]), reads=[pb], writes=[bVX[l]])

                        with contextlib.ExitStack() as es2:
                            U = k.sb([128, 4, N], BF16, es2); bU = Buf()
                            ZXo = k.sb([128, 4, N], F32, es2); bZXo = Buf()
                            GZo = k.sb([128, 4, N], BF16, es2); bGZo = Buf()
                            VN = k.sb([128, 4, 512], BF16, es2); bVN = Buf()
                            AO = k.sb([128, 4, N], BF16, es2); bAO = Buf()
                            ST = k.sb([128, 12], F32, es2); bST = [Buf() for _ in range(6)]
                            def x_loads(t):
                                ld = []
                                for tc in range(4):
                                    xi, bxi = XIN.next()
                                    r0 = t * N + tc * 128
                                    dma("sp", xi[:], x_d[r0:r0 + 128, :], writes=[bxi])
                                    ld.append((xi, bxi))
                                return ld

                            def x_transposes(ld):
                                X, bX = XT.next()
                                for tc, (xi, bxi) in enumerate(ld):
                                    for half in range(2):
                                        pt, pb = k.bank()
                                        for c4 in range(4):
                                            c = half * 4 + c4
                                            op("pe", lambda e: e.transpose(out=pt[:, c4 * 128:(c4 + 1) * 128], in_=xi[:, c * 128:(c + 1) * 128], identity=IDENT[:]),
                                               reads=[bxi, bC], writes=[pb], inc=(c4 == 3))
                                        op("dve", lambda e: e.tensor_copy(out=X[:, half * 4:(half + 1) * 4, tc * 128:(tc + 1) * 128],
                                                                          in_=pt[:, :].rearrange("p (c n) -> p c n", c=4)), reads=[pb], writes=[bX])
                                for c in range(KC):
                                    hb_chunk(X, bX, c, ("ffn1_norm", 0))
                                return X, bX

                            nxtX = x_transposes(x_loads(0))
                            for t in range(NT):
                                X, bX = nxtX
                                if t + 1 < NT:
                                    ld_next = x_loads(t + 1)
                                ffn(X, bX, 0, hoisted=True)
                                store_fm(X, bX, t, a_XS, tb["XS"])
                                rmsnorm(X, bX, "mix_norm", 0)

                                def cons(oc, pt, pb):
                                    kind, c = oc // 4, oc % 4
                                    if kind == 1:
                                        op("act", lambda e: e.copy(out=ZXo[:, c, :], in_=pt[:, :]), reads=[pb], writes=[bZXo])
                                    else:
                                        t1, bt1 = TMP.next()
                                        if kind == 0:
                                            gelu_from_psum(pt, pb, N, U[:, c, :], bU, t1[:], bt1)
                                        else:
                                            gelu_from_psum(pt, pb, N, GZo[:, c, :], bGZo, t1[:], bt1)
                                proj_fm("ev_in", s_ev_in, 6, HB, bHB, N, cons)
                                dma("pool", a_ZX[:, :, t * N:(t + 1) * N].rearrange("c p n -> p c n"), ZXo[:], reads=[bZXo], writes=[tb["ZX"][t]])
                                dma("pool", a_GZ[:, :, t * N:(t + 1) * N].rearrange("c p n -> p c n"), GZo[:], reads=[bGZo], writes=[tb["GZ"][t]])
                                dma("sp", WT[:, :, 0:512], s_ev_v, reads=wb["ev_v"][0], writes=bWTl)
                                zb = []
                                for tc in range(4):
                                    pt, pb = k.bank()
                                    for kc in range(KC):
                                        op("pe", lambda e: e.matmul(pt[:, :], lhsT=HB[:, kc, tc * 128:(tc + 1) * 128], rhs=WT[:, kc, 0:512],
                                                                    start=(kc == 0), stop=(kc == KC - 1)), reads=[bHB] + bWTl, writes=[pb], inc=(kc == KC - 1))
                                    vg, bvg = TMP.next()
                                    t1, bt1 = TMP.next()
                                    zb.append((pt, pb, vg, bvg, t1, bt1))
                                for pt, pb, vg, bvg, t1, bt1 in zb:
                                    op("act", lambda e: e.activation(out=t1[:], in_=pt[:, :], func=AF.Square), reads=[pb], writes=[bt1])
                                for pt, pb, vg, bvg, t1, bt1 in zb:
                                    op("dve", lambda e: e.tensor_scalar(out=t1[:], in0=t1[:], scalar1=0.044715, scalar2=1.0, op0=ALU.mult, op1=ALU.add),
                                       reads=[bt1], writes=[bt1])
                                    op("dve", lambda e: e.tensor_tensor(out=t1[:], in0=t1[:], in1=pt[:, :], op=ALU.mult), reads=[bt1, pb], writes=[bt1])
                                for pt, pb, vg, bvg, t1, bt1 in zb:
                                    op("act", lambda e: e.activation(out=t1[:], in_=t1[:], func=AF.Sigmoid, scale=GELU_C), reads=[bt1], writes=[bt1])
                                for tc, (pt, pb, vg, bvg, t1, bt1) in enumerate(zb):
                                    op("dve", lambda e: e.tensor_tensor(out=vg[:], in0=t1[:], in1=pt[:, :], op=ALU.mult), reads=[bt1, pb], writes=[bvg])
                                    op("dve", lambda e: e.reduce_sum(out=ST[:, tc:tc + 1], in_=vg[:], axis=mybir.AxisListType.X), reads=[bvg], writes=[bST[tc]])
                                    op("dve", lambda e: e.tensor_scalar(out=ST[:, tc:tc + 1], in0=ST[:, tc:tc + 1], scalar1=-1.0 / 512, scalar2=None, op0=ALU.mult),
                                       reads=[bST[tc]], writes=[bST[tc]])
                                for tc, (pt, pb, vg, bvg, t1, bt1) in enumerate(zb):
                                    op("act", lambda e: e.activation(out=t1[:], in_=vg[:], func=AF.Square, bias=ST[:, tc:tc + 1]), reads=[bvg, bST[tc]], writes=[bt1])
                                for tc, (pt, pb, vg, bvg, t1, bt1) in enumerate(zb):
                                    op("dve", lambda e: e.reduce_sum(out=ST[:, 4 + tc:5 + tc], in_=t1[:], axis=mybir.AxisListType.X), reads=[bt1], writes=[bST[4]])
                                op("act", lambda e: e.activation(out=ST[:, 8:12], in_=ST[:, 4:8], func=AF.Sqrt, scale=1.0 / 512, bias=CST[:, 0:1]),
                                   reads=[bST[4], bC], writes=[bST[5]])
                                op("dve", lambda e: e.reciprocal(out=ST[:, 8:12], in_=ST[:, 8:12]), reads=[bST[5]], writes=[bST[5]])
                                for tc, (pt, pb, vg, bvg, t1, bt1) in enumerate(zb):
                                    op("dve", lambda e: e.tensor_scalar(out=vg[:], in0=vg[:], scalar1=ST[:, tc:tc + 1], scalar2=ST[:, 8 + tc:9 + tc], op0=ALU.add, op1=ALU.mult),
                                       reads=[bvg, bST[tc], bST[5]], writes=[bvg])
                                    op("dve", lambda e: e.tensor_tensor(out=vg[:], in0=vg[:], in1=LNG[:], op=ALU.mult), reads=[bvg, bC], writes=[bvg])
                                    op("dve", lambda e: e.tensor_tensor(out=VN[:, tc, :], in0=vg[:], in1=LNB[:], op=ALU.add), reads=[bvg, bC], writes=[bVN])
                                if t + 1 < NT:
                                    nxtX = x_transposes(ld_next)
                                for g in range(4):
                                    pt, pb = k.bank()
                                    for tc in range(4):
                                        op("pe", lambda e: e.matmul(pt[:, tc * 128:(tc + 1) * 128], lhsT=VN[:, tc, g * 128:(g + 1) * 128], rhs=WST[:, g, :],
                                                                    start=True, stop=True), reads=[bVN, bC], writes=[pb], inc=(tc == 3))
                                    t1, bt1 = TMP.next()
                                    op("dve", lambda e: e.tensor_tensor(out=t1[:], in0=pt[:, :], in1=BSB[:, g, :, :].rearrange("p a b -> p (a b)"), op=ALU.add),
                                       reads=[pb, bC], writes=[bt1])
                                    op("dve", lambda e: e.tensor_tensor(out=AO[:, g, :], in0=t1[:], in1=U[:, g, :], op=ALU.mult), reads=[bt1, bU], writes=[bAO])
                                dma("pool", a_AO[:, :, t * N:(t + 1) * N].rearrange("c p n -> p c n"), AO[:], reads=[bAO], writes=[tb["AO"][t]])
                                drain_dq(dq_per_tile)
                            drain_dq(len(DQ))
                            k.barrier()

                    if which == "B":
                        with contextlib.ExitStack() as es2:
                            ROPE = Ring(k, 1, [128, 2, N], F32, es2)
                            QTo, bQTo = QB, bQB
                            KTo = k.sb([128, KC, N], BF16, es2); bKTo = Buf()
                            VTo = MIX[:, :, :].rearrange("p (tc a) n -> p tc (a n)", a=2)
                            bVTo = bMIX
                            def b_loads(t):
                                X, bX = XT.next()
                                load_fm(X, bX, t, a_XS, tb["XS"])
                                mi, bmi = MIXIN.next()
                                dma("sp", mi[:, 0:4, :], a_AO[:, :, t * N:(t + 1) * N].rearrange("c p n -> p c n"), reads=[tb["AO"][t]], writes=[bmi])
                                dma("sp", mi[:, 4:8, :], a_BO[:, :, t * N:(t + 1) * N].rearrange("c p n -> p c n"), reads=[tb["BO"][t]], writes=[bmi])
                                return X, bX, mi, bmi

                            nxt = b_loads(0)
                            for t in range(NT):
                                X, bX, mi, bmi = nxt
                                add_proj(X, bX, "ev_out", s_ev_out, mi, bmi, post=("xa_norm", 0))
                                xattn(X, bX, 0, hoisted=True, post=("ffn2_norm", 0))
                                if t + 1 < NT:
                                    nxt = b_loads(t + 1)
                                ffn(X, bX, 1, hoisted=True, post=("ffn1_norm", 1))
                                rp, brp = ROPE.next()
                                dma("sp", rp[:, 0, :], c_cos[:, t * N:(t + 1) * N], writes=[brp])
                                dma("sp", rp[:, 1, :], c_sin[:, t * N:(t + 1) * N], writes=[brp])
                                ffn(X, bX, 2, hoisted=True)
                                store_fm(X, bX, t, a_XS, tb["XS"])
                                rmsnorm(X, bX, "mix_norm", 1)

                                def finqk(o, q, bq, rs, brs):
                                    isk = o // KC
                                    dst, bdst = (KTo, bKTo) if isk else (QTo, bQTo)
                                    op("dve", lambda e: e.scalar_tensor_tensor(out=q[:], in0=q[:], scalar=OQG[:, isk:isk + 1], in1=rs[:],
                                                                               op0=ALU.mult, op1=ALU.mult), reads=[bq, brs, bC], writes=[bq])
                                    pt, pb = k.bank()
                                    op("pe", lambda e: e.matmul(pt[:, :], lhsT=ROTM[:], rhs=q[:], start=True, stop=True), reads=[bC, bq], writes=[pb])
                                    t2, bt2 = TMP.next()
                                    op("dve", lambda e: e.tensor_tensor(out=t2[:], in0=pt[:, :], in1=rp[:, 1, :], op=ALU.mult), reads=[pb, brp], writes=[bt2])
                                    op("pool", lambda e: e.tensor_tensor(out=q[:], in0=q[:], in1=rp[:, 0, :], op=ALU.mult), reads=[bq, brp], writes=[bq])
                                    op("dve", lambda e: e.tensor_tensor(out=dst[:, o % KC, :], in0=q[:], in1=t2[:], op=ALU.add), reads=[bq, bt2], writes=[bdst])
                                headnorm_proj("od_qk", s_od_qk, 8, HB, bHB, N, BLK1, 1.0 / 64, 1, None, finqk)
                                dma("pool", a_QT[:, :, t * N:(t + 1) * N].rearrange("c p n -> p c n"), QTo[:], reads=[bQTo], writes=[tb["QT"][t]])
                                dma("pool", a_KT[:, :, t * N:(t + 1) * N].rearrange("c p n -> p c n"), KTo[:], reads=[bKTo], writes=[tb["KT"][t]])
                                dma("sp", WT[:], s_od_v, reads=wb["od_v"][0], writes=bWTl)
                                for tc in range(4):
                                    for half in range(2):
                                        pt, pb = k.bank()
                                        for kc in range(KC):
                                            op("pe", lambda e: e.matmul(pt[:, :], lhsT=HB[:, kc, tc * 128:(tc + 1) * 128], rhs=WT[:, kc, half * 512:(half + 1) * 512],
                                                                        start=(kc == 0), stop=(kc == KC - 1)), reads=[bHB] + bWTl, writes=[pb], inc=(kc == KC - 1))
                                        op("act", lambda e: e.copy(out=VTo[:, tc, half * 512:(half + 1) * 512], in_=pt[:, :]), reads=[pb], writes=[bVTo])
                                dma("pool", a_V[t * N:(t + 1) * N, :].rearrange("(tc p) d -> p tc d", p=128), VTo[:], reads=[bVTo], writes=[tb["V"][t]])
                            k.barrier()

                    if which == "C2":
                        with contextlib.ExitStack() as es2:
                            XO = Ring(k, 2, [128, D], F32, es2)
                            def c_loads(t):
                                X, bX = XT.next()
                                load_fm(X, bX, t, a_XS, tb["XS"])
                                mi, bmi = MIXIN.next()
                                dma("sp", mi[:], a_OT[:, :, t * N:(t + 1) * N].rearrange("c p n -> p c n"), reads=[tb["OT"][t]], writes=[bmi])
                                return X, bX, mi, bmi

                            nxt = c_loads(0)
                            for t in range(NT):
                                X, bX, mi, bmi = nxt
                                add_proj(X, bX, "od_out", s_od_out, mi, bmi, post=("xa_norm", 1))
                                xattn(X, bX, 1, hoisted=True, post=("ffn2_norm", 1))
                                if t + 1 < NT:
                                    nxt = c_loads(t + 1)
                                ffn(X, bX, 3, hoisted=True)
                                for tc in range(4):
                                    xo, bxo = XO.next()
                                    for half in range(2):
                                        pt, pb = k.bank()
                                        for c4 in range(4):
                                            c = half * 4 + c4
                                            op("pe", lambda e: e.transpose(out=pt[:, c4 * 128:(c4 + 1) * 128], in_=X[:, c, tc * 128:(tc + 1) * 128], identity=IDENT[:]),
                                               reads=[bX, bC], writes=[pb], inc=(c4 == 3))
                                        op("act", lambda e: e.copy(out=xo[:, half * 512:(half + 1) * 512], in_=pt[:, :]), reads=[pb], writes=[bxo])
                                    r0 = t * N + tc * 128
                                    dma("sp", y_d[r0:r0 + 128, :], xo[:], reads=[bxo])
                            k.barrier()

            def other_stage(which):
                if which == "R":
                    with contextlib.ExitStack() as es2:
                        PZ = 1024
                        NP = S // PZ
                        ZF = Ring(k, 1, [128, SMAX + 4], F32, es2)
                        HT = k.sb([128, SMAX], F32, es2); bHT = Buf()
                        R1 = Ring(k, 10, [128, PZ], F32, es2)
                        HP = Ring(k, 3, [128, PZ], F32, es2)
                        XCB = Ring(k, 3, [128, PZ], BF16, es2)
                        GZp = Ring(k, 2, [128, PZ], BF16, es2)
                        BOp = Ring(k, 2, [128, PZ], BF16, es2)
                        zfs = {}

                        def zf_load(c):
                            if c in zfs or c >= 4:
                                return
                            zf, bzf = ZF.next()
                            op("pool", lambda e: e.memset(zf[:, 0:2], 0.0), writes=[bzf])
                            op("pool", lambda e: e.memset(zf[:, S + 2:S + 4], 0.0), writes=[bzf])
                            dma("sp", zf[:, 2:2 + S], a_ZX[c, :, 0:S], reads=tb["ZX"][0:NT], writes=[bzf])
                            zfs[c] = (zf, bzf)

                        rjobs = []
                        for c in range(4):
                            for d in range(2):
                                for pi in (range(NP) if d == 0 else range(NP - 1, -1, -1)):
                                    rjobs.append(dict(c=c, d=d, pi=pi, first=(pi == (0 if d == 0 else NP - 1))))

                        def ph1(J):
                            c, d, p0 = J["c"], J["d"], J["pi"] * PZ
                            zf_load(c)
                            zf, bzf = zfs[c]
                            xc, bxc = R1.next()
                            op("dve", lambda e: e.tensor_scalar(out=xc[:], in0=zf[:, p0:p0 + PZ], scalar1=CW[:, 0, c:c + 1], scalar2=CB[:, c:c + 1],
                                                                op0=ALU.mult, op1=ALU.add), reads=[bzf, bC], writes=[bxc])
                            for tp in range(1, 4):
                                op("dve", lambda e: e.scalar_tensor_tensor(out=xc[:], in0=zf[:, p0 + tp:p0 + tp + PZ], scalar=CW[:, tp, c:c + 1], in1=xc[:],
                                                                           op0=ALU.mult, op1=ALU.add), reads=[bzf, bC, bxc], writes=[bxc])
                            xb, bxb = XCB.next()
                            op("act", lambda e: e.copy(out=xb[:], in_=xc[:]), reads=[bxc], writes=[bxb])
                            rg, brg = R1.next()
                            ig, big = R1.next()
                            for ai, (dst, bdst, bias) in enumerate(((rg, brg, BA), (ig, big, BI))):
                                for q4 in range(PZ // 512):
                                    pt, pb = k.bank()
                                    op("pe", lambda e: e.matmul(pt[:, :], lhsT=WBD[:, (d * 2 + ai) * 4 + c, :], rhs=xb[:, q4 * 512:(q4 + 1) * 512],
                                                                start=True, stop=True), reads=[bC, bxb], writes=[pb])
                                    op("act", lambda e: e.activation(out=dst[:, q4 * 512:(q4 + 1) * 512], in_=pt[:, :], func=AF.Sigmoid,
                                                                     bias=bias[:, d, c:c + 1]), reads=[pb, bC], writes=[bdst])
                            op("act", lambda e: e.activation(out=rg[:], in_=rg[:], func=AF.Exp, scale=NSP[:, d, c:c + 1]), reads=[brg, bC], writes=[brg])
                            mm, bmm = R1.next()
                            op("act", lambda e: e.activation(out=mm[:], in_=rg[:], func=AF.Square, scale=1.0 - 1e-6), reads=[brg], writes=[bmm])
                            op("act", lambda e: e.activation(out=mm[:], in_=mm[:], func=AF.Sqrt, scale=-1.0, bias=CST[:, 1:2]), reads=[bmm, bC], writes=[bmm])
                            if d == 1:
                                gz, bgz = GZp.next()
                                dma("sp", gz[:], a_GZ[c, :, p0:p0 + PZ], reads=tb["GZ"][p0 // N:(p0 + PZ) // N], writes=[bgz])
                                J["gz"] = (gz, bgz)
                            J["t"] = (xc, bxc, rg, brg, ig, big, mm, bmm)

                        prev = [None]

                        def ph2(J):
                            c, d, p0 = J["c"], J["d"], J["pi"] * PZ
                            xc, bxc, rg, brg, ig, big, mm, bmm = J["t"]
                            op("pool", lambda e: e.tensor_tensor(out=ig[:], in0=ig[:], in1=xc[:], op=ALU.mult), reads=[big, bxc], writes=[big])
                            op("pool", lambda e: e.tensor_tensor(out=ig[:], in0=ig[:], in1=mm[:], op=ALU.mult), reads=[big, bmm], writes=[big])
                            hp, bhp = HP.next()
                            pv = None if J["first"] else prev[0]
                            if d == 0:
                                init = 0.0 if pv is None else pv[0][:, PZ - 1:PZ]
                                op("dve", lambda e: e.tensor_tensor_scan(out=hp[:], data0=rg[:], data1=ig[:], initial=init, op0=ALU.mult, op1=ALU.add),
                                   reads=[brg, big] + ([pv[1]] if pv else []), writes=[bhp])
                                op("act", lambda e: e.copy(out=HT[:, p0:p0 + PZ], in_=hp[:]), reads=[bhp], writes=[bHT])
                            else:
                                init = 0.0 if pv is None else pv[0][:, 0:1]
                                op("dve", lambda e: e.tensor_tensor_scan(out=hp[:, ::-1], data0=rg[:, ::-1], data1=ig[:, ::-1], initial=init,
                                                                         op0=ALU.mult, op1=ALU.add),
                                   reads=[brg, big] + ([pv[1]] if pv else []), writes=[bhp])
                                gz, bgz = J["gz"]
                                tot, btot = R1.next()
                                op("pool", lambda e: e.tensor_tensor(out=tot[:], in0=hp[:], in1=HT[:, p0:p0 + PZ], op=ALU.add), reads=[bhp, bHT], writes=[btot])
                                bo, bbo = BOp.next()
                                op("pool", lambda e: e.tensor_tensor(out=bo[:], in0=tot[:], in1=gz[:], op=ALU.mult), reads=[btot, bgz], writes=[bbo])
                                dma("pool", a_BO[c, :, p0:p0 + PZ], bo[:], reads=[bbo], writes=tb["BO"][p0 // N:(p0 + PZ) // N])
                            prev[0] = (hp, bhp)

                        ph1(rjobs[0])
                        for ji, J in enumerate(rjobs):
                            if ji + 1 < len(rjobs):
                                ph1(rjobs[ji + 1])
                            ph2(J)
                        k.barrier()

                if which == "C1":
                    with contextlib.ExitStack() as es2:
                        KW = Ring(k, 2, [128, AB + 2048], BF16, es2)
                        QC = Ring(k, 2, [128, AB], BF16, es2)
                        VWB = Ring(k, 2, [128, 17, 2, 128], BF16, es2)
                        VWS = Ring(k, 6, [128, 5, 2, 128], BF16, es2)
                        VW16 = k.sb([128, 2, 16, 2, 128], BF16, es2); bVW16 = Buf()
                        ACC = Ring(k, 2, [128, 2, AB], F32, es2)
                        PF = Ring(k, 4, [128, 2, 4, 128], BF16, es2)
                        PM = Ring(k, 4, [128, 2, 4, 128], BF16, es2)
                        RD = k.sb([128, AB], F32, es2); bRD = Buf()
                        OC = Ring(k, 2, [128, AB], BF16, es2)
                        MV = k.sb([128, 6, 4, 128], BF16, es2); bMV = Buf()
                        variants = [(0, 1, 0, 1), (2, 1, 0, 1), (0, 1, 0, 3), (2, 1, 0, 3), (2, 1, 2, 1), (0, 3, 0, 3)]
                        IDB = k.sb([128, 128], BF16, es2)
                        for vi, var in enumerate(variants):
                            for ci, mt in enumerate(var):
                                op("pool", lambda e: e.tensor_copy(out=MV[:, vi, ci, :], in_=MASK[:, mt, 0, :]), reads=[bC], writes=[bMV])
                        op("pool", lambda e: e.tensor_scalar(out=MV[:], in0=MV[:], scalar1=1.0, scalar2=30000.0, op0=ALU.subtract, op1=ALU.mult),
                           reads=[bMV], writes=[bMV])
                        op("pool", lambda e: e.tensor_copy(out=IDB[:], in_=IDENT[:]), reads=[bC], writes=[bMV])
                        for i in range(2):
                            op("pool", lambda e: e.memset(KW.t[i][:], 0.0), writes=[KW.b[i]])
                        op("pool", lambda e: e.memset(VW16[:], 0.0), writes=[bVW16])
                        op("pool", lambda e: e.memset(VW16[:, :, :, :, 64:128], 1.0), writes=[bVW16])
                        for VWr in (VWB, VWS):
                            for i in range(len(VWr.t)):
                                op("pool", lambda e: e.memset(VWr.t[i][:], 0.0), writes=[VWr.b[i]])
                                op("pool", lambda e: e.memset(VWr.t[i][:, :, :, 64:128], 1.0), writes=[VWr.b[i]])
                        NB = S // AB
                        classes = [(d, r) for d in DILS if d < 16 for r in range(d)]

                        def load_vw(c, P0, d, r):
                            vw, bvw = (VWB if d == 1 else VWS).next()
                            ng = AB // (128 * d)
                            L = S // d
                            m0 = P0 // d
                            full = [i for i in range(ng + 1) if m0 - 64 + 128 * i >= 0 and m0 - 64 + 128 * i + 128 <= L]
                            if full:
                                i0, i1 = full[0], full[-1] + 1
                                rs = (m0 - 64 + 128 * i0) * d + r
                                nrow = 128 * (i1 - i0)
                                for h in range(2):
                                    src = a_V[rs:rs + (nrow - 1) * d + 1:d, c * 128 + h * 64:c * 128 + (h + 1) * 64]
                                    dma("sp", vw[:, i0:i1, h, 0:64], src.rearrange("(i p) e -> p i e", p=128),
                                        reads=tb["V"][rs // N:(rs + (nrow - 1) * d) // N + 1], writes=[bvw])
                            for i in range(ng + 1):
                                if i in full:
                                    continue
                                a0 = m0 - 64 + 128 * i
                                a_, b_ = max(a0, 0), min(a0 + 128, L)
                                if b_ <= a_:
                                    continue
                                src = a_V[a_ * d + r:(b_ - 1) * d + r + 1:d, c * 128:(c + 1) * 128]
                                dma("sp", vw[a_ - a0:b_ - a0, i, :, 0:64], src.rearrange("k (h e) -> k h e", h=2),
                                    reads=tb["V"][(a_ * d + r) // N:((b_ - 1) * d + r) // N + 1], writes=[bvw])
                            return vw, bvw

                        def load_vw16(ci):
                            if ci >= len(chunks):
                                return
                            blk, c = chunks[ci]
                            L = S // 16
                            m0 = blk * AB // 16
                            for i in range(2):
                                a0 = m0 - 64 + 128 * i
                                plo, phi = max(0, -a0), min(128, L - a0)
                                if phi <= plo:
                                    continue
                                rs, re = (a0 + plo) * 16, (a0 + phi) * 16
                                for h in range(2):
                                    src = a_V[rs:re, c * 128 + h * 64:c * 128 + (h + 1) * 64]
                                    dma("sp", VW16[plo:phi, i, :, h, 0:64], src.rearrange("(p r) e -> p r e", r=16),
                                        reads=tb["V"][rs // N:(re - 1) // N + 1], writes=[bVW16])

                        chunks = [(blk, c) for blk in range(NB) for c in range(KC)]
                        cstate = {}

                        def chunk_loads(ci):
                            if ci >= len(chunks) or ci in cstate:
                                return
                            blk, c = chunks[ci]
                            P0 = blk * AB
                            kw, bkw = KW.next()
                            lo, hi = max(0, P0 - 1024), min(S, P0 + AB + 1024)
                            dma("sp", kw[:, lo - (P0 - 1024):hi - (P0 - 1024)], a_KT[c, :, lo:hi], reads=tb["KT"][lo // N:hi // N], writes=[bkw])
                            qc, bqc = QC.next()
                            dma("sp", qc[:], a_QT[c, :, P0:P0 + AB], reads=tb["QT"][P0 // N:(P0 + AB) // N], writes=[bqc])
                            cstate[ci] = dict(kw=kw, bkw=bkw, qc=qc, bqc=bqc, vws={}, acc=None)

                        def ensure_vw(ci, k_idx):
                            while k_idx >= len(classes):
                                ci += 1
                                k_idx -= len(classes)
                            if ci >= len(chunks):
                                return
                            chunk_loads(ci)
                            st = cstate[ci]
                            if k_idx not in st["vws"]:
                                blk, c = chunks[ci]
                                d, r = classes[k_idx]
                                st["vws"][k_idx] = load_vw(c, blk * AB, d, r)

                        jobs = []
                        for ci, (blk, c) in enumerate(chunks):
                            pl = []
                            for d in DILS:
                                ng = AB // (128 * d)
                                if ng >= 2:
                                    for r in range(d):
                                        for g in range(0, ng, 2):
                                            pl.append(((d, r, g), (d, r, g + 1)))
                                else:
                                    for r in range(0, d, 2):
                                        pl.append(((d, r, 0), (d, r + 1, 0)))
                            for pi, pair in enumerate(pl):
                                jobs.append(dict(ci=ci, pair=pair, first=(pi == 0), last=(pi == len(pl) - 1)))

                        def s1(job):
                            ci = job["ci"]
                            blk, c = chunks[ci]
                            P0 = blk * AB
                            if job["first"]:
                                chunk_loads(ci)
                                chunk_loads(ci + 1)
                                cstate[ci]["acc"] = ACC.next()
                            st = cstate[ci]
                            kw, bkw, qc, bqc = st["kw"], st["bkw"], st["qc"], st["bqc"]
                            d = job["pair"][0][0]
                            L = S // d
                            m0 = P0 // d
                            combos = []
                            for (d_, r, g) in job["pair"]:
                                if d_ < 16:
                                    kidx = classes.index((d_, r))
                                    ensure_vw(ci, kidx)
                                    ensure_vw(ci, kidx + 1)
                                    ensure_vw(ci, kidx + 2)
                                    vw, bvw = st["vws"][kidx]
                                else:
                                    vw, bvw = VW16[:, :, r, :, :], bVW16
                                mabs = m0 + 128 * g
                                for part in range(2):
                                    i = g + part
                                    col0 = 1024 - 64 * d + 128 * d * i + r
                                    if part == 0:
                                        mt = 2 if mabs == 0 else 0
                                    else:
                                        mt = 3 if mabs + 128 == L else 1
                                    combos.append((r, g, i, col0, mt, vw, bvw))
                            vi = variants.index(tuple(cb[4] for cb in combos))
                            pm, bpm = PM.next()
                            sbk = [k.bank(), k.bank()]
                            for hp in range(2):
                                pss, pssb = sbk[hp]
                                op("pe", lambda e: e.matmul(pss[:, :], lhsT=IDB[:], rhs=MV[:, vi, :, :].rearrange("p c n -> p (c n)"),
                                                            start=True, stop=False), reads=[bMV], writes=[pssb], inc=False)
                            for cj, (r, g, i, col0, mt, vw, bvw) in enumerate(combos):
                                q0 = 128 * g * d + r
                                for hp in range(2):
                                    pss, pssb = sbk[hp]
                                    op("pe", lambda e: e.matmul(pss[:, cj * 128:(cj + 1) * 128],
                                                                lhsT=kw[hp * 64:(hp + 1) * 64, col0:col0 + 127 * d + 1:d],
                                                                rhs=qc[hp * 64:(hp + 1) * 64, q0:q0 + 127 * d + 1:d],
                                                                start=False, stop=(cj == 3)), reads=[bkw, bqc], writes=[pssb], inc=(cj == 3))
                            for hp in range(2):
                                pss, pssb = sbk[hp]
                                op("act", lambda e: e.activation(out=pm[:, hp, :, :], in_=pss[:, :].rearrange("p (c n) -> p c n", c=4), func=AF.Exp),
                                   reads=[pssb], writes=[bpm])
                            job["combos"], job["pm"], job["bpm"] = combos, pm, bpm

                        def s2(job):
                            ci = job["ci"]
                            blk, c = chunks[ci]
                            P0 = blk * AB
                            st = cstate[ci]
                            acc, bacc = st["acc"]
                            combos, pm, bpm = job["combos"], job["pm"], job["bpm"]
                            d = job["pair"][0][0]
                            pso, psob = k.bank()
                            for it in range(2):
                                for hp in range(2):
                                    for part in range(2):
                                        r, g, i, col0, mt, vw, bvw = combos[it * 2 + part]
                                        op("pe", lambda e: e.matmul(pso[:, (it * 2 + hp) * 128:(it * 2 + hp + 1) * 128], lhsT=vw[:, i, hp, :],
                                                                    rhs=pm[:, hp, it * 2 + part, :], start=(part == 0), stop=(part == 1)),
                                           reads=[bvw, bpm], writes=[psob], inc=(it == 1 and hp == 1 and part == 1))
                            r0_, g0_ = combos[0][0], combos[0][1]
                            src_ap = pso[:, :].rearrange("p (i h q) -> p h i q", i=2, h=2)
                            if d < 16:
                                q0 = 128 * g0_ * d + r0_
                                acc_ap = acc[:, :, q0:q0 + 255 * d + 1:d].rearrange("p h (i q) -> p h i q", i=2)
                            else:
                                acc_ap = acc[:, :, :].rearrange("p h (q s) -> p h s q", s=16)[:, :, r0_:r0_ + 2, :]
                            if d == 1:
                                op("dve", lambda e: e.tensor_copy(out=acc_ap, in_=src_ap), reads=[psob], writes=[bacc])
                            else:
                                op("dve", lambda e: e.tensor_tensor(out=acc_ap, in0=acc_ap, in1=src_ap, op=ALU.add), reads=[psob, bacc], writes=[bacc])
                            if job["last"]:
                                oc, boc = OC.next()
                                op("act", lambda e: e.activation(out=acc[64:128, :, :], in_=acc[64:128, :, :], func=AF.Ln), reads=[bacc], writes=[bacc])
                                op("act", lambda e: e.activation(out=acc[64:128, :, :], in_=acc[64:128, :, :], func=AF.Exp, scale=-1.0), reads=[bacc], writes=[bacc])
                                for hp in range(2):
                                    op("dve", lambda e: e.tensor_copy(out=RD[0:64, :], in_=acc[64:128, hp, :]), reads=[bacc], writes=[bRD])
                                    if hp == 0:
                                        op("dve", lambda e: e.tensor_tensor(out=oc[0:64, :], in0=acc[0:64, 0, :], in1=RD[0:64, :], op=ALU.mult),
                                           reads=[bacc, bRD], writes=[boc])
                                    else:
                                        op("dve", lambda e: e.tensor_tensor(out=RD[0:64, :], in0=acc[0:64, 1, :], in1=RD[0:64, :], op=ALU.mult),
                                           reads=[bacc, bRD], writes=[bRD])
                                        op("dve", lambda e: e.tensor_copy(out=oc[64:128, :], in_=RD[0:64, :]), reads=[bRD], writes=[boc])
                                dma("pool", a_OT[c, :, P0:P0 + AB], oc[:], reads=[boc], writes=tb["OT"][P0 // N:(P0 + AB) // N])
                                del cstate[ci]["vws"]
                                load_vw16(ci + 1)

                        load_vw16(0)
                        s1(jobs[0])
                        s1(jobs[1])
                        for ji, job in enumerate(jobs):
                            if ji + 2 < len(jobs):
                                s1(jobs[ji + 2])
                            s2(job)
                        k.barrier()


            stop = False
            for stg in ("A", "R", "B", "C1", "C2"):
                (ffn_stage if stg in ("A", "B", "C2") else other_stage)(stg)
                if debug == stg:
                    stop = True
                    break
            if stop:
                break
        k_unused = None
        k.finish()
    return nc


def _consts(SMAX):
    ident = np.eye(128, dtype=np.float32)
    rot = np.zeros((128, 128), np.float32)
    for hp in range(2):
        for i in range(8):
            rot[hp * 64 + i + 8, hp * 64 + i] = -1.0
            rot[hp * 64 + i, hp * 64 + i + 8] = 1.0
    half = 8
    inv = np.power(np.float32(ROPE_THETA), -np.arange(half, dtype=np.float32) * np.float32(2.0 / 16)).astype(np.float32)
    pos = np.arange(SMAX, dtype=np.float32)
    ang = (pos[:, None] * inv[None, :]).astype(np.float32)
    cos = np.ones((128, SMAX), np.float32)
    sin = np.zeros((128, SMAX), np.float32)
    for hp in range(2):
        for i in range(16):
            cos[hp * 64 + i] = np.cos(ang[:, i % 8])
            sin[hp * 64 + i] = np.sin(ang[:, i % 8])
    kk = np.arange(128)[:, None]
    qq = np.arange(128)[None, :]
    mA = (kk >= qq)
    mB = (kk <= qq)
    masks = np.stack([mA, mB, mA & (kk >= 64), mB & (kk < 64)], axis=1).astype(np.float32)
    masks = np.repeat(masks[:, :, None, :], 2, axis=2)
    return dict(c_ident=ident, c_rot=rot, c_cos=cos, c_sin=sin, c_mask=np.ascontiguousarray(masks))


_WNAMES = ["ffn1_norm", "ffn1_w_in", "ffn1_w_out", "mix_norm", "ev_w_in", "ev_w_out", "gm_ln_g", "gm_ln_b", "gm_w_s", "gm_b_s",
           "rg_conv_w", "rg_conv_b", "rg_w_a", "rg_b_a", "rg_w_i", "rg_b_i", "rg_lam", "od_w_in", "od_w_out", "od_q_norm",
           "od_k_norm", "xa_norm", "xa_mem_norm", "xa_w_q", "xa_w_kv", "xa_w_o", "xa_q_norm", "xa_k_norm", "ffn2_norm",
           "ffn2_w_in", "ffn2_w_out"]


def kernel(x_prompt, x_sample, mem_prompt, mem_sample, **weights):
    n = 8
    S_list = (x_prompt.shape[1], x_sample.shape[1])
    nc = build(S_list)
    consts = _consts(max(S_list))
    wmap = {nm: np.ascontiguousarray(np.asarray(weights[nm], dtype=np.float32)) for nm in _WNAMES}
    in_maps = []
    for c in range(n):
        m = dict(wmap)
        m.update(consts)
        m["x0"] = np.ascontiguousarray(np.asarray(x_prompt[c], dtype=np.float32))
        m["x1"] = np.ascontiguousarray(np.asarray(x_sample[c], dtype=np.float32))
        m["m0"] = np.ascontiguousarray(np.asarray(mem_prompt[c], dtype=np.float32))
        m["m1"] = np.ascontiguousarray(np.asarray(mem_sample[c], dtype=np.float32))
        in_maps.append(m)
    res = run_bass_kernel_spmd(nc, in_maps, core_ids=list(range(n)))
    y0 = np.stack([np.asarray(r["y0"], dtype=np.float32) for r in res.results], axis=0)
    y1 = np.stack([np.asarray(r["y1"], dtype=np.float32) for r in res.results], axis=0)
    return (y0, y1)
```

```python
import contextlib
import math
import numpy as np
import concourse.bass as bass
import concourse.mybir as mybir
from concourse.bass_utils import run_bass_kernel_spmd

F32 = mybir.dt.float32
BF16 = mybir.dt.bfloat16
ALU = mybir.AluOpType
AF = mybir.ActivationFunctionType

D = 1024
KC = 8
DFF = 2816
NJ = DFF // 128
N = 512
NMEM = 256
EPS = 1e-6
ROPE_THETA = 500000.0
GELU_C = 1.5957691216057308
AB = 2048
DILS = (1, 4, 16)


class Buf:
    __slots__ = ("w", "r")

    def __init__(self):
        self.w = None
        self.r = {}


class Eng:
    def __init__(self, name, e, sem):
        self.name = name
        self.e = e
        self.sem = sem
        self.cnt = 0
        self.known = {}
        self.pending = False


class K:
    def __init__(self, nc, es):
        self.nc = nc
        self.es = es
        self.sems = []
        self.E = {}
        for name, e in (("pe", nc.tensor), ("act", nc.scalar), ("dve", nc.vector),
                        ("pool", nc.gpsimd), ("sp", nc.sync)):
            self.E[name] = Eng(name, e, self.mksem("c_" + name))
        self.rings = {}
        self.ring_pos = {}
        self.ring_val = {}
        for q, n in (("sp", 10), ("pool", 6), ("act", 2)):
            self.rings[q] = [self.mksem("d_%s%d" % (q, i)) for i in range(n)]
            self.ring_pos[q] = 0
            for s in self.rings[q]:
                self.ring_val[s] = 0
        self.banks = []
        self.bank_bufs = []
        for i in range(8):
            t = es.enter_context(nc.psum_tensor("ps%d" % i, [128, 512], F32))
            self.banks.append(t)
            self.bank_bufs.append(Buf())
        self.bank_i = 0
        self.nalloc = 0

    def mksem(self, name):
        h = self.es.enter_context(self.nc.semaphore(name))
        self.sems.append(h)
        return len(self.sems) - 1

    def sb(self, shape, dtype, es=None):
        self.nalloc += 1
        nb = int(np.prod(shape[1:])) * (4 if dtype == F32 else 2)
        nb = (nb + 31) // 32 * 32
        self.cur = getattr(self, "cur", 0) + nb
        self.peak = max(getattr(self, "peak", 0), self.cur)
        self.rem = min(getattr(self, "rem", 1 << 30), self.nc.sbuf_bytes_remaining)
        t = (es or self.es).enter_context(self.nc.sbuf_tensor("t%d" % self.nalloc, list(shape), dtype))

        def rel():
            self.cur -= nb
        (es or self.es).callback(rel)
        return t

    def bank(self):
        i = self.bank_i
        self.bank_i = (i + 1) % 8
        return self.banks[i], self.bank_bufs[i]

    def need(self, E, s, v):
        if E.known.get(s, 0) >= v:
            return
        E.e.wait_ge(self.sems[s], v)
        E.known[s] = v

    def _deps(self, E, reads, writes):
        for b in reads:
            if b.w is not None:
                self._dep(E, b.w)
        for b in writes:
            if b.w is not None:
                self._dep(E, b.w)
            for s, v in b.r.items():
                self._dep(E, (s, v))

    def _dep(self, E, ev):
        s, v = ev
        if s == E.sem and E.name == "pe":
            return
        self.need(E, s, v)

    def _record(self, ev, reads, writes):
        s, v = ev
        for b in reads:
            if b.r.get(s, 0) < v:
                b.r[s] = v
        for b in writes:
            b.w = ev
            b.r = {}

    def op(self, eng, fn, reads=(), writes=(), inc=True):
        E = self.E[eng]
        self._deps(E, reads, writes)
        ins = fn(E.e)
        if inc:
            E.cnt += 1
            ins.then_inc(self.sems[E.sem], 1)
            ev = (E.sem, E.cnt)
            E.pending = False
        else:
            assert eng == "pe"
            ev = (E.sem, E.cnt + 1)
            E.pending = True
        self._record(ev, reads, writes)

    def dma(self, q, out, in_, reads=(), writes=(), **kw):
        E = self.E[q]
        self._deps(E, reads, writes)
        ring = self.rings[q]
        s = ring[self.ring_pos[q] % len(ring)]
        self.ring_pos[q] += 1
        prev = self.ring_val[s]
        self.need(E, s, prev)
        E.e.dma_start(out=out, in_=in_, **kw).then_inc(self.sems[s], 16)
        self.ring_val[s] = prev + 16
        self._record((s, prev + 16), reads, writes)

    def barrier(self):
        for E in self.E.values():
            assert not E.pending
        for E in self.E.values():
            for O in self.E.values():
                if O is not E and O.cnt > 0:
                    self.need(E, O.sem, O.cnt)
            for s, v in self.ring_val.items():
                if v > 0:
                    self.need(E, s, v)

    def finish(self):
        E = self.E["sp"]
        for O in self.E.values():
            assert not O.pending
            if O is not E and O.cnt > 0:
                self.need(E, O.sem, O.cnt)
        for s, v in self.ring_val.items():
            if v > 0:
                self.need(E, s, v)


class Ring:
    def __init__(self, k, n, shape, dtype, es=None):
        self.t = [k.sb(shape, dtype, es) for _ in range(n)]
        self.b = [Buf() for _ in range(n)]
        self.i = 0

    def next(self):
        i = self.i
        self.i = (i + 1) % len(self.t)
        return self.t[i], self.b[i]


def build(S_list, debug=False):
    nc = bass.Bass("TRN2", target_bir_lowering=False)
    SMAX = max(S_list)

    def din(name, shape):
        return nc.dram_tensor(name, list(shape), F32, kind="ExternalInput").ap()

    def dscr(name, shape, dtype):
        kind = "ExternalOutput" if (debug and name.startswith("a_")) else "Internal"
        return nc.dram_tensor(name, list(shape), dtype, kind=kind).ap()

    x_in = [din("x%d" % i, [S, D]) for i, S in enumerate(S_list)]
    m_in = [din("m%d" % i, [NMEM, D]) for i in range(len(S_list))]
    y_out = [nc.dram_tensor("y%d" % i, [S, D], F32, kind="ExternalOutput").ap() for i, S in enumerate(S_list)]
    W = {}
    for name, shape in (
        ("ffn1_norm", [2, D]), ("ffn1_w_in", [2, D, 2 * DFF]), ("ffn1_w_out", [2, DFF, D]), ("mix_norm", [2, D]),
        ("ev_w_in", [1, D, 2048]), ("ev_w_out", [1, D, D]), ("gm_ln_g", [1, 512]), ("gm_ln_b", [1, 512]),
        ("gm_w_s", [1, 4, 128, 128]), ("gm_b_s", [1, 4, 128]), ("rg_conv_w", [1, 4, 512]), ("rg_conv_b", [1, 512]),
        ("rg_w_a", [1, 2, 8, 64, 64]), ("rg_b_a", [1, 2, 512]), ("rg_w_i", [1, 2, 8, 64, 64]), ("rg_b_i", [1, 2, 512]),
        ("rg_lam", [1, 2, 512]), ("od_w_in", [1, D, 3 * D]), ("od_w_out", [1, D, D]), ("od_q_norm", [1, 64]),
        ("od_k_norm", [1, 64]), ("xa_norm", [2, D]), ("xa_mem_norm", [2, D]), ("xa_w_q", [2, D, D]),
        ("xa_w_kv", [2, D, 2 * D]), ("xa_w_o", [2, D, D]), ("xa_q_norm", [2, 256]), ("xa_k_norm", [2, 256]),
        ("ffn2_norm", [2, D]), ("ffn2_w_in", [2, D, 2 * DFF]), ("ffn2_w_out", [2, DFF, D]),
    ):
        W[name] = din(name, shape)
    c_ident = din("c_ident", [128, 128])
    c_rot = din("c_rot", [128, 128])
    c_cos = din("c_cos", [128, SMAX])
    c_sin = din("c_sin", [128, SMAX])
    c_mask = din("c_mask", [128, 4, 2, 128])

    ffn_names = [("ffn1", 0), ("ffn2", 0), ("ffn1", 1), ("ffn2", 1)]
    s_win = [dscr("s_win%d" % i, [NJ, 128, KC, 256], BF16) for i in range(4)]
    s_wout = [dscr("s_wout%d" % i, [KC, 128, NJ, 128], BF16) for i in range(4)]
    s_ev_in = dscr("s_ev_in", [6, 128, KC, 256], BF16)
    s_ev_v = dscr("s_ev_v", [128, KC, 512], BF16)
    s_ev_out = dscr("s_ev_out", [4, 128, KC, 256], BF16)
    s_od_qk = dscr("s_od_qk", [8, 128, KC, 256], BF16)
    s_od_v = dscr("s_od_v", [128, KC, 1024], BF16)
    s_od_out = dscr("s_od_out", [4, 128, KC, 256], BF16)
    s_xq = [dscr("s_xq%d" % l, [4, 128, KC, 256], BF16) for l in range(2)]
    s_xk = [dscr("s_xk%d" % l, [4, 128, KC, 256], BF16) for l in range(2)]
    s_xv = [dscr("s_xv%d" % l, [128, KC, 1024], BF16) for l in range(2)]
    s_xo = [dscr("s_xo%d" % l, [4, 128, KC, 256], BF16) for l in range(2)]
    a_XS = dscr("a_XS", [KC, 128, SMAX], F32)
    a_AO = dscr("a_AO", [4, 128, SMAX], BF16)
    a_BO = dscr("a_BO", [4, 128, SMAX], BF16)
    a_ZX = dscr("a_ZX", [4, 128, SMAX], F32)
    a_GZ = dscr("a_GZ", [4, 128, SMAX], BF16)
    a_QT = dscr("a_QT", [KC, 128, SMAX], BF16)
    a_KT = dscr("a_KT", [KC, 128, SMAX], BF16)
    a_V = dscr("a_V", [SMAX, D], BF16)
    a_OT = dscr("a_OT", [KC, 128, SMAX], BF16)
    NTMAX = SMAX // N
    tb = {nm: [Buf() for _ in range(NTMAX)] for nm in ("XS", "AO", "BO", "ZX", "GZ", "QT", "KT", "V", "OT")}
    dbg = {}

    with contextlib.ExitStack() as es:
        k = K(nc, es)
        op, dma = k.op, k.dma

        IDENT = k.sb([128, 128], F32)
        ROTM = k.sb([128, 128], F32)
        ONESB = k.sb([128, 128], BF16)
        BLK1 = k.sb([128, 128], BF16)
        MASK = k.sb([128, 4, 2, 128], BF16)
        CST = k.sb([128, 4], F32)
        bC = Buf()
        G = {}
        for nm in ("ffn1_norm", "ffn2_norm", "mix_norm", "xa_norm", "xa_mem_norm"):
            G[nm] = k.sb([128, 2, KC], F32)
        XQG = k.sb([128, 2, 2], F32)
        XKG = k.sb([128, 2, 2], F32)
        OQG = k.sb([128, 2], F32)
        CW = k.sb([128, 4, 4], F32)
        CB = k.sb([128, 4], F32)
        BA = k.sb([128, 2, 4], F32)
        BI = k.sb([128, 2, 4], F32)
        NSP = k.sb([128, 2, 4], F32)
        WBD = k.sb([128, 16, 128], BF16)
        WST = k.sb([128, 4, 128], BF16)
        LNG = k.sb([128, 512], F32)
        LNB = k.sb([128, 512], F32)
        BSB = k.sb([128, 4, 4, 128], F32)
        KX = [k.sb([128, KC, NMEM], BF16) for _ in range(2)]
        VX = [k.sb([128, 2, D], BF16) for _ in range(2)]
        bKX = [Buf(), Buf()]
        bVX = [Buf(), Buf()]
        WG = Ring(k, 4, [128, KC, 256], BF16)

        nsq = lambda e: None
        with nc.allow_non_contiguous_dma(reason="tiny param loads"):
            dma("sp", IDENT[:], c_ident[:, :], writes=[bC])
            dma("sp", ROTM[:], c_rot[:, :], writes=[bC])
            for nm in G:
                for l in range(2):
                    dma("sp", G[nm][:, l, :], W[nm][l].rearrange("(c p) -> p c", p=128), writes=[bC])
            for l in range(2):
                dma("sp", XQG[:, l, :], W["xa_q_norm"][l].rearrange("(c p) -> p c", p=128), writes=[bC])
                dma("sp", XKG[:, l, :], W["xa_k_norm"][l].rearrange("(c p) -> p c", p=128), writes=[bC])
            for hp in range(2):
                dma("sp", OQG[hp * 64:(hp + 1) * 64, 0:1], W["od_q_norm"][0].rearrange("(p o) -> p o", o=1), writes=[bC])
                dma("sp", OQG[hp * 64:(hp + 1) * 64, 1:2], W["od_k_norm"][0].rearrange("(p o) -> p o", o=1), writes=[bC])
            for t in range(4):
                dma("sp", CW[:, t, :], W["rg_conv_w"][0, t].rearrange("(c p) -> p c", p=128), writes=[bC])
            dma("sp", CB[:], W["rg_conv_b"][0].rearrange("(c p) -> p c", p=128), writes=[bC])
            for d in range(2):
                dma("sp", BA[:, d, :], W["rg_b_a"][0, d].rearrange("(c p) -> p c", p=128), writes=[bC])
                dma("sp", BI[:, d, :], W["rg_b_i"][0, d].rearrange("(c p) -> p c", p=128), writes=[bC])
                dma("sp", NSP[:, d, :], W["rg_lam"][0, d].rearrange("(c p) -> p c", p=128), writes=[bC])
            dma("sp", LNG[:], W["gm_ln_g"][0].partition_broadcast(128), writes=[bC])
            dma("sp", LNB[:], W["gm_ln_b"][0].partition_broadcast(128), writes=[bC])
            for g in range(4):
                for tc in range(4):
                    dma("sp", BSB[:, g, tc, :], W["gm_b_s"][0, g].partition_broadcast(128), writes=[bC])
        with contextlib.ExitStack() as es0:
            STG = k.sb([128, 16, 128], F32, es0)
            MSTG = k.sb([128, 4, 2, 128], F32, es0)
            WSS = k.sb([128, 4, 128], F32, es0)
            bS = Buf()
            op("pool", lambda e: e.memset(STG[:], 0.0), writes=[bS])
            for d in range(2):
                for ai, nm in enumerate(("rg_w_a", "rg_w_i")):
                    for c in range(4):
                        for hp in range(2):
                            dma("sp", STG[hp * 64:(hp + 1) * 64, (d * 2 + ai) * 4 + c, hp * 64:(hp + 1) * 64],
                                W[nm][0, d, 2 * c + hp], writes=[bS])
            dma("sp", MSTG[:], c_mask[:, :, :, :], writes=[bS])
            for g in range(4):
                dma("sp", WSS[:, g, :], W["gm_w_s"][0, g], writes=[bS])
            op("dve", lambda e: e.tensor_copy(out=WBD[:], in_=STG[:]), reads=[bS], writes=[bC])
            op("dve", lambda e: e.tensor_copy(out=MASK[:], in_=MSTG[:]), reads=[bS], writes=[bC])
            op("dve", lambda e: e.memset(ONESB[:], 1.0), writes=[bC])
            op("dve", lambda e: e.memset(BLK1[:], 0.0), writes=[bC])
            op("dve", lambda e: e.memset(BLK1[0:64, 0:64], 1.0), writes=[bC])
            op("dve", lambda e: e.memset(BLK1[64:128, 64:128], 1.0), writes=[bC])
            op("dve", lambda e: e.memset(CST[:, 0:1], EPS), writes=[bC])
            op("dve", lambda e: e.memset(CST[:, 1:2], 1.0), writes=[bC])
            op("dve", lambda e: e.tensor_scalar(out=XKG[:], in0=XKG[:], scalar1=1.0 / 16.0, scalar2=None, op0=ALU.mult), reads=[bC], writes=[bC])
            op("dve", lambda e: e.tensor_scalar(out=OQG[:, 0:1], in0=OQG[:, 0:1], scalar1=0.125, scalar2=None, op0=ALU.mult), reads=[bC], writes=[bC])
            op("act", lambda e: e.activation(out=NSP[:], in_=NSP[:], func=AF.Exp, scale=-1.0), reads=[bC], writes=[bC])
            op("act", lambda e: e.activation(out=NSP[:], in_=NSP[:], func=AF.Ln, bias=CST[:, 1:2]), reads=[bC], writes=[bC])
            op("dve", lambda e: e.tensor_scalar(out=NSP[:], in0=NSP[:], scalar1=-8.0, scalar2=None, op0=ALU.mult), reads=[bC], writes=[bC])
            for g in range(4):
                pt, pb = k.bank()
                op("pe", lambda e: e.transpose(out=pt[:, 0:128], in_=WSS[:, g, :], identity=IDENT[:]), reads=[bS, bC], writes=[pb])
                op("act", lambda e: e.copy(out=WST[:, g, :], in_=pt[:, 0:128]), reads=[pb], writes=[bC])
            k.barrier()

        wb = {}
        DQ = []
        defer = [False]

        def cdma(out, in_, b):
            if defer[0]:
                DQ.append((out, in_, b))
            else:
                dma("pool", out, in_, writes=[b])

        def drain_dq(n):
            for _ in range(min(n, len(DQ))):
                out, in_, b = DQ.pop(0)
                dma("pool", out, in_, writes=[b])

        def conv_fm(key, src, dst, col_pairs):
            bl = []
            for g, cols in enumerate(col_pairs):
                bg = []
                for h, c0 in enumerate(cols):
                    b = Buf()
                    cdma(dst[g, :, :, h * 128:(h + 1) * 128], src[:, c0:c0 + 128].rearrange("(kc p) c -> p kc c", p=128), b)
                    bg.append(b)
                bl.append(bg)
            wb[key] = bl

        def conv_wout(key, src, dst):
            bl = []
            for c in range(KC):
                bs = []
                for j0 in (0, NJ // 2):
                    b = Buf()
                    cdma(dst[c, :, j0:j0 + NJ // 2, :],
                         src[j0 * 128:(j0 + NJ // 2) * 128, c * 128:(c + 1) * 128].rearrange("(j p) c -> p j c", p=128), b)
                    bs.append(b)
                bl.append(bs)
            wb[key] = bl

        def conv_tm(key, src, dst, c0, ncols):
            bl = []
            for kc in range(KC):
                b = Buf()
                cdma(dst[:, kc, :], src[kc * 128:(kc + 1) * 128, c0:c0 + ncols], b)
                bl.append(b)
            wb[key] = [bl]

        def seqpairs(c0, n):
            return [(c0 + 256 * g, c0 + 256 * g + 128) for g in range(n)]

        def conv_ffn(i):
            nm, l = ffn_names[i]
            conv_fm("win%d" % i, W[nm + "_w_in"][l], s_win[i], [(j * 128, DFF + j * 128) for j in range(NJ)])
            conv_wout("wout%d" % i, W[nm + "_w_out"][l], s_wout[i])

        for l in range(2):
            conv_fm("xk%d" % l, W["xa_w_kv"][l], s_xk[l], seqpairs(0, 4))
            conv_tm("xv%d" % l, W["xa_w_kv"][l], s_xv[l], 1024, 1024)
        conv_ffn(0)
        conv_fm("ev_in", W["ev_w_in"][0], s_ev_in, seqpairs(0, 2) + seqpairs(1024, 2) + seqpairs(1536, 2))
        conv_tm("ev_v", W["ev_w_in"][0], s_ev_v, 512, 512)
        defer[0] = True
        conv_fm("ev_out", W["ev_w_out"][0], s_ev_out, seqpairs(0, 4))
        conv_fm("xq0", W["xa_w_q"][0], s_xq[0], seqpairs(0, 4))
        conv_fm("xo0", W["xa_w_o"][0], s_xo[0], seqpairs(0, 4))
        conv_ffn(1)
        conv_ffn(2)
        conv_fm("od_qk", W["od_w_in"][0], s_od_qk, seqpairs(0, 8))
        conv_tm("od_v", W["od_w_in"][0], s_od_v, 2048, 1024)
        conv_fm("od_out", W["od_w_out"][0], s_od_out, seqpairs(0, 4))
        conv_fm("xq1", W["xa_w_q"][1], s_xq[1], seqpairs(0, 4))
        conv_fm("xo1", W["xa_w_o"][1], s_xo[1], seqpairs(0, 4))
        conv_ffn(3)
        defer[0] = False
        dq_per_tile = -(-len(DQ) // (S_list[0] // N))

        def proj_fm(key, scr, ngroups, SRC, bSRC, ncols, consumer, hook=None, delay=False):
            pend_a, pend_b = [], []
            for g in range(ngroups):
                wt, wbuf = WG.next()
                dma("sp", wt[:], scr[g], reads=wb[key][g], writes=[wbuf])
                for h in range(2):
                    pt, pb = k.bank()
                    for kc in range(KC):
                        op("pe", lambda e: e.matmul(pt[:, 0:ncols], lhsT=wt[:, kc, h * 128:(h + 1) * 128], rhs=SRC[:, kc, 0:ncols],
                                                    start=(kc == 0), stop=(kc == KC - 1)),
                           reads=[wbuf, bSRC], writes=[pb], inc=(kc == KC - 1))
                    if hook is not None and g == 0 and h == 0:
                        hook()
                    if not delay:
                        r = consumer(2 * g + h, pt, pb)
                        if r is not None:
                            r()
                    else:
                        if len(pend_b) > 1:
                            pend_b.pop(0)()
                        if pend_a:
                            r = consumer(*pend_a.pop(0))
                            if r is not None:
                                pend_b.append(r)
                        pend_a.append((2 * g + h, pt, pb))
            while pend_a or pend_b:
                if pend_b:
                    pend_b.pop(0)()
                if pend_a:
                    r = consumer(*pend_a.pop(0))
                    if r is not None:
                        pend_b.append(r)

        def rms_stats(SQ, bSQ, nch, ncols, lhs, inv_n, RS, bRS):
            pt, pb = k.bank()
            for c in range(nch):
                op("pe", lambda e: e.matmul(pt[:, 0:ncols], lhsT=lhs[:], rhs=SQ[c], start=(c == 0), stop=(c == nch - 1)),
                   reads=[bSQ, bC], writes=[pb], inc=(c == nch - 1))
            op("act", lambda e: e.activation(out=RS[:, 0:ncols], in_=pt[:, 0:ncols], func=AF.Ln, scale=inv_n, bias=CST[:, 0:1]),
               reads=[pb, bC], writes=[bRS])
            op("act", lambda e: e.activation(out=RS[:, 0:ncols], in_=RS[:, 0:ncols], func=AF.Exp, scale=-0.5), reads=[bRS], writes=[bRS])

        def gelu_from_psum(pt, pb, ncols, OUT, bOUT, T1, bT1):
            op("act", lambda e: e.activation(out=T1, in_=pt[:, 0:ncols], func=AF.Square), reads=[pb], writes=[bT1])
            op("dve", lambda e: e.tensor_scalar(out=T1, in0=T1, scalar1=0.044715, scalar2=1.0, op0=ALU.mult, op1=ALU.add),
               reads=[bT1], writes=[bT1])
            op("dve", lambda e: e.tensor_tensor(out=T1, in0=T1, in1=pt[:, 0:ncols], op=ALU.mult), reads=[bT1, pb], writes=[bT1])
            op("act", lambda e: e.activation(out=T1, in_=T1, func=AF.Sigmoid, scale=GELU_C), reads=[bT1], writes=[bT1])
            op("dve", lambda e: e.tensor_tensor(out=OUT, in0=T1, in1=pt[:, 0:ncols], op=ALU.mult), reads=[bT1, pb], writes=[bOUT])

        for si, S in enumerate(S_list):
            NT = S // N
            x_d, m_d, y_d = x_in[si], m_in[si], y_out[si]

            def ffn_stage(which):
                with contextlib.ExitStack() as es1:
                    XT = Ring(k, 2, [128, KC, N], F32, es1)
                    HB = k.sb([128, KC, N], BF16, es1); bHB = Buf()
                    SQ = k.sb([128, KC, N], BF16, es1); bSQ = Buf()
                    ACTB = k.sb([128, NJ, N], BF16, es1); bACT = [Buf() for _ in range(NJ)]
                    WO = Ring(k, 2, [128, NJ, 128], BF16, es1)
                    TMP = Ring(k, 10, [128, N], F32, es1)
                    RS = k.sb([128, N], F32, es1); bRS = Buf()
                    XIN = Ring(k, 4, [128, D], F32, es1) if which == "A" else None
                    if which != "A":
                        MIX = k.sb([128, KC, N], BF16, es1); bMIX = Buf()
                        QB = k.sb([128, KC, N], BF16, es1); bQB = Buf()
                        PT = Ring(k, 4, [128, 2, N], BF16, es1)
                        MIXIN = Ring(k, 1, [128, KC, N], BF16, es1)
                    WT = ACTB[:, 0:16, :].rearrange("p (kc a) n -> p kc (a n)", a=2)
                    bWTl = bACT[0:16]

                    def rmsnorm(X, bX, gname, l, ncols=N):
                        op("act", lambda e: e.activation(out=SQ[:, :, 0:ncols], in_=X[:, :, 0:ncols], func=AF.Square), reads=[bX], writes=[bSQ])
                        rms_stats([SQ[:, c, 0:ncols] for c in range(KC)], bSQ, KC, ncols, ONESB, 1.0 / D, RS, bRS)
                        for c in range(KC):
                            op("dve", lambda e: e.scalar_tensor_tensor(out=HB[:, c, 0:ncols], in0=X[:, c, 0:ncols], scalar=G[gname][:, l, c:c + 1],
                                                                       in1=RS[:, 0:ncols], op0=ALU.mult, op1=ALU.mult),
                               reads=[bX, bRS, bC], writes=[bHB])

                    def hb_chunk(X, bX, c, post):
                        gname, l = post
                        op("dve", lambda e: e.tensor_scalar(out=HB[:, c, :], in0=X[:, c, :], scalar1=G[gname][:, l, c:c + 1], scalar2=None, op0=ALU.mult),
                           reads=[bX, bC], writes=[bHB])

                    def rmsnorm_def(X, bX, gname, l, hoisted=False):
                        for c in (range(KC) if not hoisted else []):
                            op("dve", lambda e: e.tensor_scalar(out=HB[:, c, :], in0=X[:, c, :], scalar1=G[gname][:, l, c:c + 1], scalar2=None, op0=ALU.mult),
                               reads=[bX, bC], writes=[bHB])
                        op("act", lambda e: e.activation(out=SQ[:, :, :], in_=X[:, :, :], func=AF.Square), reads=[bX], writes=[bSQ])

                        def stats():
                            rms_stats([SQ[:, c, :] for c in range(KC)], bSQ, KC, N, ONESB, 1.0 / D, RS, bRS)
                        return stats

                    def ffn(X, bX, i, hoisted=False, post=None):
                        nm, l = ffn_names[i]
                        stats = rmsnorm_def(X, bX, nm + "_norm", l, hoisted)
                        pend = []
                        for j in range(NJ):
                            wt, wbuf = WG.next()
                            dma("sp", wt[:], s_win[i][j], reads=wb["win%d" % i][j], writes=[wbuf])
                            pg, pgb = k.bank()
                            pu, pub = k.bank()
                            for h, (pt, pb) in enumerate(((pg, pgb), (pu, pub))):
                                for kc in range(KC):
                                    op("pe", lambda e: e.matmul(pt[:, :], lhsT=wt[:, kc, h * 128:(h + 1) * 128], rhs=HB[:, kc, :],
                                                                start=(kc == 0), stop=(kc == KC - 1)),
                                       reads=[wbuf, bHB], writes=[pb], inc=(kc == KC - 1))
                            pend.append((j, pg, pgb, pu, pub))
                            if j == 0:
                                continue
                            if j == 1:
                                stats()
                            while pend:
                                jj, pg_, pgb_, pu_, pub_ = pend.pop(0)
                                sg, bsg = TMP.next()
                                op("dve", lambda e: e.tensor_tensor(out=sg[:], in0=pg_[:, :], in1=RS[:], op=ALU.mult), reads=[pgb_, bRS], writes=[bsg])
                                op("act", lambda e: e.activation(out=sg[:], in_=sg[:], func=AF.Silu), reads=[bsg], writes=[bsg])
                                tu, btu = TMP.next()
                                op("dve", lambda e: e.tensor_tensor(out=tu[:], in0=pu_[:, :], in1=RS[:], op=ALU.mult), reads=[pub_, bRS], writes=[btu])
                                op("dve", lambda e: e.tensor_tensor(out=ACTB[:, jj, :], in0=tu[:], in1=sg[:], op=ALU.mult),
                                   reads=[btu, bsg], writes=[bACT[jj]])
                        for c in range(KC):
                            wt, wbuf = WO.next()
                            dma("sp", wt[:], s_wout[i][c], reads=wb["wout%d" % i][c], writes=[wbuf])
                            pt, pb = k.bank()
                            for j in range(NJ):
                                op("pe", lambda e: e.matmul(pt[:, :], lhsT=wt[:, j, :], rhs=ACTB[:, j, :], start=(j == 0), stop=(j == NJ - 1)),
                                   reads=[wbuf, bACT[j]], writes=[pb], inc=(j == NJ - 1))
                            op("dve", lambda e: e.scalar_tensor_tensor(out=X[:, c, :], in0=pt[:, :], scalar=0.5, in1=X[:, c, :],
                                                                       op0=ALU.mult, op1=ALU.add), reads=[pb, bX], writes=[bX])
                            if post is not None:
                                hb_chunk(X, bX, c, post)

                    def add_proj(X, bX, key, scr, SRC, bSRC, post=None):
                        def cons(oc, pt, pb):
                            op("dve", lambda e: e.tensor_tensor(out=X[:, oc, :], in0=pt[:, :], in1=X[:, oc, :], op=ALU.add),
                               reads=[pb, bX], writes=[bX])
                            if post is not None:
                                hb_chunk(X, bX, oc, post)
                        proj_fm(key, scr, 4, SRC, bSRC, N, cons)

                    def headnorm_proj(key, scr, ngroups, SRCH, bSRCH, ncols, lhs, inv_n, per, gain_fn, finish, rs_tok=False, hook=None):
                        hold = []

                        def cons(oc, pt, pb):
                            qf, bqf = TMP.next()
                            if rs_tok:
                                op("dve", lambda e: e.tensor_tensor(out=qf[:, 0:ncols], in0=pt[:, 0:ncols], in1=RS[:, 0:ncols], op=ALU.mult),
                                   reads=[pb, bRS], writes=[bqf])
                                op("act", lambda e: e.activation(out=SQ[:, oc % KC, 0:ncols], in_=qf[:, 0:ncols], func=AF.Square), reads=[bqf], writes=[bSQ])
                            else:
                                op("act", lambda e: e.copy(out=qf[:, 0:ncols], in_=pt[:, 0:ncols]), reads=[pb], writes=[bqf])
                                op("act", lambda e: e.activation(out=SQ[:, oc % KC, 0:ncols], in_=pt[:, 0:ncols], func=AF.Square), reads=[pb], writes=[bSQ])
                            hold.append((oc, qf, bqf))
                            if len(hold) == per:
                                rs, brs = TMP.next()
                                rms_stats([SQ[:, o % KC, 0:ncols] for o, _, _ in hold], bSQ, per, ncols, lhs, inv_n, rs, brs)
                                items = list(hold)
                                hold.clear()

                                def cont():
                                    for o, q, bq in items:
                                        finish(o, q, bq, rs, brs)
                                return cont
                            return None
                        proj_fm(key, scr, ngroups, SRCH, bSRCH, ncols, cons, hook=hook, delay=True)

                    def xattn(X, bX, l, hoisted=False, post=None):
                        stats = rmsnorm_def(X, bX, "xa_norm", l, hoisted)

                        def fin(o, q, bq, rs, brs):
                            op("dve", lambda e: e.scalar_tensor_tensor(out=QB[:, o, :], in0=q[:], scalar=XQG[:, l, (o % 2):(o % 2) + 1], in1=rs[:],
                                                                       op0=ALU.mult, op1=ALU.mult), reads=[bq, brs, bC], writes=[bQB])
                        headnorm_proj("xq%d" % l, s_xq[l], 4, HB, bHB, N, ONESB, 1.0 / 256, 2, None, fin, rs_tok=True, hook=stats)
                        pts = []
                        for h in range(4):
                            p_t, bp_t = PT.next()
                            pts.append((p_t, bp_t))
                            for mc in range(2):
                                pt, pb = k.bank()
                                for ec in range(2):
                                    op("pe", lambda e: e.matmul(pt[:, :], lhsT=KX[l][:, 2 * h + ec, mc * 128:(mc + 1) * 128], rhs=QB[:, 2 * h + ec, :],
                                                                start=(ec == 0), stop=(ec == 1)), reads=[bKX[l], bQB], writes=[pb], inc=(ec == 1))
                                op("act", lambda e: e.activation(out=p_t[:, mc, :], in_=pt[:, :], func=AF.Exp), reads=[pb], writes=[bp_t])
                        rds = []
                        for h in range(4):
                            p_t, bp_t = pts[h]
                            pd, pdb = k.bank()
                            for mc in range(2):
                                op("pe", lambda e: e.matmul(pd[:, :], lhsT=ONESB[:], rhs=p_t[:, mc, :], start=(mc == 0), stop=(mc == 1)),
                                   reads=[bC, bp_t], writes=[pdb], inc=(mc == 1))
                            rd, brd = TMP.next()
                            op("act", lambda e: e.activation(out=rd[:], in_=pd[:, :], func=AF.Ln), reads=[pdb], writes=[brd])
                            op("act", lambda e: e.activation(out=rd[:], in_=rd[:], func=AF.Exp, scale=-1.0), reads=[brd], writes=[brd])
                            rds.append((rd, brd))
                        for h in range(4):
                            p_t, bp_t = pts[h]
                            rd, brd = rds[h]
                            for ec in range(2):
                                po, pob = k.bank()
                                for mc in range(2):
                                    op("pe", lambda e: e.matmul(po[:, :], lhsT=VX[l][:, mc, (2 * h + ec) * 128:(2 * h + ec + 1) * 128], rhs=p_t[:, mc, :],
                                                                start=(mc == 0), stop=(mc == 1)), reads=[bVX[l], bp_t], writes=[pob], inc=(mc == 1))
                                op("dve", lambda e: e.tensor_tensor(out=MIX[:, 2 * h + ec, :], in0=po[:, :], in1=rd[:], op=ALU.mult),
                                   reads=[pob, brd], writes=[bMIX])
                        add_proj(X, bX, "xo%d" % l, s_xo[l], MIX, bMIX, post=post)

                    def load_fm(X, bX, t, scr, bufs, q="sp"):
                        dma(q, X[:], scr[:, :, t * N:(t + 1) * N].rearrange("c p n -> p c n"), reads=[bufs[t]], writes=[bX])

                    def store_fm(X, bX, t, scr, bufs, nch=KC, q="pool"):
                        dma(q, scr[:, :, t * N:(t + 1) * N].rearrange("c p n -> p c n"), X[:, 0:nch, :], reads=[bX], writes=[bufs[t]])

                    if which == "A":
                        MT, bMT = XT.next()
                        for mc in range(2):
                            xi, bxi = XIN.next()
                            dma("sp", xi[:], m_d[mc * 128:(mc + 1) * 128, :], writes=[bxi])
                            for half in range(2):
                                pt, pb = k.bank()
                                for c4 in range(4):
                                    c = half * 4 + c4
                                    op("pe", lambda e: e.transpose(out=pt[:, c4 * 128:(c4 + 1) * 128], in_=xi[:, c * 128:(c + 1) * 128], identity=IDENT[:]),
                                       reads=[bxi, bC], writes=[pb], inc=(c4 == 3))
                                op("dve", lambda e: e.tensor_copy(out=MT[:, half * 4:(half + 1) * 4, mc * 128:(mc + 1) * 128],
                                                                  in_=pt[:, :].rearrange("p (c n) -> p c n", c=4)), reads=[pb], writes=[bMT])
                        for l in range(2):
                            rmsnorm(MT, bMT, "xa_mem_norm", l, ncols=NMEM)

                            def fink(o, q, bq, rs, brs, l=l):
                                op("dve", lambda e: e.scalar_tensor_tensor(out=KX[l][:, o, :], in0=q[:, 0:NMEM], scalar=XKG[:, l, (o % 2):(o % 2) + 1],
                                                                           in1=rs[:, 0:NMEM], op0=ALU.mult, op1=ALU.mult), reads=[bq, brs, bC], writes=[bKX[l]])
                            headnorm_proj("xk%d" % l, s_xk[l], 4, HB, bHB, NMEM, ONESB, 1.0 / 256, 2, None, fink)
                            dma("sp", WT[:], s_xv[l], reads=wb["xv%d" % l][0], writes=bWTl)
                            for mc in range(2):
                                for half in range(2):
                                    pt, pb = k.bank()
                                    for kc in range(KC):
                                        op("pe", lambda e: e.matmul(pt[:, :], lhsT=HB[:, kc, mc * 128:(mc + 1) * 128], rhs=WT[:, kc, half * 512:(half + 1) * 512],
                                                                    start=(kc == 0), stop=(kc == KC - 1)), reads=[bHB] + bWTl, writes=[pb], inc=(kc == KC - 1))
                                    op("act", lambda e: e.copy(out=VX[l][:, mc, half * 512:(half + 1) * 512], in_=pt[:, :]), reads=[pb], writes=[bVX[l]])

                        with contextlib.ExitStack() as es2:
                            U = k.sb([128, 4, N], BF16, es2); bU = Buf()
                            ZXo = k.sb([128, 4, N], F32, es2); bZXo = Buf()
                            GZo = k.sb([128, 4, N], BF16, es2); bGZo = Buf()
                            VN = k.sb([128, 4, 512], BF16, es2); bVN = Buf()
                            AO = k.sb([128, 4, N], BF16, es2); bAO = Buf()
                            ST = k.sb([128, 12], F32, es2); bST = [Buf() for _ in range(6)]
                            def x_loads(t):
                                ld = []
                                for tc in range(4):
                                    xi, bxi = XIN.next()
                                    r0 = t * N + tc * 128
                                    dma("sp", xi[:], x_d[r0:r0 + 128, :], writes=[bxi])
                                    ld.append((xi, bxi))
                                return ld

                            def x_transposes(ld):
                                X, bX = XT.next()
                                for tc, (xi, bxi) in enumerate(ld):
                                    for half in range(2):
                                        pt, pb = k.bank()
                                        for c4 in range(4):
                                            c = half * 4 + c4
                                            op("pe", lambda e: e.transpose(out=pt[:, c4 * 128:(c4 + 1) * 128], in_=xi[:, c * 128:(c + 1) * 128], identity=IDENT[:]),
                                               reads=[bxi, bC], writes=[pb], inc=(c4 == 3))
                                        op("dve", lambda e: e.tensor_copy(out=X[:, half * 4:(half + 1) * 4, tc * 128:(tc + 1) * 128],
                                                                          in_=pt[:, :].rearrange("p (c n) -> p c n", c=4)), reads=[pb], writes=[bX])
                                for c in range(KC):
                                    hb_chunk(X, bX, c, ("ffn1_norm", 0))
                                return X, bX

                            nxtX = x_transposes(x_loads(0))
                            for t in range(NT):
                                X, bX = nxtX
                                if t + 1 < NT:
                                    ld_next = x_loads(t + 1)
                                ffn(X, bX, 0, hoisted=True)
                                store_fm(X, bX, t, a_XS, tb["XS"])
                                rmsnorm(X, bX, "mix_norm", 0)

                                def cons(oc, pt, pb):
                                    kind, c = oc // 4, oc % 4
                                    if kind == 1:
                                        op("act", lambda e: e.copy(out=ZXo[:, c, :], in_=pt[:, :]), reads=[pb], writes=[bZXo])
                                    else:
                                        t1, bt1 = TMP.next()
                                        if kind == 0:
                                            gelu_from_psum(pt, pb, N, U[:, c, :], bU, t1[:], bt1)
                                        else:
                                            gelu_from_psum(pt, pb, N, GZo[:, c, :], bGZo, t1[:], bt1)
                                proj_fm("ev_in", s_ev_in, 6, HB, bHB, N, cons)
                                dma("pool", a_ZX[:, :, t * N:(t + 1) * N].rearrange("c p n -> p c n"), ZXo[:], reads=[bZXo], writes=[tb["ZX"][t]])
                                dma("pool", a_GZ[:, :, t * N:(t + 1) * N].rearrange("c p n -> p c n"), GZo[:], reads=[bGZo], writes=[tb["GZ"][t]])
                                dma("sp", WT[:, :, 0:512], s_ev_v, reads=wb["ev_v"][0], writes=bWTl)
                                zb = []
                                for tc in range(4):
                                    pt, pb = k.bank()
                                    for kc in range(KC):
                                        op("pe", lambda e: e.matmul(pt[:, :], lhsT=HB[:, kc, tc * 128:(tc + 1) * 128], rhs=WT[:, kc, 0:512],
                                                                    start=(kc == 0), stop=(kc == KC - 1)), reads=[bHB] + bWTl, writes=[pb], inc=(kc == KC - 1))
                                    vg, bvg = TMP.next()
                                    t1, bt1 = TMP.next()
                                    zb.append((pt, pb, vg, bvg, t1, bt1))
                                for pt, pb, vg, bvg, t1, bt1 in zb:
                                    op("act", lambda e: e.activation(out=t1[:], in_=pt[:, :], func=AF.Square), reads=[pb], writes=[bt1])
                                for pt, pb, vg, bvg, t1, bt1 in zb:
                                    op("dve", lambda e: e.tensor_scalar(out=t1[:], in0=t1[:], scalar1=0.044715, scalar2=1.0, op0=ALU.mult, op1=ALU.add),
                                       reads=[bt1], writes=[bt1])
                                    op("dve", lambda e: e.tensor_tensor(out=t1[:], in0=t1[:], in1=pt[:, :], op=ALU.mult), reads=[bt1, pb], writes=[bt1])
                                for pt, pb, vg, bvg, t1, bt1 in zb:
                                    op("act", lambda e: e.activation(out=t1[:], in_=t1[:], func=AF.Sigmoid, scale=GELU_C), reads=[bt1], writes=[bt1])
                                for tc, (pt, pb, vg, bvg, t1, bt1) in enumerate(zb):
                                    op("dve", lambda e: e.tensor_tensor(out=vg[:], in0=t1[:], in1=pt[:, :], op=ALU.mult), reads=[bt1, pb], writes=[bvg])
                                    op("dve", lambda e: e.reduce_sum(out=ST[:, tc:tc + 1], in_=vg[:], axis=mybir.AxisListType.X), reads=[bvg], writes=[bST[tc]])
                                    op("dve", lambda e: e.tensor_scalar(out=ST[:, tc:tc + 1], in0=ST[:, tc:tc + 1], scalar1=-1.0 / 512, scalar2=None, op0=ALU.mult),
                                       reads=[bST[tc]], writes=[bST[tc]])
                                for tc, (pt, pb, vg, bvg, t1, bt1) in enumerate(zb):
                                    op("act", lambda e: e.activation(out=t1[:], in_=vg[:], func=AF.Square, bias=ST[:, tc:tc + 1]), reads=[bvg, bST[tc]], writes=[bt1])
                                for tc, (pt, pb, vg, bvg, t1, bt1) in enumerate(zb):
                                    op("dve", lambda e: e.reduce_sum(out=ST[:, 4 + tc:5 + tc], in_=t1[:], axis=mybir.AxisListType.X), reads=[bt1], writes=[bST[4]])
                                op("act", lambda e: e.activation(out=ST[:, 8:12], in_=ST[:, 4:8], func=AF.Sqrt, scale=1.0 / 512, bias=CST[:, 0:1]),
                                   reads=[bST[4], bC], writes=[bST[5]])
                                op("dve", lambda e: e.reciprocal(out=ST[:, 8:12], in_=ST[:, 8:12]), reads=[bST[5]], writes=[bST[5]])
                                for tc, (pt, pb, vg, bvg, t1, bt1) in enumerate(zb):
                                    op("dve", lambda e: e.tensor_scalar(out=vg[:], in0=vg[:], scalar1=ST[:, tc:tc + 1], scalar2=ST[:, 8 + tc:9 + tc], op0=ALU.add, op1=ALU.mult),
                                       reads=[bvg, bST[tc], bST[5]], writes=[bvg])
                                    op("dve", lambda e: e.tensor_tensor(out=vg[:], in0=vg[:], in1=LNG[:], op=ALU.mult), reads=[bvg, bC], writes=[bvg])
                                    op("dve", lambda e: e.tensor_tensor(out=VN[:, tc, :], in0=vg[:], in1=LNB[:], op=ALU.add), reads=[bvg, bC], writes=[bVN])
                                if t + 1 < NT:
                                    nxtX = x_transposes(ld_next)
                                for g in range(4):
                                    pt, pb = k.bank()
                                    for tc in range(4):
                                        op("pe", lambda e: e.matmul(pt[:, tc * 128:(tc + 1) * 128], lhsT=VN[:, tc, g * 128:(g + 1) * 128], rhs=WST[:, g, :],
                                                                    start=True, stop=True), reads=[bVN, bC], writes=[pb], inc=(tc == 3))
                                    t1, bt1 = TMP.next()
                                    op("dve", lambda e: e.tensor_tensor(out=t1[:], in0=pt[:, :], in1=BSB[:, g, :, :].rearrange("p a b -> p (a b)"), op=ALU.add),
                                       reads=[pb, bC], writes=[bt1])
                                    op("dve", lambda e: e.tensor_tensor(out=AO[:, g, :], in0=t1[:], in1=U[:, g, :], op=ALU.mult), reads=[bt1, bU], writes=[bAO])
                                dma("pool", a_AO[:, :, t * N:(t + 1) * N].rearrange("c p n -> p c n"), AO[:], reads=[bAO], writes=[tb["AO"][t]])
                                drain_dq(dq_per_tile)
                            drain_dq(len(DQ))
                            k.barrier()

                    if which == "B":
                        with contextlib.ExitStack() as es2:
                            ROPE = Ring(k, 1, [128, 2, N], F32, es2)
                            QTo, bQTo = QB, bQB
                            KTo = k.sb([128, KC, N], BF16, es2); bKTo = Buf()
                            VTo = MIX[:, :, :].rearrange("p (tc a) n -> p tc (a n)", a=2)
                            bVTo = bMIX
                            def b_loads(t):
                                X, bX = XT.next()
                                load_fm(X, bX, t, a_XS, tb["XS"])
                                mi, bmi = MIXIN.next()
                                dma("sp", mi[:, 0:4, :], a_AO[:, :, t * N:(t + 1) * N].rearrange("c p n -> p c n"), reads=[tb["AO"][t]], writes=[bmi])
                                dma("sp", mi[:, 4:8, :], a_BO[:, :, t * N:(t + 1) * N].rearrange("c p n -> p c n"), reads=[tb["BO"][t]], writes=[bmi])
                                return X, bX, mi, bmi

                            nxt = b_loads(0)
                            for t in range(NT):
                                X, bX, mi, bmi = nxt
                                add_proj(X, bX, "ev_out", s_ev_out, mi, bmi, post=("xa_norm", 0))
                                xattn(X, bX, 0, hoisted=True, post=("ffn2_norm", 0))
                                if t + 1 < NT:
                                    nxt = b_loads(t + 1)
                                ffn(X, bX, 1, hoisted=True, post=("ffn1_norm", 1))
                                rp, brp = ROPE.next()
                                dma("sp", rp[:, 0, :], c_cos[:, t * N:(t + 1) * N], writes=[brp])
                                dma("sp", rp[:, 1, :], c_sin[:, t * N:(t + 1) * N], writes=[brp])
                                ffn(X, bX, 2, hoisted=True)
                                store_fm(X, bX, t, a_XS, tb["XS"])
                                rmsnorm(X, bX, "mix_norm", 1)

                                def finqk(o, q, bq, rs, brs):
                                    isk = o // KC
                                    dst, bdst = (KTo, bKTo) if isk else (QTo, bQTo)
                                    op("dve", lambda e: e.scalar_tensor_tensor(out=q[:], in0=q[:], scalar=OQG[:, isk:isk + 1], in1=rs[:],
                                                                               op0=ALU.mult, op1=ALU.mult), reads=[bq, brs, bC], writes=[bq])
                                    pt, pb = k.bank()
                                    op("pe", lambda e: e.matmul(pt[:, :], lhsT=ROTM[:], rhs=q[:], start=True, stop=True), reads=[bC, bq], writes=[pb])
                                    t2, bt2 = TMP.next()
                                    op("dve", lambda e: e.tensor_tensor(out=t2[:], in0=pt[:, :], in1=rp[:, 1, :], op=ALU.mult), reads=[pb, brp], writes=[bt2])
                                    op("pool", lambda e: e.tensor_tensor(out=q[:], in0=q[:], in1=rp[:, 0, :], op=ALU.mult), reads=[bq, brp], writes=[bq])
                                    op("dve", lambda e: e.tensor_tensor(out=dst[:, o % KC, :], in0=q[:], in1=t2[:], op=ALU.add), reads=[bq, bt2], writes=[bdst])
                                headnorm_proj("od_qk", s_od_qk, 8, HB, bHB, N, BLK1, 1.0 / 64, 1, None, finqk)
                                dma("pool", a_QT[:, :, t * N:(t + 1) * N].rearrange("c p n -> p c n"), QTo[:], reads=[bQTo], writes=[tb["QT"][t]])
                                dma("pool", a_KT[:, :, t * N:(t + 1) * N].rearrange("c p n -> p c n"), KTo[:], reads=[bKTo], writes=[tb["KT"][t]])
                                dma("sp", WT[:], s_od_v, reads=wb["od_v"][0], writes=bWTl)
                                for tc in range(4):
                                    for half in range(2):
                                        pt, pb = k.bank()
                                        for kc in range(KC):
                                            op("pe", lambda e: e.matmul(pt[:, :], lhsT=HB[:, kc, tc * 128:(tc + 1) * 128], rhs=WT[:, kc, half * 512:(half + 1) * 512],
                                                                        start=(kc == 0), stop=(kc == KC - 1)), reads=[bHB] + bWTl, writes=[pb], inc=(kc == KC - 1))
                                        op("act", lambda e: e.copy(out=VTo[:, tc, half * 512:(half + 1) * 512], in_=pt[:, :]), reads=[pb], writes=[bVTo])
                                dma("pool", a_V[t * N:(t + 1) * N, :].rearrange("(tc p) d -> p tc d", p=128), VTo[:], reads=[bVTo], writes=[tb["V"][t]])
                            k.barrier()

                    if which == "C2":
                        with contextlib.ExitStack() as es2:
                            XO = Ring(k, 2, [128, D], F32, es2)
                            def c_loads(t):
                                X, bX = XT.next()
                                load_fm(X, bX, t, a_XS, tb["XS"])
                                mi, bmi = MIXIN.next()
                                dma("sp", mi[:], a_OT[:, :, t * N:(t + 1) * N].rearrange("c p n -> p c n"), reads=[tb["OT"][t]], writes=[bmi])
                                return X, bX, mi, bmi

                            nxt = c_loads(0)
                            for t in range(NT):
                                X, bX, mi, bmi = nxt
                                add_proj(X, bX, "od_out", s_od_out, mi, bmi, post=("xa_norm", 1))
                                xattn(X, bX, 1, hoisted=True, post=("ffn2_norm", 1))
                                if t + 1 < NT:
                                    nxt = c_loads(t + 1)
                                ffn(X, bX, 3, hoisted=True)
                                for tc in range(4):
                                    xo, bxo = XO.next()
                                    for half in range(2):
                                        pt, pb = k.bank()
                                        for c4 in range(4):
                                            c = half * 4 + c4
                                            op("pe", lambda e: e.transpose(out=pt[:, c4 * 128:(c4 + 1) * 128], in_=X[:, c, tc * 128:(tc + 1) * 128], identity=IDENT[:]),
                                               reads=[bX, bC], writes=[pb], inc=(c4 == 3))
                                        op("act", lambda e: e.copy(out=xo[:, half * 512:(half + 1) * 512], in_=pt[:, :]), reads=[pb], writes=[bxo])
                                    r0 = t * N + tc * 128
                                    dma("sp", y_d[r0:r0 + 128, :], xo[:], reads=[bxo])
                            k.barrier()

            def other_stage(which):
                if which == "R":
                    with contextlib.ExitStack() as es2:
                        PZ = 1024
                        NP = S // PZ
                        ZF = Ring(k, 1, [128, SMAX + 4], F32, es2)
                        HT = k.sb([128, SMAX], F32, es2); bHT = Buf()
                        R1 = Ring(k, 10, [128, PZ], F32, es2)
                        HP = Ring(k, 3, [128, PZ], F32, es2)
                        XCB = Ring(k, 3, [128, PZ], BF16, es2)
                        GZp = Ring(k, 2, [128, PZ], BF16, es2)
                        BOp = Ring(k, 2, [128, PZ], BF16, es2)
                        zfs = {}

                        def zf_load(c):
                            if c in zfs or c >= 4:
                                return
                            zf, bzf = ZF.next()
                            op("pool", lambda e: e.memset(zf[:, 0:2], 0.0), writes=[bzf])
                            op("pool", lambda e: e.memset(zf[:, S + 2:S + 4], 0.0), writes=[bzf])
                            dma("sp", zf[:, 2:2 + S], a_ZX[c, :, 0:S], reads=tb["ZX"][0:NT], writes=[bzf])
                            zfs[c] = (zf, bzf)

                        rjobs = []
                        for c in range(4):
                            for d in range(2):
                                for pi in (range(NP) if d == 0 else range(NP - 1, -1, -1)):
                                    rjobs.append(dict(c=c, d=d, pi=pi, first=(pi == (0 if d == 0 else NP - 1))))

                        def ph1(J):
                            c, d, p0 = J["c"], J["d"], J["pi"] * PZ
                            zf_load(c)
                            zf, bzf = zfs[c]
                            xc, bxc = R1.next()
                            op("dve", lambda e: e.tensor_scalar(out=xc[:], in0=zf[:, p0:p0 + PZ], scalar1=CW[:, 0, c:c + 1], scalar2=CB[:, c:c + 1],
                                                                op0=ALU.mult, op1=ALU.add), reads=[bzf, bC], writes=[bxc])
                            for tp in range(1, 4):
                                op("dve", lambda e: e.scalar_tensor_tensor(out=xc[:], in0=zf[:, p0 + tp:p0 + tp + PZ], scalar=CW[:, tp, c:c + 1], in1=xc[:],
                                                                           op0=ALU.mult, op1=ALU.add), reads=[bzf, bC, bxc], writes=[bxc])
                            xb, bxb = XCB.next()
                            op("act", lambda e: e.copy(out=xb[:], in_=xc[:]), reads=[bxc], writes=[bxb])
                            rg, brg = R1.next()
                            ig, big = R1.next()
                            for ai, (dst, bdst, bias) in enumerate(((rg, brg, BA), (ig, big, BI))):
                                for q4 in range(PZ // 512):
                                    pt, pb = k.bank()
                                    op("pe", lambda e: e.matmul(pt[:, :], lhsT=WBD[:, (d * 2 + ai) * 4 + c, :], rhs=xb[:, q4 * 512:(q4 + 1) * 512],
                                                                start=True, stop=True), reads=[bC, bxb], writes=[pb])
                                    op("act", lambda e: e.activation(out=dst[:, q4 * 512:(q4 + 1) * 512], in_=pt[:, :], func=AF.Sigmoid,
                                                                     bias=bias[:, d, c:c + 1]), reads=[pb, bC], writes=[bdst])
                            op("act", lambda e: e.activation(out=rg[:], in_=rg[:], func=AF.Exp, scale=NSP[:, d, c:c + 1]), reads=[brg, bC], writes=[brg])
                            mm, bmm = R1.next()
                            op("act", lambda e: e.activation(out=mm[:], in_=rg[:], func=AF.Square, scale=1.0 - 1e-6), reads=[brg], writes=[bmm])
                            op("act", lambda e: e.activation(out=mm[:], in_=mm[:], func=AF.Sqrt, scale=-1.0, bias=CST[:, 1:2]), reads=[bmm, bC], writes=[bmm])
                            if d == 1:
                                gz, bgz = GZp.next()
                                dma("sp", gz[:], a_GZ[c, :, p0:p0 + PZ], reads=tb["GZ"][p0 // N:(p0 + PZ) // N], writes=[bgz])
                                J["gz"] = (gz, bgz)
                            J["t"] = (xc, bxc, rg, brg, ig, big, mm, bmm)

                        prev = [None]

                        def ph2(J):
                            c, d, p0 = J["c"], J["d"], J["pi"] * PZ
                            xc, bxc, rg, brg, ig, big, mm, bmm = J["t"]
                            op("pool", lambda e: e.tensor_tensor(out=ig[:], in0=ig[:], in1=xc[:], op=ALU.mult), reads=[big, bxc], writes=[big])
                            op("pool", lambda e: e.tensor_tensor(out=ig[:], in0=ig[:], in1=mm[:], op=ALU.mult), reads=[big, bmm], writes=[big])
                            hp, bhp = HP.next()
                            pv = None if J["first"] else prev[0]
                            if d == 0:
                                init = 0.0 if pv is None else pv[0][:, PZ - 1:PZ]
                                op("dve", lambda e: e.tensor_tensor_scan(out=hp[:], data0=rg[:], data1=ig[:], initial=init, op0=ALU.mult, op1=ALU.add),
                                   reads=[brg, big] + ([pv[1]] if pv else []), writes=[bhp])
                                op("act", lambda e: e.copy(out=HT[:, p0:p0 + PZ], in_=hp[:]), reads=[bhp], writes=[bHT])
                            else:
                                init = 0.0 if pv is None else pv[0][:, 0:1]
                                op("dve", lambda e: e.tensor_tensor_scan(out=hp[:, ::-1], data0=rg[:, ::-1], data1=ig[:, ::-1], initial=init,
                                                                         op0=ALU.mult, op1=ALU.add),
                                   reads=[brg, big] + ([pv[1]] if pv else []), writes=[bhp])
                                gz, bgz = J["gz"]
                                tot, btot = R1.next()
                                op("pool", lambda e: e.tensor_tensor(out=tot[:], in0=hp[:], in1=HT[:, p0:p0 + PZ], op=ALU.add), reads=[bhp, bHT], writes=[btot])
                                bo, bbo = BOp.next()
                                op("pool", lambda e: e.tensor_tensor(out=bo[:], in0=tot[:], in1=gz[:], op=ALU.mult), reads=[btot, bgz], writes=[bbo])
                                dma("pool", a_BO[c, :, p0:p0 + PZ], bo[:], reads=[bbo], writes=tb["BO"][p0 // N:(p0 + PZ) // N])
                            prev[0] = (hp, bhp)

                        ph1(rjobs[0])
                        for ji, J in enumerate(rjobs):
                            if ji + 1 < len(rjobs):
                                ph1(rjobs[ji + 1])
                            ph2(J)
                        k.barrier()

                if which == "C1":
                    with contextlib.ExitStack() as es2:
                        KW = Ring(k, 2, [128, AB + 2048], BF16, es2)
                        QC = Ring(k, 2, [128, AB], BF16, es2)
                        VWB = Ring(k, 2, [128, 17, 2, 128], BF16, es2)
                        VWS = Ring(k, 6, [128, 5, 2, 128], BF16, es2)
                        VW16 = k.sb([128, 2, 16, 2, 128], BF16, es2); bVW16 = Buf()
                        ACC = Ring(k, 2, [128, 2, AB], F32, es2)
                        PF = Ring(k, 4, [128, 2, 4, 128], BF16, es2)
                        PM = Ring(k, 4, [128, 2, 4, 128], BF16, es2)
                        RD = k.sb([128, AB], F32, es2); bRD = Buf()
                        OC = Ring(k, 2, [128, AB], BF16, es2)
                        MV = k.sb([128, 6, 4, 128], BF16, es2); bMV = Buf()
                        variants = [(0, 1, 0, 1), (2, 1, 0, 1), (0, 1, 0, 3), (2, 1, 0, 3), (2, 1, 2, 1), (0, 3, 0, 3)]
                        IDB = k.sb([128, 128], BF16, es2)
                        for vi, var in enumerate(variants):
                            for ci, mt in enumerate(var):
                                op("pool", lambda e: e.tensor_copy(out=MV[:, vi, ci, :], in_=MASK[:, mt, 0, :]), reads=[bC], writes=[bMV])
                        op("pool", lambda e: e.tensor_scalar(out=MV[:], in0=MV[:], scalar1=1.0, scalar2=30000.0, op0=ALU.subtract, op1=ALU.mult),
                           reads=[bMV], writes=[bMV])
                        op("pool", lambda e: e.tensor_copy(out=IDB[:], in_=IDENT[:]), reads=[bC], writes=[bMV])
                        for i in range(2):
                            op("pool", lambda e: e.memset(KW.t[i][:], 0.0), writes=[KW.b[i]])
                        op("pool", lambda e: e.memset(VW16[:], 0.0), writes=[bVW16])
                        op("pool", lambda e: e.memset(VW16[:, :, :, :, 64:128], 1.0), writes=[bVW16])
                        for VWr in (VWB, VWS):
                            for i in range(len(VWr.t)):
                                op("pool", lambda e: e.memset(VWr.t[i][:], 0.0), writes=[VWr.b[i]])
                                op("pool", lambda e: e.memset(VWr.t[i][:, :, :, 64:128], 1.0), writes=[VWr.b[i]])
                        NB = S // AB
                        classes = [(d, r) for d in DILS if d < 16 for r in range(d)]

                        def load_vw(c, P0, d, r):
                            vw, bvw = (VWB if d == 1 else VWS).next()
                            ng = AB // (128 * d)
                            L = S // d
                            m0 = P0 // d
                            full = [i for i in range(ng + 1) if m0 - 64 + 128 * i >= 0 and m0 - 64 + 128 * i + 128 <= L]
                            if full:
                                i0, i1 = full[0], full[-1] + 1
                                rs = (m0 - 64 + 128 * i0) * d + r
                                nrow = 128 * (i1 - i0)
                                for h in range(2):
                                    src = a_V[rs:rs + (nrow - 1) * d + 1:d, c * 128 + h * 64:c * 128 + (h + 1) * 64]
                                    dma("sp", vw[:, i0:i1, h, 0:64], src.rearrange("(i p) e -> p i e", p=128),
                                        reads=tb["V"][rs // N:(rs + (nrow - 1) * d) // N + 1], writes=[bvw])
                            for i in range(ng + 1):
                                if i in full:
                                    continue
                                a0 = m0 - 64 + 128 * i
                                a_, b_ = max(a0, 0), min(a0 + 128, L)
                                if b_ <= a_:
                                    continue
                                src = a_V[a_ * d + r:(b_ - 1) * d + r + 1:d, c * 128:(c + 1) * 128]
                                dma("sp", vw[a_ - a0:b_ - a0, i, :, 0:64], src.rearrange("k (h e) -> k h e", h=2),
                                    reads=tb["V"][(a_ * d + r) // N:((b_ - 1) * d + r) // N + 1], writes=[bvw])
                            return vw, bvw

                        def load_vw16(ci):
                            if ci >= len(chunks):
                                return
                            blk, c = chunks[ci]
                            L = S // 16
                            m0 = blk * AB // 16
                            for i in range(2):
                                a0 = m0 - 64 + 128 * i
                                plo, phi = max(0, -a0), min(128, L - a0)
                                if phi <= plo:
                                    continue
                                rs, re = (a0 + plo) * 16, (a0 + phi) * 16
                                for h in range(2):
                                    src = a_V[rs:re, c * 128 + h * 64:c * 128 + (h + 1) * 64]
                                    dma("sp", VW16[plo:phi, i, :, h, 0:64], src.rearrange("(p r) e -> p r e", r=16),
                                        reads=tb["V"][rs // N:(re - 1) // N + 1], writes=[bVW16])

                        chunks = [(blk, c) for blk in range(NB) for c in range(KC)]
                        cstate = {}

                        def chunk_loads(ci):
                            if ci >= len(chunks) or ci in cstate:
                                return
                            blk, c = chunks[ci]
                            P0 = blk * AB
                            kw, bkw = KW.next()
                            lo, hi = max(0, P0 - 1024), min(S, P0 + AB + 1024)
                            dma("sp", kw[:, lo - (P0 - 1024):hi - (P0 - 1024)], a_KT[c, :, lo:hi], reads=tb["KT"][lo // N:hi // N], writes=[bkw])
                            qc, bqc = QC.next()
                            dma("sp", qc[:], a_QT[c, :, P0:P0 + AB], reads=tb["QT"][P0 // N:(P0 + AB) // N], writes=[bqc])
                            cstate[ci] = dict(kw=kw, bkw=bkw, qc=qc, bqc=bqc, vws={}, acc=None)

                        def ensure_vw(ci, k_idx):
                            while k_idx >= len(classes):
                                ci += 1
                                k_idx -= len(classes)
                            if ci >= len(chunks):
                                return
                            chunk_loads(ci)
                            st = cstate[ci]
                            if k_idx not in st["vws"]:
                                blk, c = chunks[ci]
                                d, r = classes[k_idx]
                                st["vws"][k_idx] = load_vw(c, blk * AB, d, r)

                        jobs = []
                        for ci, (blk, c) in enumerate(chunks):
                            pl = []
                            for d in DILS:
                                ng = AB // (128 * d)
                                if ng >= 2:
                                    for r in range(d):
                                        for g in range(0, ng, 2):
                                            pl.append(((d, r, g), (d, r, g + 1)))
                                else:
                                    for r in range(0, d, 2):
                                        pl.append(((d, r, 0), (d, r + 1, 0)))
                            for pi, pair in enumerate(pl):
                                jobs.append(dict(ci=ci, pair=pair, first=(pi == 0), last=(pi == len(pl) - 1)))

                        def s1(job):
                            ci = job["ci"]
                            blk, c = chunks[ci]
                            P0 = blk * AB
                            if job["first"]:
                                chunk_loads(ci)
                                chunk_loads(ci + 1)
                                cstate[ci]["acc"] = ACC.next()
                            st = cstate[ci]
                            kw, bkw, qc, bqc = st["kw"], st["bkw"], st["qc"], st["bqc"]
                            d = job["pair"][0][0]
                            L = S // d
                            m0 = P0 // d
                            combos = []
                            for (d_, r, g) in job["pair"]:
                                if d_ < 16:
                                    kidx = classes.index((d_, r))
                                    ensure_vw(ci, kidx)
                                    ensure_vw(ci, kidx + 1)
                                    ensure_vw(ci, kidx + 2)
                                    vw, bvw = st["vws"][kidx]
                                else:
                                    vw, bvw = VW16[:, :, r, :, :], bVW16
                                mabs = m0 + 128 * g
                                for part in range(2):
                                    i = g + part
                                    col0 = 1024 - 64 * d + 128 * d * i + r
                                    if part == 0:
                                        mt = 2 if mabs == 0 else 0
                                    else:
                                        mt = 3 if mabs + 128 == L else 1
                                    combos.append((r, g, i, col0, mt, vw, bvw))
                            vi = variants.index(tuple(cb[4] for cb in combos))
                            pm, bpm = PM.next()
                            sbk = [k.bank(), k.bank()]
                            for hp in range(2):
                                pss, pssb = sbk[hp]
                                op("pe", lambda e: e.matmul(pss[:, :], lhsT=IDB[:], rhs=MV[:, vi, :, :].rearrange("p c n -> p (c n)"),
                                                            start=True, stop=False), reads=[bMV], writes=[pssb], inc=False)
                            for cj, (r, g, i, col0, mt, vw, bvw) in enumerate(combos):
                                q0 = 128 * g * d + r
                                for hp in range(2):
                                    pss, pssb = sbk[hp]
                                    op("pe", lambda e: e.matmul(pss[:, cj * 128:(cj + 1) * 128],
                                                                lhsT=kw[hp * 64:(hp + 1) * 64, col0:col0 + 127 * d + 1:d],
                                                                rhs=qc[hp * 64:(hp + 1) * 64, q0:q0 + 127 * d + 1:d],
                                                                start=False, stop=(cj == 3)), reads=[bkw, bqc], writes=[pssb], inc=(cj == 3))
                            for hp in range(2):
                                pss, pssb = sbk[hp]
                                op("act", lambda e: e.activation(out=pm[:, hp, :, :], in_=pss[:, :].rearrange("p (c n) -> p c n", c=4), func=AF.Exp),
                                   reads=[pssb], writes=[bpm])
                            job["combos"], job["pm"], job["bpm"] = combos, pm, bpm

                        def s2(job):
                            ci = job["ci"]
                            blk, c = chunks[ci]
                            P0 = blk * AB
                            st = cstate[ci]
                            acc, bacc = st["acc"]
                            combos, pm, bpm = job["combos"], job["pm"], job["bpm"]
                            d = job["pair"][0][0]
                            pso, psob = k.bank()
                            for it in range(2):
                                for hp in range(2):
                                    for part in range(2):
                                        r, g, i, col0, mt, vw, bvw = combos[it * 2 + part]
                                        op("pe", lambda e: e.matmul(pso[:, (it * 2 + hp) * 128:(it * 2 + hp + 1) * 128], lhsT=vw[:, i, hp, :],
                                                                    rhs=pm[:, hp, it * 2 + part, :], start=(part == 0), stop=(part == 1)),
                                           reads=[bvw, bpm], writes=[psob], inc=(it == 1 and hp == 1 and part == 1))
                            r0_, g0_ = combos[0][0], combos[0][1]
                            src_ap = pso[:, :].rearrange("p (i h q) -> p h i q", i=2, h=2)
                            if d < 16:
                                q0 = 128 * g0_ * d + r0_
                                acc_ap = acc[:, :, q0:q0 + 255 * d + 1:d].rearrange("p h (i q) -> p h i q", i=2)
                            else:
                                acc_ap = acc[:, :, :].rearrange("p h (q s) -> p h s q", s=16)[:, :, r0_:r0_ + 2, :]
                            if d == 1:
                                op("dve", lambda e: e.tensor_copy(out=acc_ap, in_=src_ap), reads=[psob], writes=[bacc])
                            else:
                                op("dve", lambda e: e.tensor_tensor(out=acc_ap, in0=acc_ap, in1=src_ap, op=ALU.add), reads=[psob, bacc], writes=[bacc])
                            if job["last"]:
                                oc, boc = OC.next()
                                op("act", lambda e: e.activation(out=acc[64:128, :, :], in_=acc[64:128, :, :], func=AF.Ln), reads=[bacc], writes=[bacc])
                                op("act", lambda e: e.activation(out=acc[64:128, :, :], in_=acc[64:128, :, :], func=AF.Exp, scale=-1.0), reads=[bacc], writes=[bacc])
                                for hp in range(2):
                                    op("dve", lambda e: e.tensor_copy(out=RD[0:64, :], in_=acc[64:128, hp, :]), reads=[bacc], writes=[bRD])
                                    if hp == 0:
                                        op("dve", lambda e: e.tensor_tensor(out=oc[0:64, :], in0=acc[0:64, 0, :], in1=RD[0:64, :], op=ALU.mult),
                                           reads=[bacc, bRD], writes=[boc])
                                    else:
                                        op("dve", lambda e: e.tensor_tensor(out=RD[0:64, :], in0=acc[0:64, 1, :], in1=RD[0:64, :], op=ALU.mult),
                                           reads=[bacc, bRD], writes=[bRD])
                                        op("dve", lambda e: e.tensor_copy(out=oc[64:128, :], in_=RD[0:64, :]), reads=[bRD], writes=[boc])
                                dma("pool", a_OT[c, :, P0:P0 + AB], oc[:], reads=[boc], writes=tb["OT"][P0 // N:(P0 + AB) // N])
                                del cstate[ci]["vws"]
                                load_vw16(ci + 1)

                        load_vw16(0)
                        s1(jobs[0])
                        s1(jobs[1])
                        for ji, job in enumerate(jobs):
                            if ji + 2 < len(jobs):
                                s1(jobs[ji + 2])
                            s2(job)
                        k.barrier()


            stop = False
            for stg in ("A", "R", "B", "C1", "C2"):
                (ffn_stage if stg in ("A", "B", "C2") else other_stage)(stg)
                if debug == stg:
                    stop = True
                    break
            if stop:
                break
        k_unused = None
        k.finish()
    return nc


def _consts(SMAX):
    ident = np.eye(128, dtype=np.float32)
    rot = np.zeros((128, 128), np.float32)
    for hp in range(2):
        for i in range(8):
            rot[hp * 64 + i + 8, hp * 64 + i] = -1.0
            rot[hp * 64 + i, hp * 64 + i + 8] = 1.0
    half = 8
    inv = np.power(np.float32(ROPE_THETA), -np.arange(half, dtype=np.float32) * np.float32(2.0 / 16)).astype(np.float32)
    pos = np.arange(SMAX, dtype=np.float32)
    ang = (pos[:, None] * inv[None, :]).astype(np.float32)
    cos = np.ones((128, SMAX), np.float32)
    sin = np.zeros((128, SMAX), np.float32)
    for hp in range(2):
        for i in range(16):
            cos[hp * 64 + i] = np.cos(ang[:, i % 8])
            sin[hp * 64 + i] = np.sin(ang[:, i % 8])
    kk = np.arange(128)[:, None]
    qq = np.arange(128)[None, :]
    mA = (kk >= qq)
    mB = (kk <= qq)
    masks = np.stack([mA, mB, mA & (kk >= 64), mB & (kk < 64)], axis=1).astype(np.float32)
    masks = np.repeat(masks[:, :, None, :], 2, axis=2)
    return dict(c_ident=ident, c_rot=rot, c_cos=cos, c_sin=sin, c_mask=np.ascontiguousarray(masks))


_WNAMES = ["ffn1_norm", "ffn1_w_in", "ffn1_w_out", "mix_norm", "ev_w_in", "ev_w_out", "gm_ln_g", "gm_ln_b", "gm_w_s", "gm_b_s",
           "rg_conv_w", "rg_conv_b", "rg_w_a", "rg_b_a", "rg_w_i", "rg_b_i", "rg_lam", "od_w_in", "od_w_out", "od_q_norm",
           "od_k_norm", "xa_norm", "xa_mem_norm", "xa_w_q", "xa_w_kv", "xa_w_o", "xa_q_norm", "xa_k_norm", "ffn2_norm",
           "ffn2_w_in", "ffn2_w_out"]


def kernel(x_prompt, x_sample, mem_prompt, mem_sample, **weights):
    n = 8
    S_list = (x_prompt.shape[1], x_sample.shape[1])
    nc = build(S_list)
    consts = _consts(max(S_list))
    wmap = {nm: np.ascontiguousarray(np.asarray(weights[nm], dtype=np.float32)) for nm in _WNAMES}
    in_maps = []
    for c in range(n):
        m = dict(wmap)
        m.update(consts)
        m["x0"] = np.ascontiguousarray(np.asarray(x_prompt[c], dtype=np.float32))
        m["x1"] = np.ascontiguousarray(np.asarray(x_sample[c], dtype=np.float32))
        m["m0"] = np.ascontiguousarray(np.asarray(mem_prompt[c], dtype=np.float32))
        m["m1"] = np.ascontiguousarray(np.asarray(mem_sample[c], dtype=np.float32))
        in_maps.append(m)
    res = run_bass_kernel_spmd(nc, in_maps, core_ids=list(range(n)))
    y0 = np.stack([np.asarray(r["y0"], dtype=np.float32) for r in res.results], axis=0)
    y1 = np.stack([np.asarray(r["y1"], dtype=np.float32) for r in res.results], axis=0)
    return (y0, y1)
```

```python
import contextlib
import math
import numpy as np
import concourse.bass as bass
import concourse.mybir as mybir
from concourse.bass_utils import run_bass_kernel_spmd

F32 = mybir.dt.float32
BF16 = mybir.dt.bfloat16
ALU = mybir.AluOpType
AF = mybir.ActivationFunctionType

D = 1024
KC = 8
DFF = 2816
NJ = DFF // 128
N = 512
NMEM = 256
EPS = 1e-6
ROPE_THETA = 500000.0
GELU_C = 1.5957691216057308
AB = 2048
DILS = (1, 4, 16)


class Buf:
    __slots__ = ("w", "r")

    def __init__(self):
        self.w = None
        self.r = {}


class Eng:
    def __init__(self, name, e, sem):
        self.name = name
        self.e = e
        self.sem = sem
        self.cnt = 0
        self.known = {}
        self.pending = False


class K:
    def __init__(self, nc, es):
        self.nc = nc
        self.es = es
        self.sems = []
        self.E = {}
        for name, e in (("pe", nc.tensor), ("act", nc.scalar), ("dve", nc.vector),
                        ("pool", nc.gpsimd), ("sp", nc.sync)):
            self.E[name] = Eng(name, e, self.mksem("c_" + name))
        self.rings = {}
        self.ring_pos = {}
        self.ring_val = {}
        for q, n in (("sp", 10), ("pool", 6), ("act", 2)):
            self.rings[q] = [self.mksem("d_%s%d" % (q, i)) for i in range(n)]
            self.ring_pos[q] = 0
            for s in self.rings[q]:
                self.ring_val[s] = 0
        self.banks = []
        self.bank_bufs = []
        for i in range(8):
            t = es.enter_context(nc.psum_tensor("ps%d" % i, [128, 512], F32))
            self.banks.append(t)
            self.bank_bufs.append(Buf())
        self.bank_i = 0
        self.nalloc = 0

    def mksem(self, name):
        h = self.es.enter_context(self.nc.semaphore(name))
        self.sems.append(h)
        return len(self.sems) - 1

    def sb(self, shape, dtype, es=None):
        self.nalloc += 1
        nb = int(np.prod(shape[1:])) * (4 if dtype == F32 else 2)
        nb = (nb + 31) // 32 * 32
        self.cur = getattr(self, "cur", 0) + nb
        self.peak = max(getattr(self, "peak", 0), self.cur)
        self.rem = min(getattr(self, "rem", 1 << 30), self.nc.sbuf_bytes_remaining)
        t = (es or self.es).enter_context(self.nc.sbuf_tensor("t%d" % self.nalloc, list(shape), dtype))

        def rel():
            self.cur -= nb
        (es or self.es).callback(rel)
        return t

    def bank(self):
        i = self.bank_i
        self.bank_i = (i + 1) % 8
        return self.banks[i], self.bank_bufs[i]

    def need(self, E, s, v):
        if E.known.get(s, 0) >= v:
            return
        E.e.wait_ge(self.sems[s], v)
        E.known[s] = v

    def _deps(self, E, reads, writes):
        for b in reads:
            if b.w is not None:
                self._dep(E, b.w)
        for b in writes:
            if b.w is not None:
                self._dep(E, b.w)
            for s, v in b.r.items():
                self._dep(E, (s, v))

    def _dep(self, E, ev):
        s, v = ev
        if s == E.sem and E.name == "pe":
            return
        self.need(E, s, v)

    def _record(self, ev, reads, writes):
        s, v = ev
        for b in reads:
            if b.r.get(s, 0) < v:
                b.r[s] = v
        for b in writes:
            b.w = ev
            b.r = {}

    def op(self, eng, fn, reads=(), writes=(), inc=True):
        E = self.E[eng]
        self._deps(E, reads, writes)
        ins = fn(E.e)
        if inc:
            E.cnt += 1
            ins.then_inc(self.sems[E.sem], 1)
            ev = (E.sem, E.cnt)
            E.pending = False
        else:
            assert eng == "pe"
            ev = (E.sem, E.cnt + 1)
            E.pending = True
        self._record(ev, reads, writes)

    def dma(self, q, out, in_, reads=(), writes=(), **kw):
        E = self.E[q]
        self._deps(E, reads, writes)
        ring = self.rings[q]
        s = ring[self.ring_pos[q] % len(ring)]
        self.ring_pos[q] += 1
        prev = self.ring_val[s]
        self.need(E, s, prev)
        E.e.dma_start(out=out, in_=in_, **kw).then_inc(self.sems[s], 16)
        self.ring_val[s] = prev + 16
        self._record((s, prev + 16), reads, writes)

    def barrier(self):
        for E in self.E.values():
            assert not E.pending
        for E in self.E.values():
            for O in self.E.values():
                if O is not E and O.cnt > 0:
                    self.need(E, O.sem, O.cnt)
            for s, v in self.ring_val.items():
                if v > 0:
                    self.need(E, s, v)

    def finish(self):
        E = self.E["sp"]
        for O in self.E.values():
            assert not O.pending
            if O is not E and O.cnt > 0:
                self.need(E, O.sem, O.cnt)
        for s, v in self.ring_val.items():
            if v > 0:
                self.need(E, s, v)


class Ring:
    def __init__(self, k, n, shape, dtype, es=None):
        self.t = [k.sb(shape, dtype, es) for _ in range(n)]
        self.b = [Buf() for _ in range(n)]
        self.i = 0

    def next(self):
        i = self.i
        self.i = (i + 1) % len(self.t)
        return self.t[i], self.b[i]


def build(S_list, debug=False):
    nc = bass.Bass("TRN2", target_bir_lowering=False)
    SMAX = max(S_list)

    def din(name, shape):
        return nc.dram_tensor(name, list(shape), F32, kind="ExternalInput").ap()

    def dscr(name, shape, dtype):
        kind = "ExternalOutput" if (debug and name.startswith("a_")) else "Internal"
        return nc.dram_tensor(name, list(shape), dtype, kind=kind).ap()

    x_in = [din("x%d" % i, [S, D]) for i, S in enumerate(S_list)]
    m_in = [din("m%d" % i, [NMEM, D]) for i in range(len(S_list))]
    y_out = [nc.dram_tensor("y%d" % i, [S, D], F32, kind="ExternalOutput").ap() for i, S in enumerate(S_list)]
    W = {}
    for name, shape in (
        ("ffn1_norm", [2, D]), ("ffn1_w_in", [2, D, 2 * DFF]), ("ffn1_w_out", [2, DFF, D]), ("mix_norm", [2, D]),
        ("ev_w_in", [1, D, 2048]), ("ev_w_out", [1, D, D]), ("gm_ln_g", [1, 512]), ("gm_ln_b", [1, 512]),
        ("gm_w_s", [1, 4, 128, 128]), ("gm_b_s", [1, 4, 128]), ("rg_conv_w", [1, 4, 512]), ("rg_conv_b", [1, 512]),
        ("rg_w_a", [1, 2, 8, 64, 64]), ("rg_b_a", [1, 2, 512]), ("rg_w_i", [1, 2, 8, 64, 64]), ("rg_b_i", [1, 2, 512]),
        ("rg_lam", [1, 2, 512]), ("od_w_in", [1, D, 3 * D]), ("od_w_out", [1, D, D]), ("od_q_norm", [1, 64]),
        ("od_k_norm", [1, 64]), ("xa_norm", [2, D]), ("xa_mem_norm", [2, D]), ("xa_w_q", [2, D, D]),
        ("xa_w_kv", [2, D, 2 * D]), ("xa_w_o", [2, D, D]), ("xa_q_norm", [2, 256]), ("xa_k_norm", [2, 256]),
        ("ffn2_norm", [2, D]), ("ffn2_w_in", [2, D, 2 * DFF]), ("ffn2_w_out", [2, DFF, D]),
    ):
        W[name] = din(name, shape)
    c_ident = din("c_ident", [128, 128])
    c_rot = din("c_rot", [128, 128])
    c_cos = din("c_cos", [128, SMAX])
    c_sin = din("c_sin", [128, SMAX])
    c_mask = din("c_mask", [128, 4, 2, 128])

    ffn_names = [("ffn1", 0), ("ffn2", 0), ("ffn1", 1), ("ffn2", 1)]
    s_win = [dscr("s_win%d" % i, [NJ, 128, KC, 256], BF16) for i in range(4)]
    s_wout = [dscr("s_wout%d" % i, [KC, 128, NJ, 128], BF16) for i in range(4)]
    s_ev_in = dscr("s_ev_in", [6, 128, KC, 256], BF16)
    s_ev_v = dscr("s_ev_v", [128, KC, 512], BF16)
    s_ev_out = dscr("s_ev_out", [4, 128, KC, 256], BF16)
    s_od_qk = dscr("s_od_qk", [8, 128, KC, 256], BF16)
    s_od_v = dscr("s_od_v", [128, KC, 1024], BF16)
    s_od_out = dscr("s_od_out", [4, 128, KC, 256], BF16)
    s_xq = [dscr("s_xq%d" % l, [4, 128, KC, 256], BF16) for l in range(2)]
    s_xk = [dscr("s_xk%d" % l, [4, 128, KC, 256], BF16) for l in range(2)]
    s_xv = [dscr("s_xv%d" % l, [128, KC, 1024], BF16) for l in range(2)]
    s_xo = [dscr("s_xo%d" % l, [4, 128, KC, 256], BF16) for l in range(2)]
    a_XS = dscr("a_XS", [KC, 128, SMAX], F32)
    a_AO = dscr("a_AO", [4, 128, SMAX], BF16)
    a_BO = dscr("a_BO", [4, 128, SMAX], BF16)
    a_ZX = dscr("a_ZX", [4, 128, SMAX], F32)
    a_GZ = dscr("a_GZ", [4, 128, SMAX], BF16)
    a_QT = dscr("a_QT", [KC, 128, SMAX], BF16)
    a_KT = dscr("a_KT", [KC, 128, SMAX], BF16)
    a_V = dscr("a_V", [SMAX, D], BF16)
    a_OT = dscr("a_OT", [KC, 128, SMAX], BF16)
    NTMAX = SMAX // N
    tb = {nm: [Buf() for _ in range(NTMAX)] for nm in ("XS", "AO", "BO", "ZX", "GZ", "QT", "KT", "V", "OT")}
    dbg = {}

    with contextlib.ExitStack() as es:
        k = K(nc, es)
        op, dma = k.op, k.dma

        IDENT = k.sb([128, 128], F32)
        ROTM = k.sb([128, 128], F32)
        ROTB = k.sb([128, 128], BF16)
        ONESB = k.sb([128, 128], BF16)
        BLK1 = k.sb([128, 128], BF16)
        MASK = k.sb([128, 4, 2, 128], BF16)
        CST = k.sb([128, 4], F32)
        bC = Buf()
        G = {}
        for nm in ("ffn1_norm", "ffn2_norm", "mix_norm", "xa_norm", "xa_mem_norm"):
            G[nm] = k.sb([128, 2, KC], F32)
        XQG = k.sb([128, 2, 2], F32)
        XKG = k.sb([128, 2, 2], F32)
        OQG = k.sb([128, 2], F32)
        CW = k.sb([128, 4, 4], F32)
        CB = k.sb([128, 4], F32)
        BA = k.sb([128, 2, 4], F32)
        BI = k.sb([128, 2, 4], F32)
        NSP = k.sb([128, 2, 4], F32)
        WBD = k.sb([128, 16, 128], BF16)
        WST = k.sb([128, 4, 128], BF16)
        LNG = k.sb([128, 512], F32)
        LNB = k.sb([128, 512], F32)
        BSB = k.sb([128, 4, 4, 128], F32)
        KX = [k.sb([128, KC, NMEM], BF16) for _ in range(2)]
        VX = [k.sb([128, 2, D], BF16) for _ in range(2)]
        bKX = [Buf(), Buf()]
        bVX = [Buf(), Buf()]
        WG = Ring(k, 4, [128, KC, 256], BF16)

        nsq = lambda e: None
        with nc.allow_non_contiguous_dma(reason="tiny param loads"):
            dma("sp", IDENT[:], c_ident[:, :], writes=[bC])
            dma("sp", ROTM[:], c_rot[:, :], writes=[bC])
            for nm in G:
                for l in range(2):
                    dma("sp", G[nm][:, l, :], W[nm][l].rearrange("(c p) -> p c", p=128), writes=[bC])
            for l in range(2):
                dma("sp", XQG[:, l, :], W["xa_q_norm"][l].rearrange("(c p) -> p c", p=128), writes=[bC])
                dma("sp", XKG[:, l, :], W["xa_k_norm"][l].rearrange("(c p) -> p c", p=128), writes=[bC])
            for hp in range(2):
                dma("sp", OQG[hp * 64:(hp + 1) * 64, 0:1], W["od_q_norm"][0].rearrange("(p o) -> p o", o=1), writes=[bC])
                dma("sp", OQG[hp * 64:(hp + 1) * 64, 1:2], W["od_k_norm"][0].rearrange("(p o) -> p o", o=1), writes=[bC])
            for t in range(4):
                dma("sp", CW[:, t, :], W["rg_conv_w"][0, t].rearrange("(c p) -> p c", p=128), writes=[bC])
            dma("sp", CB[:], W["rg_conv_b"][0].rearrange("(c p) -> p c", p=128), writes=[bC])
            for d in range(2):
                dma("sp", BA[:, d, :], W["rg_b_a"][0, d].rearrange("(c p) -> p c", p=128), writes=[bC])
                dma("sp", BI[:, d, :], W["rg_b_i"][0, d].rearrange("(c p) -> p c", p=128), writes=[bC])
                dma("sp", NSP[:, d, :], W["rg_lam"][0, d].rearrange("(c p) -> p c", p=128), writes=[bC])
            dma("sp", LNG[:], W["gm_ln_g"][0].partition_broadcast(128), writes=[bC])
            dma("sp", LNB[:], W["gm_ln_b"][0].partition_broadcast(128), writes=[bC])
            for g in range(4):
                for tc in range(4):
                    dma("sp", BSB[:, g, tc, :], W["gm_b_s"][0, g].partition_broadcast(128), writes=[bC])
        with contextlib.ExitStack() as es0:
            STG = k.sb([128, 16, 128], F32, es0)
            MSTG = k.sb([128, 4, 2, 128], F32, es0)
            WSS = k.sb([128, 4, 128], F32, es0)
            bS = Buf()
            op("pool", lambda e: e.memset(STG[:], 0.0), writes=[bS])
            for d in range(2):
                for ai, nm in enumerate(("rg_w_a", "rg_w_i")):
                    for c in range(4):
                        for hp in range(2):
                            dma("sp", STG[hp * 64:(hp + 1) * 64, (d * 2 + ai) * 4 + c, hp * 64:(hp + 1) * 64],
                                W[nm][0, d, 2 * c + hp], writes=[bS])
            dma("sp", MSTG[:], c_mask[:, :, :, :], writes=[bS])
            for g in range(4):
                dma("sp", WSS[:, g, :], W["gm_w_s"][0, g], writes=[bS])
            op("dve", lambda e: e.tensor_copy(out=WBD[:], in_=STG[:]), reads=[bS], writes=[bC])
            op("dve", lambda e: e.tensor_copy(out=MASK[:], in_=MSTG[:]), reads=[bS], writes=[bC])
            op("dve", lambda e: e.memset(ONESB[:], 1.0), writes=[bC])
            op("dve", lambda e: e.tensor_copy(out=ROTB[:], in_=ROTM[:]), reads=[bC], writes=[bC])
            op("dve", lambda e: e.memset(BLK1[:], 0.0), writes=[bC])
            op("dve", lambda e: e.memset(BLK1[0:64, 0:64], 1.0), writes=[bC])
            op("dve", lambda e: e.memset(BLK1[64:128, 64:128], 1.0), writes=[bC])
            op("dve", lambda e: e.memset(CST[:, 0:1], EPS), writes=[bC])
            op("dve", lambda e: e.memset(CST[:, 1:2], 1.0), writes=[bC])
            op("dve", lambda e: e.tensor_scalar(out=XKG[:], in0=XKG[:], scalar1=1.0 / 16.0, scalar2=None, op0=ALU.mult), reads=[bC], writes=[bC])
            op("dve", lambda e: e.tensor_scalar(out=OQG[:, 0:1], in0=OQG[:, 0:1], scalar1=0.125, scalar2=None, op0=ALU.mult), reads=[bC], writes=[bC])
            op("act", lambda e: e.activation(out=NSP[:], in_=NSP[:], func=AF.Exp, scale=-1.0), reads=[bC], writes=[bC])
            op("act", lambda e: e.activation(out=NSP[:], in_=NSP[:], func=AF.Ln, bias=CST[:, 1:2]), reads=[bC], writes=[bC])
            op("dve", lambda e: e.tensor_scalar(out=NSP[:], in0=NSP[:], scalar1=-8.0, scalar2=None, op0=ALU.mult), reads=[bC], writes=[bC])
            for g in range(4):
                pt, pb = k.bank()
                op("pe", lambda e: e.transpose(out=pt[:, 0:128], in_=WSS[:, g, :], identity=IDENT[:]), reads=[bS, bC], writes=[pb])
                op("act", lambda e: e.copy(out=WST[:, g, :], in_=pt[:, 0:128]), reads=[pb], writes=[bC])
            k.barrier()

        wb = {}
        DQ = []
        defer = [False]

        def cdma(out, in_, b):
            if defer[0]:
                DQ.append((out, in_, b))
            else:
                dma("pool", out, in_, writes=[b])

        def drain_dq(n):
            for _ in range(min(n, len(DQ))):
                out, in_, b = DQ.pop(0)
                dma("pool", out, in_, writes=[b])

        def conv_fm(key, src, dst, col_pairs):
            bl = []
            for g, cols in enumerate(col_pairs):
                bg = []
                for h, c0 in enumerate(cols):
                    b = Buf()
                    cdma(dst[g, :, :, h * 128:(h + 1) * 128], src[:, c0:c0 + 128].rearrange("(kc p) c -> p kc c", p=128), b)
                    bg.append(b)
                bl.append(bg)
            wb[key] = bl

        def conv_wout(key, src, dst):
            bl = []
            for c in range(KC):
                bs = []
                for j0 in (0, NJ // 2):
                    b = Buf()
                    cdma(dst[c, :, j0:j0 + NJ // 2, :],
                         src[j0 * 128:(j0 + NJ // 2) * 128, c * 128:(c + 1) * 128].rearrange("(j p) c -> p j c", p=128), b)
                    bs.append(b)
                bl.append(bs)
            wb[key] = bl

        def conv_tm(key, src, dst, c0, ncols):
            bl = []
            for kc in range(KC):
                b = Buf()
                cdma(dst[:, kc, :], src[kc * 128:(kc + 1) * 128, c0:c0 + ncols], b)
                bl.append(b)
            wb[key] = [bl]

        def seqpairs(c0, n):
            return [(c0 + 256 * g, c0 + 256 * g + 128) for g in range(n)]

        def conv_ffn(i):
            nm, l = ffn_names[i]
            conv_fm("win%d" % i, W[nm + "_w_in"][l], s_win[i], [(j * 128, DFF + j * 128) for j in range(NJ)])
            conv_wout("wout%d" % i, W[nm + "_w_out"][l], s_wout[i])

        for l in range(2):
            conv_fm("xk%d" % l, W["xa_w_kv"][l], s_xk[l], seqpairs(0, 4))
            conv_tm("xv%d" % l, W["xa_w_kv"][l], s_xv[l], 1024, 1024)
        conv_ffn(0)
        conv_fm("ev_in", W["ev_w_in"][0], s_ev_in, seqpairs(0, 2) + seqpairs(1024, 2) + seqpairs(1536, 2))
        conv_tm("ev_v", W["ev_w_in"][0], s_ev_v, 512, 512)
        defer[0] = True
        conv_fm("ev_out", W["ev_w_out"][0], s_ev_out, seqpairs(0, 4))
        conv_fm("xq0", W["xa_w_q"][0], s_xq[0], seqpairs(0, 4))
        conv_fm("xo0", W["xa_w_o"][0], s_xo[0], seqpairs(0, 4))
        conv_ffn(1)
        conv_ffn(2)
        conv_fm("od_qk", W["od_w_in"][0], s_od_qk, seqpairs(0, 8))
        conv_tm("od_v", W["od_w_in"][0], s_od_v, 2048, 1024)
        conv_fm("od_out", W["od_w_out"][0], s_od_out, seqpairs(0, 4))
        conv_fm("xq1", W["xa_w_q"][1], s_xq[1], seqpairs(0, 4))
        conv_fm("xo1", W["xa_w_o"][1], s_xo[1], seqpairs(0, 4))
        conv_ffn(3)
        defer[0] = False
        dq_per_tile = -(-len(DQ) // (S_list[0] // N))

        def proj_fm(key, scr, ngroups, SRC, bSRC, ncols, consumer, hook=None, delay=False):
            pend_a, pend_b = [], []
            for g in range(ngroups):
                wt, wbuf = WG.next()
                dma("sp", wt[:], scr[g], reads=wb[key][g], writes=[wbuf])
                for h in range(2):
                    pt, pb = k.bank()
                    for kc in range(KC):
                        op("pe", lambda e: e.matmul(pt[:, 0:ncols], lhsT=wt[:, kc, h * 128:(h + 1) * 128], rhs=SRC[:, kc, 0:ncols],
                                                    start=(kc == 0), stop=(kc == KC - 1)),
                           reads=[wbuf, bSRC], writes=[pb], inc=(kc == KC - 1))
                    if hook is not None and g == 0 and h == 0:
                        hook()
                    if not delay:
                        r = consumer(2 * g + h, pt, pb)
                        if r is not None:
                            r()
                    else:
                        if len(pend_b) > 1:
                            pend_b.pop(0)()
                        if pend_a:
                            r = consumer(*pend_a.pop(0))
                            if r is not None:
                                pend_b.append(r)
                        pend_a.append((2 * g + h, pt, pb))
            while pend_a or pend_b:
                if pend_b:
                    pend_b.pop(0)()
                if pend_a:
                    r = consumer(*pend_a.pop(0))
                    if r is not None:
                        pend_b.append(r)

        def rms_stats(SQ, bSQ, nch, ncols, lhs, inv_n, RS, bRS):
            pt, pb = k.bank()
            for c in range(nch):
                op("pe", lambda e: e.matmul(pt[:, 0:ncols], lhsT=lhs[:], rhs=SQ[c], start=(c == 0), stop=(c == nch - 1)),
                   reads=[bSQ, bC], writes=[pb], inc=(c == nch - 1))
            op("act", lambda e: e.activation(out=RS[:, 0:ncols], in_=pt[:, 0:ncols], func=AF.Ln, scale=inv_n, bias=CST[:, 0:1]),
               reads=[pb, bC], writes=[bRS])
            op("act", lambda e: e.activation(out=RS[:, 0:ncols], in_=RS[:, 0:ncols], func=AF.Exp, scale=-0.5), reads=[bRS], writes=[bRS])

        def gelu_from_psum(pt, pb, ncols, OUT, bOUT, T1, bT1):
            op("act", lambda e: e.activation(out=T1, in_=pt[:, 0:ncols], func=AF.Square), reads=[pb], writes=[bT1])
            op("dve", lambda e: e.tensor_scalar(out=T1, in0=T1, scalar1=0.044715, scalar2=1.0, op0=ALU.mult, op1=ALU.add),
               reads=[bT1], writes=[bT1])
            op("dve", lambda e: e.tensor_tensor(out=T1, in0=T1, in1=pt[:, 0:ncols], op=ALU.mult), reads=[bT1, pb], writes=[bT1])
            op("act", lambda e: e.activation(out=T1, in_=T1, func=AF.Sigmoid, scale=GELU_C), reads=[bT1], writes=[bT1])
            op("dve", lambda e: e.tensor_tensor(out=OUT, in0=T1, in1=pt[:, 0:ncols], op=ALU.mult), reads=[bT1, pb], writes=[bOUT])

        for si, S in enumerate(S_list):
            NT = S // N
            x_d, m_d, y_d = x_in[si], m_in[si], y_out[si]

            def ffn_stage(which):
                with contextlib.ExitStack() as es1:
                    XT = Ring(k, 2, [128, KC, N], F32, es1)
                    HB = k.sb([128, KC, N], BF16, es1); bHB = Buf()
                    SQ = k.sb([128, KC, N], BF16, es1); bSQ = Buf()
                    ACTB = k.sb([128, NJ, N], BF16, es1); bACT = [Buf() for _ in range(NJ)]
                    WO = Ring(k, 2, [128, NJ, 128], BF16, es1)
                    TMP = Ring(k, 10, [128, N], F32, es1)
                    RS = k.sb([128, N], F32, es1); bRS = Buf()
                    XIN = Ring(k, 4, [128, D], F32, es1) if which == "A" else None
                    if which != "A":
                        MIX = k.sb([128, KC, N], BF16, es1); bMIX = Buf()
                        QB = k.sb([128, KC, N], BF16, es1); bQB = Buf()
                        PT = Ring(k, 4, [128, 2, N], BF16, es1)
                        MIXIN = Ring(k, 1, [128, KC, N], BF16, es1)
                    WT = ACTB[:, 0:16, :].rearrange("p (kc a) n -> p kc (a n)", a=2)
                    bWTl = bACT[0:16]

                    def rmsnorm(X, bX, gname, l, ncols=N):
                        op("act", lambda e: e.activation(out=SQ[:, :, 0:ncols], in_=X[:, :, 0:ncols], func=AF.Square), reads=[bX], writes=[bSQ])
                        rms_stats([SQ[:, c, 0:ncols] for c in range(KC)], bSQ, KC, ncols, ONESB, 1.0 / D, RS, bRS)
                        for c in range(KC):
                            op("dve", lambda e: e.scalar_tensor_tensor(out=HB[:, c, 0:ncols], in0=X[:, c, 0:ncols], scalar=G[gname][:, l, c:c + 1],
                                                                       in1=RS[:, 0:ncols], op0=ALU.mult, op1=ALU.mult),
                               reads=[bX, bRS, bC], writes=[bHB])

                    def hb_chunk(X, bX, c, post):
                        gname, l = post
                        op("dve", lambda e: e.tensor_scalar(out=HB[:, c, :], in0=X[:, c, :], scalar1=G[gname][:, l, c:c + 1], scalar2=None, op0=ALU.mult),
                           reads=[bX, bC], writes=[bHB])

                    def rmsnorm_def(X, bX, gname, l, hoisted=False):
                        for c in (range(KC) if not hoisted else []):
                            op("dve", lambda e: e.tensor_scalar(out=HB[:, c, :], in0=X[:, c, :], scalar1=G[gname][:, l, c:c + 1], scalar2=None, op0=ALU.mult),
                               reads=[bX, bC], writes=[bHB])
                        op("act", lambda e: e.activation(out=SQ[:, :, :], in_=X[:, :, :], func=AF.Square), reads=[bX], writes=[bSQ])

                        op("pool", lambda e: e.tensor_tensor(out=SQ[:, 0:4, :], in0=SQ[:, 0:4, :], in1=SQ[:, 4:8, :], op=ALU.add), reads=[bSQ], writes=[bSQ])

                        def stats():
                            rms_stats([SQ[:, c, :] for c in range(4)], bSQ, 4, N, ONESB, 1.0 / D, RS, bRS)
                        return stats

                    def ffn(X, bX, i, hoisted=False, post=None):
                        nm, l = ffn_names[i]
                        stats = rmsnorm_def(X, bX, nm + "_norm", l, hoisted)
                        pend = []
                        for j in range(NJ):
                            wt, wbuf = WG.next()
                            dma("sp", wt[:], s_win[i][j], reads=wb["win%d" % i][j], writes=[wbuf])
                            pg, pgb = k.bank()
                            pu, pub = k.bank()
                            for h, (pt, pb) in enumerate(((pg, pgb), (pu, pub))):
                                for kc in range(KC):
                                    op("pe", lambda e: e.matmul(pt[:, :], lhsT=wt[:, kc, h * 128:(h + 1) * 128], rhs=HB[:, kc, :],
                                                                start=(kc == 0), stop=(kc == KC - 1)),
                                       reads=[wbuf, bHB], writes=[pb], inc=(kc == KC - 1))
                            pend.append((j, pg, pgb, pu, pub))
                            if j == 0:
                                continue
                            if j == 1:
                                stats()
                            while pend:
                                jj, pg_, pgb_, pu_, pub_ = pend.pop(0)
                                sg, bsg = TMP.next()
                                op("dve", lambda e: e.tensor_tensor(out=sg[:], in0=pg_[:, :], in1=RS[:], op=ALU.mult), reads=[pgb_, bRS], writes=[bsg])
                                op("act", lambda e: e.activation(out=sg[:], in_=sg[:], func=AF.Silu), reads=[bsg], writes=[bsg])
                                tu, btu = TMP.next()
                                op("dve", lambda e: e.tensor_tensor(out=tu[:], in0=pu_[:, :], in1=RS[:], op=ALU.mult), reads=[pub_, bRS], writes=[btu])
                                op("dve", lambda e: e.tensor_tensor(out=ACTB[:, jj, :], in0=tu[:], in1=sg[:], op=ALU.mult),
                                   reads=[btu, bsg], writes=[bACT[jj]])
                        for c in range(KC):
                            wt, wbuf = WO.next()
                            dma("sp", wt[:], s_wout[i][c], reads=wb["wout%d" % i][c], writes=[wbuf])
                            pt, pb = k.bank()
                            for j in range(NJ):
                                op("pe", lambda e: e.matmul(pt[:, :], lhsT=wt[:, j, :], rhs=ACTB[:, j, :], start=(j == 0), stop=(j == NJ - 1)),
                                   reads=[wbuf, bACT[j]], writes=[pb], inc=(j == NJ - 1))
                            op("dve", lambda e: e.scalar_tensor_tensor(out=X[:, c, :], in0=pt[:, :], scalar=0.5, in1=X[:, c, :],
                                                                       op0=ALU.mult, op1=ALU.add), reads=[pb, bX], writes=[bX])
                            if post is not None:
                                hb_chunk(X, bX, c, post)

                    def add_proj(X, bX, key, scr, SRC, bSRC, post=None):
                        def cons(oc, pt, pb):
                            op("dve", lambda e: e.tensor_tensor(out=X[:, oc, :], in0=pt[:, :], in1=X[:, oc, :], op=ALU.add),
                               reads=[pb, bX], writes=[bX])
                            if post is not None:
                                hb_chunk(X, bX, oc, post)
                        proj_fm(key, scr, 4, SRC, bSRC, N, cons)

                    def headnorm_proj(key, scr, ngroups, SRCH, bSRCH, ncols, lhs, inv_n, per, gain_fn, finish, rs_tok=False, hook=None):
                        hold = []

                        def cons(oc, pt, pb):
                            qf, bqf = TMP.next()
                            if rs_tok:
                                op("dve", lambda e: e.tensor_tensor(out=qf[:, 0:ncols], in0=pt[:, 0:ncols], in1=RS[:, 0:ncols], op=ALU.mult),
                                   reads=[pb, bRS], writes=[bqf])
                                op("act", lambda e: e.activation(out=SQ[:, oc % KC, 0:ncols], in_=qf[:, 0:ncols], func=AF.Square), reads=[bqf], writes=[bSQ])
                            else:
                                op("act", lambda e: e.copy(out=qf[:, 0:ncols], in_=pt[:, 0:ncols]), reads=[pb], writes=[bqf])
                                op("act", lambda e: e.activation(out=SQ[:, oc % KC, 0:ncols], in_=pt[:, 0:ncols], func=AF.Square), reads=[pb], writes=[bSQ])
                            hold.append((oc, qf, bqf))
                            if len(hold) == per:
                                rs, brs = TMP.next()
                                rms_stats([SQ[:, o % KC, 0:ncols] for o, _, _ in hold], bSQ, per, ncols, lhs, inv_n, rs, brs)
                                items = list(hold)
                                hold.clear()

                                def cont():
                                    for o, q, bq in items:
                                        finish(o, q, bq, rs, brs)
                                return cont
                            return None
                        proj_fm(key, scr, ngroups, SRCH, bSRCH, ncols, cons, hook=hook, delay=True)

                    def xattn(X, bX, l, hoisted=False, post=None):
                        stats = rmsnorm_def(X, bX, "xa_norm", l, hoisted)

                        def fin(o, q, bq, rs, brs):
                            op("dve", lambda e: e.scalar_tensor_tensor(out=QB[:, o, :], in0=q[:], scalar=XQG[:, l, (o % 2):(o % 2) + 1], in1=rs[:],
                                                                       op0=ALU.mult, op1=ALU.mult), reads=[bq, brs, bC], writes=[bQB])
                        headnorm_proj("xq%d" % l, s_xq[l], 4, HB, bHB, N, ONESB, 1.0 / 256, 2, None, fin, rs_tok=True, hook=stats)
                        pts = []
                        for h in range(4):
                            p_t, bp_t = PT.next()
                            pts.append((p_t, bp_t))
                            for mc in range(2):
                                pt, pb = k.bank()
                                for ec in range(2):
                                    op("pe", lambda e: e.matmul(pt[:, :], lhsT=KX[l][:, 2 * h + ec, mc * 128:(mc + 1) * 128], rhs=QB[:, 2 * h + ec, :],
                                                                start=(ec == 0), stop=(ec == 1)), reads=[bKX[l], bQB], writes=[pb], inc=(ec == 1))
                                op("act", lambda e: e.activation(out=p_t[:, mc, :], in_=pt[:, :], func=AF.Exp), reads=[pb], writes=[bp_t])
                        rds = []
                        for h in range(4):
                            p_t, bp_t = pts[h]
                            pd, pdb = k.bank()
                            for mc in range(2):
                                op("pe", lambda e: e.matmul(pd[:, :], lhsT=ONESB[:], rhs=p_t[:, mc, :], start=(mc == 0), stop=(mc == 1)),
                                   reads=[bC, bp_t], writes=[pdb], inc=(mc == 1))
                            rd, brd = TMP.next()
                            op("act", lambda e: e.activation(out=rd[:], in_=pd[:, :], func=AF.Ln), reads=[pdb], writes=[brd])
                            op("act", lambda e: e.activation(out=rd[:], in_=rd[:], func=AF.Exp, scale=-1.0), reads=[brd], writes=[brd])
                            rds.append((rd, brd))
                        for h in range(4):
                            p_t, bp_t = pts[h]
                            rd, brd = rds[h]
                            for ec in range(2):
                                po, pob = k.bank()
                                for mc in range(2):
                                    op("pe", lambda e: e.matmul(po[:, :], lhsT=VX[l][:, mc, (2 * h + ec) * 128:(2 * h + ec + 1) * 128], rhs=p_t[:, mc, :],
                                                                start=(mc == 0), stop=(mc == 1)), reads=[bVX[l], bp_t], writes=[pob], inc=(mc == 1))
                                op("dve", lambda e: e.tensor_tensor(out=MIX[:, 2 * h + ec, :], in0=po[:, :], in1=rd[:], op=ALU.mult),
                                   reads=[pob, brd], writes=[bMIX])
                        add_proj(X, bX, "xo%d" % l, s_xo[l], MIX, bMIX, post=post)

                    def load_fm(X, bX, t, scr, bufs, q="sp"):
                        dma(q, X[:], scr[:, :, t * N:(t + 1) * N].rearrange("c p n -> p c n"), reads=[bufs[t]], writes=[bX])

                    def store_fm(X, bX, t, scr, bufs, nch=KC, q="pool"):
                        dma(q, scr[:, :, t * N:(t + 1) * N].rearrange("c p n -> p c n"), X[:, 0:nch, :], reads=[bX], writes=[bufs[t]])

                    if which == "A":
                        MT, bMT = XT.next()
                        for mc in range(2):
                            xi, bxi = XIN.next()
                            dma("sp", xi[:], m_d[mc * 128:(mc + 1) * 128, :], writes=[bxi])
                            for half in range(2):
                                pt, pb = k.bank()
                                for c4 in range(4):
                                    c = half * 4 + c4
                                    op("pe", lambda e: e.transpose(out=pt[:, c4 * 128:(c4 + 1) * 128], in_=xi[:, c * 128:(c + 1) * 128], identity=IDENT[:]),
                                       reads=[bxi, bC], writes=[pb], inc=(c4 == 3))
                                op("dve", lambda e: e.tensor_copy(out=MT[:, half * 4:(half + 1) * 4, mc * 128:(mc + 1) * 128],
                                                                  in_=pt[:, :].rearrange("p (c n) -> p c n", c=4)), reads=[pb], writes=[bMT])
                        for l in range(2):
                            rmsnorm(MT, bMT, "xa_mem_norm", l, ncols=NMEM)

                            def fink(o, q, bq, rs, brs, l=l):
                                op("dve", lambda e: e.scalar_tensor_tensor(out=KX[l][:, o, :], in0=q[:, 0:NMEM], scalar=XKG[:, l, (o % 2):(o % 2) + 1],
                                                                           in1=rs[:, 0:NMEM], op0=ALU.mult, op1=ALU.mult), reads=[bq, brs, bC], writes=[bKX[l]])
                            headnorm_proj("xk%d" % l, s_xk[l], 4, HB, bHB, NMEM, ONESB, 1.0 / 256, 2, None, fink)
                            dma("sp", WT[:], s_xv[l], reads=wb["xv%d" % l][0], writes=bWTl)
                            for mc in range(2):
                                for half in range(2):
                                    pt, pb = k.bank()
                                    for kc in range(KC):
                                        op("pe", lambda e: e.matmul(pt[:, :], lhsT=HB[:, kc, mc * 128:(mc + 1) * 128], rhs=WT[:, kc, half * 512:(half + 1) * 512],
                                                                    start=(kc == 0), stop=(kc == KC - 1)), reads=[bHB] + bWTl, writes=[pb], inc=(kc == KC - 1))
                                    op("act", lambda e: e.copy(out=VX[l][:, mc, half * 512:(half + 1) * 512], in_=pt[:, :]), reads=[pb], writes=[bVX[l]])

                        with contextlib.ExitStack() as es2:
                            U = k.sb([128, 4, N], BF16, es2); bU = Buf()
                            ZXo = k.sb([128, 4, N], F32, es2); bZXo = Buf()
                            GZo = k.sb([128, 4, N], BF16, es2); bGZo = Buf()
                            VN = k.sb([128, 4, 512], BF16, es2); bVN = Buf()
                            AO = k.sb([128, 4, N], BF16, es2); bAO = Buf()
                            ST = k.sb([128, 12], F32, es2); bST = [Buf() for _ in range(6)]
                            def x_loads(t):
                                ld = []
                                for tc in range(4):
                                    xi, bxi = XIN.next()
                                    r0 = t * N + tc * 128
                                    dma("sp", xi[:], x_d[r0:r0 + 128, :], writes=[bxi])
                                    ld.append((xi, bxi))
                                return ld

                            def x_transposes(ld):
                                X, bX = XT.next()
                                for tc, (xi, bxi) in enumerate(ld):
                                    for half in range(2):
                                        pt, pb = k.bank()
                                        for c4 in range(4):
                                            c = half * 4 + c4
                                            op("pe", lambda e: e.transpose(out=pt[:, c4 * 128:(c4 + 1) * 128], in_=xi[:, c * 128:(c + 1) * 128], identity=IDENT[:]),
                                               reads=[bxi, bC], writes=[pb], inc=(c4 == 3))
                                        op("dve", lambda e: e.tensor_copy(out=X[:, half * 4:(half + 1) * 4, tc * 128:(tc + 1) * 128],
                                                                          in_=pt[:, :].rearrange("p (c n) -> p c n", c=4)), reads=[pb], writes=[bX])
                                for c in range(KC):
                                    hb_chunk(X, bX, c, ("ffn1_norm", 0))
                                return X, bX

                            nxtX = x_transposes(x_loads(0))
                            for t in range(NT):
                                X, bX = nxtX
                                if t + 1 < NT:
                                    ld_next = x_loads(t + 1)
                                ffn(X, bX, 0, hoisted=True)
                                store_fm(X, bX, t, a_XS, tb["XS"])
                                rmsnorm(X, bX, "mix_norm", 0)

                                def cons(oc, pt, pb):
                                    kind, c = oc // 4, oc % 4
                                    if kind == 1:
                                        op("act", lambda e: e.copy(out=ZXo[:, c, :], in_=pt[:, :]), reads=[pb], writes=[bZXo])
                                    else:
                                        t1, bt1 = TMP.next()
                                        if kind == 0:
                                            gelu_from_psum(pt, pb, N, U[:, c, :], bU, t1[:], bt1)
                                        else:
                                            gelu_from_psum(pt, pb, N, GZo[:, c, :], bGZo, t1[:], bt1)
                                proj_fm("ev_in", s_ev_in, 6, HB, bHB, N, cons)
                                dma("pool", a_ZX[:, :, t * N:(t + 1) * N].rearrange("c p n -> p c n"), ZXo[:], reads=[bZXo], writes=[tb["ZX"][t]])
                                dma("pool", a_GZ[:, :, t * N:(t + 1) * N].rearrange("c p n -> p c n"), GZo[:], reads=[bGZo], writes=[tb["GZ"][t]])
                                dma("sp", WT[:, :, 0:512], s_ev_v, reads=wb["ev_v"][0], writes=bWTl)
                                zb = []
                                for tc in range(4):
                                    pt, pb = k.bank()
                                    for kc in range(KC):
                                        op("pe", lambda e: e.matmul(pt[:, :], lhsT=HB[:, kc, tc * 128:(tc + 1) * 128], rhs=WT[:, kc, 0:512],
                                                                    start=(kc == 0), stop=(kc == KC - 1)), reads=[bHB] + bWTl, writes=[pb], inc=(kc == KC - 1))
                                    vg, bvg = TMP.next()
                                    t1, bt1 = TMP.next()
                                    zb.append((pt, pb, vg, bvg, t1, bt1))
                                for pt, pb, vg, bvg, t1, bt1 in zb:
                                    op("act", lambda e: e.activation(out=t1[:], in_=pt[:, :], func=AF.Square), reads=[pb], writes=[bt1])
                                for pt, pb, vg, bvg, t1, bt1 in zb:
                                    op("dve", lambda e: e.tensor_scalar(out=t1[:], in0=t1[:], scalar1=0.044715, scalar2=1.0, op0=ALU.mult, op1=ALU.add),
                                       reads=[bt1], writes=[bt1])
                                    op("dve", lambda e: e.tensor_tensor(out=t1[:], in0=t1[:], in1=pt[:, :], op=ALU.mult), reads=[bt1, pb], writes=[bt1])
                                for pt, pb, vg, bvg, t1, bt1 in zb:
                                    op("act", lambda e: e.activation(out=t1[:], in_=t1[:], func=AF.Sigmoid, scale=GELU_C), reads=[bt1], writes=[bt1])
                                for tc, (pt, pb, vg, bvg, t1, bt1) in enumerate(zb):
                                    op("dve", lambda e: e.tensor_tensor(out=vg[:], in0=t1[:], in1=pt[:, :], op=ALU.mult), reads=[bt1, pb], writes=[bvg])
                                    op("dve", lambda e: e.reduce_sum(out=ST[:, tc:tc + 1], in_=vg[:], axis=mybir.AxisListType.X), reads=[bvg], writes=[bST[tc]])
                                    op("dve", lambda e: e.tensor_scalar(out=ST[:, tc:tc + 1], in0=ST[:, tc:tc + 1], scalar1=-1.0 / 512, scalar2=None, op0=ALU.mult),
                                       reads=[bST[tc]], writes=[bST[tc]])
                                for tc, (pt, pb, vg, bvg, t1, bt1) in enumerate(zb):
                                    op("act", lambda e: e.activation(out=t1[:], in_=vg[:], func=AF.Square, bias=ST[:, tc:tc + 1]), reads=[bvg, bST[tc]], writes=[bt1])
                                for tc, (pt, pb, vg, bvg, t1, bt1) in enumerate(zb):
                                    op("dve", lambda e: e.reduce_sum(out=ST[:, 4 + tc:5 + tc], in_=t1[:], axis=mybir.AxisListType.X), reads=[bt1], writes=[bST[4]])
                                op("act", lambda e: e.activation(out=ST[:, 8:12], in_=ST[:, 4:8], func=AF.Sqrt, scale=1.0 / 512, bias=CST[:, 0:1]),
                                   reads=[bST[4], bC], writes=[bST[5]])
                                op("dve", lambda e: e.reciprocal(out=ST[:, 8:12], in_=ST[:, 8:12]), reads=[bST[5]], writes=[bST[5]])
                                for tc, (pt, pb, vg, bvg, t1, bt1) in enumerate(zb):
                                    op("dve", lambda e: e.tensor_scalar(out=vg[:], in0=vg[:], scalar1=ST[:, tc:tc + 1], scalar2=ST[:, 8 + tc:9 + tc], op0=ALU.add, op1=ALU.mult),
                                       reads=[bvg, bST[tc], bST[5]], writes=[bvg])
                                    op("dve", lambda e: e.tensor_tensor(out=vg[:], in0=vg[:], in1=LNG[:], op=ALU.mult), reads=[bvg, bC], writes=[bvg])
                                    op("dve", lambda e: e.tensor_tensor(out=VN[:, tc, :], in0=vg[:], in1=LNB[:], op=ALU.add), reads=[bvg, bC], writes=[bVN])
                                if t + 1 < NT:
                                    nxtX = x_transposes(ld_next)
                                for g in range(4):
                                    pt, pb = k.bank()
                                    for tc in range(4):
                                        op("pe", lambda e: e.matmul(pt[:, tc * 128:(tc + 1) * 128], lhsT=VN[:, tc, g * 128:(g + 1) * 128], rhs=WST[:, g, :],
                                                                    start=True, stop=True), reads=[bVN, bC], writes=[pb], inc=(tc == 3))
                                    t1, bt1 = TMP.next()
                                    op("dve", lambda e: e.tensor_tensor(out=t1[:], in0=pt[:, :], in1=BSB[:, g, :, :].rearrange("p a b -> p (a b)"), op=ALU.add),
                                       reads=[pb, bC], writes=[bt1])
                                    op("dve", lambda e: e.tensor_tensor(out=AO[:, g, :], in0=t1[:], in1=U[:, g, :], op=ALU.mult), reads=[bt1, bU], writes=[bAO])
                                dma("pool", a_AO[:, :, t * N:(t + 1) * N].rearrange("c p n -> p c n"), AO[:], reads=[bAO], writes=[tb["AO"][t]])
                                drain_dq(dq_per_tile)
                            drain_dq(len(DQ))
                            k.barrier()

                    if which == "B":
                        with contextlib.ExitStack() as es2:
                            ROPE = Ring(k, 1, [128, 2, N], F32, es2)
                            QBF = Ring(k, 3, [128, N], BF16, es2)
                            QTo, bQTo = QB, bQB
                            KTo = k.sb([128, KC, N], BF16, es2); bKTo = Buf()
                            VTo = MIX[:, :, :].rearrange("p (tc a) n -> p tc (a n)", a=2)
                            bVTo = bMIX
                            def b_loads(t):
                                X, bX = XT.next()
                                load_fm(X, bX, t, a_XS, tb["XS"])
                                mi, bmi = MIXIN.next()
                                dma("sp", mi[:, 0:4, :], a_AO[:, :, t * N:(t + 1) * N].rearrange("c p n -> p c n"), reads=[tb["AO"][t]], writes=[bmi])
                                dma("sp", mi[:, 4:8, :], a_BO[:, :, t * N:(t + 1) * N].rearrange("c p n -> p c n"), reads=[tb["BO"][t]], writes=[bmi])
                                return X, bX, mi, bmi

                            nxt = b_loads(0)
                            for t in range(NT):
                                X, bX, mi, bmi = nxt
                                add_proj(X, bX, "ev_out", s_ev_out, mi, bmi, post=("xa_norm", 0))
                                xattn(X, bX, 0, hoisted=True, post=("ffn2_norm", 0))
                                if t + 1 < NT:
                                    nxt = b_loads(t + 1)
                                ffn(X, bX, 1, hoisted=True, post=("ffn1_norm", 1))
                                rp, brp = ROPE.next()
                                dma("sp", rp[:, 0, :], c_cos[:, t * N:(t + 1) * N], writes=[brp])
                                dma("sp", rp[:, 1, :], c_sin[:, t * N:(t + 1) * N], writes=[brp])
                                ffn(X, bX, 2, hoisted=True)
                                store_fm(X, bX, t, a_XS, tb["XS"])
                                rmsnorm(X, bX, "mix_norm", 1)

                                def finqk(o, q, bq, rs, brs):
                                    isk = o // KC
                                    dst, bdst = (KTo, bKTo) if isk else (QTo, bQTo)
                                    qb, bqb = QBF.next()
                                    op("dve", lambda e: e.scalar_tensor_tensor(out=qb[:], in0=q[:], scalar=OQG[:, isk:isk + 1], in1=rs[:],
                                                                               op0=ALU.mult, op1=ALU.mult), reads=[bq, brs, bC], writes=[bqb])
                                    pt, pb = k.bank()
                                    op("pe", lambda e: e.matmul(pt[:, :], lhsT=ROTB[:], rhs=qb[:], start=True, stop=True), reads=[bC, bqb], writes=[pb])
                                    t2, bt2 = TMP.next()
                                    op("dve", lambda e: e.tensor_tensor(out=t2[:], in0=pt[:, :], in1=rp[:, 1, :], op=ALU.mult), reads=[pb, brp], writes=[bt2])
                                    op("pool", lambda e: e.tensor_tensor(out=q[:], in0=qb[:], in1=rp[:, 0, :], op=ALU.mult), reads=[bqb, brp], writes=[bq])
                                    op("dve", lambda e: e.tensor_tensor(out=dst[:, o % KC, :], in0=q[:], in1=t2[:], op=ALU.add), reads=[bq, bt2], writes=[bdst])
                                headnorm_proj("od_qk", s_od_qk, 8, HB, bHB, N, BLK1, 1.0 / 64, 1, None, finqk)
                                dma("pool", a_QT[:, :, t * N:(t + 1) * N].rearrange("c p n -> p c n"), QTo[:], reads=[bQTo], writes=[tb["QT"][t]])
                                dma("pool", a_KT[:, :, t * N:(t + 1) * N].rearrange("c p n -> p c n"), KTo[:], reads=[bKTo], writes=[tb["KT"][t]])
                                dma("sp", WT[:], s_od_v, reads=wb["od_v"][0], writes=bWTl)
                                for tc in range(4):
                                    for half in range(2):
                                        pt, pb = k.bank()
                                        for kc in range(KC):
                                            op("pe", lambda e: e.matmul(pt[:, :], lhsT=HB[:, kc, tc * 128:(tc + 1) * 128], rhs=WT[:, kc, half * 512:(half + 1) * 512],
                                                                        start=(kc == 0), stop=(kc == KC - 1)), reads=[bHB] + bWTl, writes=[pb], inc=(kc == KC - 1))
                                        op("act", lambda e: e.copy(out=VTo[:, tc, half * 512:(half + 1) * 512], in_=pt[:, :]), reads=[pb], writes=[bVTo])
                                dma("pool", a_V[t * N:(t + 1) * N, :].rearrange("(tc p) d -> p tc d", p=128), VTo[:], reads=[bVTo], writes=[tb["V"][t]])
                            k.barrier()

                    if which == "C2":
                        with contextlib.ExitStack() as es2:
                            XO = Ring(k, 2, [128, D], F32, es2)
                            def c_loads(t):
                                X, bX = XT.next()
                                load_fm(X, bX, t, a_XS, tb["XS"])
                                mi, bmi = MIXIN.next()
                                dma("sp", mi[:], a_OT[:, :, t * N:(t + 1) * N].rearrange("c p n -> p c n"), reads=[tb["OT"][t]], writes=[bmi])
                                return X, bX, mi, bmi

                            nxt = c_loads(0)
                            for t in range(NT):
                                X, bX, mi, bmi = nxt
                                add_proj(X, bX, "od_out", s_od_out, mi, bmi, post=("xa_norm", 1))
                                xattn(X, bX, 1, hoisted=True, post=("ffn2_norm", 1))
                                if t + 1 < NT:
                                    nxt = c_loads(t + 1)
                                ffn(X, bX, 3, hoisted=True)
                                for tc in range(4):
                                    xo, bxo = XO.next()
                                    for half in range(2):
                                        pt, pb = k.bank()
                                        for c4 in range(4):
                                            c = half * 4 + c4
                                            op("pe", lambda e: e.transpose(out=pt[:, c4 * 128:(c4 + 1) * 128], in_=X[:, c, tc * 128:(tc + 1) * 128], identity=IDENT[:]),
                                               reads=[bX, bC], writes=[pb], inc=(c4 == 3))
                                        op("act", lambda e: e.copy(out=xo[:, half * 512:(half + 1) * 512], in_=pt[:, :]), reads=[pb], writes=[bxo])
                                    r0 = t * N + tc * 128
                                    dma("sp", y_d[r0:r0 + 128, :], xo[:], reads=[bxo])
                            k.barrier()

            def other_stage(which):
                if which == "R":
                    with contextlib.ExitStack() as es2:
                        PZ = 1024
                        NP = S // PZ
                        ZF = Ring(k, 1, [128, SMAX + 4], F32, es2)
                        HT = k.sb([128, SMAX], F32, es2); bHT = Buf()
                        R1 = Ring(k, 10, [128, PZ], F32, es2)
                        HP = Ring(k, 3, [128, PZ], F32, es2)
                        XCB = Ring(k, 3, [128, PZ], BF16, es2)
                        GZp = Ring(k, 2, [128, PZ], BF16, es2)
                        BOp = Ring(k, 2, [128, PZ], BF16, es2)
                        zfs = {}

                        def zf_load(c):
                            if c in zfs or c >= 4:
                                return
                            zf, bzf = ZF.next()
                            op("pool", lambda e: e.memset(zf[:, 0:2], 0.0), writes=[bzf])
                            op("pool", lambda e: e.memset(zf[:, S + 2:S + 4], 0.0), writes=[bzf])
                            dma("sp", zf[:, 2:2 + S], a_ZX[c, :, 0:S], reads=tb["ZX"][0:NT], writes=[bzf])
                            zfs[c] = (zf, bzf)

                        rjobs = []
                        for c in range(4):
                            for d in range(2):
                                for pi in (range(NP) if d == 0 else range(NP - 1, -1, -1)):
                                    rjobs.append(dict(c=c, d=d, pi=pi, first=(pi == (0 if d == 0 else NP - 1))))

                        def ph1(J):
                            c, d, p0 = J["c"], J["d"], J["pi"] * PZ
                            zf_load(c)
                            zf, bzf = zfs[c]
                            xc, bxc = R1.next()
                            op("dve", lambda e: e.tensor_scalar(out=xc[:], in0=zf[:, p0:p0 + PZ], scalar1=CW[:, 0, c:c + 1], scalar2=CB[:, c:c + 1],
                                                                op0=ALU.mult, op1=ALU.add), reads=[bzf, bC], writes=[bxc])
                            for tp in range(1, 4):
                                op("dve", lambda e: e.scalar_tensor_tensor(out=xc[:], in0=zf[:, p0 + tp:p0 + tp + PZ], scalar=CW[:, tp, c:c + 1], in1=xc[:],
                                                                           op0=ALU.mult, op1=ALU.add), reads=[bzf, bC, bxc], writes=[bxc])
                            xb, bxb = XCB.next()
                            op("act", lambda e: e.copy(out=xb[:], in_=xc[:]), reads=[bxc], writes=[bxb])
                            rg, brg = R1.next()
                            ig, big = R1.next()
                            for ai, (dst, bdst, bias) in enumerate(((rg, brg, BA), (ig, big, BI))):
                                for q4 in range(PZ // 512):
                                    pt, pb = k.bank()
                                    op("pe", lambda e: e.matmul(pt[:, :], lhsT=WBD[:, (d * 2 + ai) * 4 + c, :], rhs=xb[:, q4 * 512:(q4 + 1) * 512],
                                                                start=True, stop=True), reads=[bC, bxb], writes=[pb])
                                    op("act", lambda e: e.activation(out=dst[:, q4 * 512:(q4 + 1) * 512], in_=pt[:, :], func=AF.Sigmoid,
                                                                     bias=bias[:, d, c:c + 1]), reads=[pb, bC], writes=[bdst])
                            op("act", lambda e: e.activation(out=rg[:], in_=rg[:], func=AF.Exp, scale=NSP[:, d, c:c + 1]), reads=[brg, bC], writes=[brg])
                            mm, bmm = R1.next()
                            op("act", lambda e: e.activation(out=mm[:], in_=rg[:], func=AF.Square, scale=1.0 - 1e-6), reads=[brg], writes=[bmm])
                            op("act", lambda e: e.activation(out=mm[:], in_=mm[:], func=AF.Sqrt, scale=-1.0, bias=CST[:, 1:2]), reads=[bmm, bC], writes=[bmm])
                            if d == 1:
                                gz, bgz = GZp.next()
                                dma("sp", gz[:], a_GZ[c, :, p0:p0 + PZ], reads=tb["GZ"][p0 // N:(p0 + PZ) // N], writes=[bgz])
                                J["gz"] = (gz, bgz)
                            J["t"] = (xc, bxc, rg, brg, ig, big, mm, bmm)

                        prev = [None]

                        def ph2(J):
                            c, d, p0 = J["c"], J["d"], J["pi"] * PZ
                            xc, bxc, rg, brg, ig, big, mm, bmm = J["t"]
                            op("pool", lambda e: e.tensor_tensor(out=ig[:], in0=ig[:], in1=xc[:], op=ALU.mult), reads=[big, bxc], writes=[big])
                            op("pool", lambda e: e.tensor_tensor(out=ig[:], in0=ig[:], in1=mm[:], op=ALU.mult), reads=[big, bmm], writes=[big])
                            hp, bhp = HP.next()
                            pv = None if J["first"] else prev[0]
                            if d == 0:
                                init = 0.0 if pv is None else pv[0][:, PZ - 1:PZ]
                                op("dve", lambda e: e.tensor_tensor_scan(out=hp[:], data0=rg[:], data1=ig[:], initial=init, op0=ALU.mult, op1=ALU.add),
                                   reads=[brg, big] + ([pv[1]] if pv else []), writes=[bhp])
                                op("act", lambda e: e.copy(out=HT[:, p0:p0 + PZ], in_=hp[:]), reads=[bhp], writes=[bHT])
                            else:
                                init = 0.0 if pv is None else pv[0][:, 0:1]
                                op("dve", lambda e: e.tensor_tensor_scan(out=hp[:, ::-1], data0=rg[:, ::-1], data1=ig[:, ::-1], initial=init,
                                                                         op0=ALU.mult, op1=ALU.add),
                                   reads=[brg, big] + ([pv[1]] if pv else []), writes=[bhp])
                                gz, bgz = J["gz"]
                                tot, btot = R1.next()
                                op("pool", lambda e: e.tensor_tensor(out=tot[:], in0=hp[:], in1=HT[:, p0:p0 + PZ], op=ALU.add), reads=[bhp, bHT], writes=[btot])
                                bo, bbo = BOp.next()
                                op("pool", lambda e: e.tensor_tensor(out=bo[:], in0=tot[:], in1=gz[:], op=ALU.mult), reads=[btot, bgz], writes=[bbo])
                                dma("pool", a_BO[c, :, p0:p0 + PZ], bo[:], reads=[bbo], writes=tb["BO"][p0 // N:(p0 + PZ) // N])
                            prev[0] = (hp, bhp)

                        ph1(rjobs[0])
                        for ji, J in enumerate(rjobs):
                            if ji + 1 < len(rjobs):
                                ph1(rjobs[ji + 1])
                            ph2(J)
                        k.barrier()

                if which == "C1":
                    with contextlib.ExitStack() as es2:
                        KW = Ring(k, 2, [128, AB + 2048], BF16, es2)
                        QC = Ring(k, 2, [128, AB], BF16, es2)
                        VWB = Ring(k, 2, [128, 17, 2, 128], BF16, es2)
                        VWS = Ring(k, 6, [128, 5, 2, 128], BF16, es2)
                        VW16 = k.sb([128, 2, 16, 2, 128], BF16, es2); bVW16 = Buf()
                        ACC = Ring(k, 2, [128, 2, AB], F32, es2)
                        PF = Ring(k, 4, [128, 2, 4, 128], BF16, es2)
                        PM = Ring(k, 4, [128, 2, 4, 128], BF16, es2)
                        RD = k.sb([128, AB], F32, es2); bRD = Buf()
                        OC = Ring(k, 2, [128, AB], BF16, es2)
                        MV = k.sb([128, 6, 4, 128], BF16, es2); bMV = Buf()
                        variants = [(0, 1, 0, 1), (2, 1, 0, 1), (0, 1, 0, 3), (2, 1, 0, 3), (2, 1, 2, 1), (0, 3, 0, 3)]
                        IDB = k.sb([128, 128], BF16, es2)
                        for vi, var in enumerate(variants):
                            for ci, mt in enumerate(var):
                                op("pool", lambda e: e.tensor_copy(out=MV[:, vi, ci, :], in_=MASK[:, mt, 0, :]), reads=[bC], writes=[bMV])
                        op("pool", lambda e: e.tensor_scalar(out=MV[:], in0=MV[:], scalar1=1.0, scalar2=30000.0, op0=ALU.subtract, op1=ALU.mult),
                           reads=[bMV], writes=[bMV])
                        op("pool", lambda e: e.tensor_copy(out=IDB[:], in_=IDENT[:]), reads=[bC], writes=[bMV])
                        for i in range(2):
                            op("pool", lambda e: e.memset(KW.t[i][:], 0.0), writes=[KW.b[i]])
                        op("pool", lambda e: e.memset(VW16[:], 0.0), writes=[bVW16])
                        op("pool", lambda e: e.memset(VW16[:, :, :, :, 64:128], 1.0), writes=[bVW16])
                        for VWr in (VWB, VWS):
                            for i in range(len(VWr.t)):
                                op("pool", lambda e: e.memset(VWr.t[i][:], 0.0), writes=[VWr.b[i]])
                                op("pool", lambda e: e.memset(VWr.t[i][:, :, :, 64:128], 1.0), writes=[VWr.b[i]])
                        NB = S // AB
                        classes = [(d, r) for d in DILS if d < 16 for r in range(d)]

                        def load_vw(c, P0, d, r):
                            vw, bvw = (VWB if d == 1 else VWS).next()
                            ng = AB // (128 * d)
                            L = S // d
                            m0 = P0 // d
                            full = [i for i in range(ng + 1) if m0 - 64 + 128 * i >= 0 and m0 - 64 + 128 * i + 128 <= L]
                            if full:
                                i0, i1 = full[0], full[-1] + 1
                                rs = (m0 - 64 + 128 * i0) * d + r
                                nrow = 128 * (i1 - i0)
                                for h in range(2):
                                    src = a_V[rs:rs + (nrow - 1) * d + 1:d, c * 128 + h * 64:c * 128 + (h + 1) * 64]
                                    dma("sp", vw[:, i0:i1, h, 0:64], src.rearrange("(i p) e -> p i e", p=128),
                                        reads=tb["V"][rs // N:(rs + (nrow - 1) * d) // N + 1], writes=[bvw])
                            for i in range(ng + 1):
                                if i in full:
                                    continue
                                a0 = m0 - 64 + 128 * i
                                a_, b_ = max(a0, 0), min(a0 + 128, L)
                                if b_ <= a_:
                                    continue
                                src = a_V[a_ * d + r:(b_ - 1) * d + r + 1:d, c * 128:(c + 1) * 128]
                                dma("sp", vw[a_ - a0:b_ - a0, i, :, 0:64], src.rearrange("k (h e) -> k h e", h=2),
                                    reads=tb["V"][(a_ * d + r) // N:((b_ - 1) * d + r) // N + 1], writes=[bvw])
                            return vw, bvw

                        def load_vw16(ci):
                            if ci >= len(chunks):
                                return
                            blk, c = chunks[ci]
                            L = S // 16
                            m0 = blk * AB // 16
                            for i in range(2):
                                a0 = m0 - 64 + 128 * i
                                plo, phi = max(0, -a0), min(128, L - a0)
                                if phi <= plo:
                                    continue
                                rs, re = (a0 + plo) * 16, (a0 + phi) * 16
                                for h in range(2):
                                    src = a_V[rs:re, c * 128 + h * 64:c * 128 + (h + 1) * 64]
                                    dma("sp", VW16[plo:phi, i, :, h, 0:64], src.rearrange("(p r) e -> p r e", r=16),
                                        reads=tb["V"][rs // N:(re - 1) // N + 1], writes=[bVW16])

                        chunks = [(blk, c) for blk in range(NB) for c in range(KC)]
                        cstate = {}

                        def chunk_loads(ci):
                            if ci >= len(chunks) or ci in cstate:
                                return
                            blk, c = chunks[ci]
                            P0 = blk * AB
                            kw, bkw = KW.next()
                            lo, hi = max(0, P0 - 1024), min(S, P0 + AB + 1024)
                            dma("sp", kw[:, lo - (P0 - 1024):hi - (P0 - 1024)], a_KT[c, :, lo:hi], reads=tb["KT"][lo // N:hi // N], writes=[bkw])
                            qc, bqc = QC.next()
                            dma("sp", qc[:], a_QT[c, :, P0:P0 + AB], reads=tb["QT"][P0 // N:(P0 + AB) // N], writes=[bqc])
                            cstate[ci] = dict(kw=kw, bkw=bkw, qc=qc, bqc=bqc, vws={}, acc=None)

                        def ensure_vw(ci, k_idx):
                            while k_idx >= len(classes):
                                ci += 1
                                k_idx -= len(classes)
                            if ci >= len(chunks):
                                return
                            chunk_loads(ci)
                            st = cstate[ci]
                            if k_idx not in st["vws"]:
                                blk, c = chunks[ci]
                                d, r = classes[k_idx]
                                st["vws"][k_idx] = load_vw(c, blk * AB, d, r)

                        jobs = []
                        for ci, (blk, c) in enumerate(chunks):
                            pl = []
                            for d in DILS:
                                ng = AB // (128 * d)
                                if ng >= 2:
                                    for r in range(d):
                                        for g in range(0, ng, 2):
                                            pl.append(((d, r, g), (d, r, g + 1)))
                                else:
                                    for r in range(0, d, 2):
                                        pl.append(((d, r, 0), (d, r + 1, 0)))
                            for pi, pair in enumerate(pl):
                                jobs.append(dict(ci=ci, pair=pair, first=(pi == 0), last=(pi == len(pl) - 1)))

                        def s1(job):
                            ci = job["ci"]
                            blk, c = chunks[ci]
                            P0 = blk * AB
                            if job["first"]:
                                chunk_loads(ci)
                                chunk_loads(ci + 1)
                                cstate[ci]["acc"] = ACC.next()
                            st = cstate[ci]
                            kw, bkw, qc, bqc = st["kw"], st["bkw"], st["qc"], st["bqc"]
                            d = job["pair"][0][0]
                            L = S // d
                            m0 = P0 // d
                            combos = []
                            for (d_, r, g) in job["pair"]:
                                if d_ < 16:
                                    kidx = classes.index((d_, r))
                                    ensure_vw(ci, kidx)
                                    ensure_vw(ci, kidx + 1)
                                    ensure_vw(ci, kidx + 2)
                                    vw, bvw = st["vws"][kidx]
                                else:
                                    vw, bvw = VW16[:, :, r, :, :], bVW16
                                mabs = m0 + 128 * g
                                for part in range(2):
                                    i = g + part
                                    col0 = 1024 - 64 * d + 128 * d * i + r
                                    if part == 0:
                                        mt = 2 if mabs == 0 else 0
                                    else:
                                        mt = 3 if mabs + 128 == L else 1
                                    combos.append((r, g, i, col0, mt, vw, bvw))
                            vi = variants.index(tuple(cb[4] for cb in combos))
                            pm, bpm = PM.next()
                            sbk = [k.bank(), k.bank()]
                            for hp in range(2):
                                pss, pssb = sbk[hp]
                                op("pe", lambda e: e.matmul(pss[:, :], lhsT=IDB[:], rhs=MV[:, vi, :, :].rearrange("p c n -> p (c n)"),
                                                            start=True, stop=False), reads=[bMV], writes=[pssb], inc=False)
                            for cj, (r, g, i, col0, mt, vw, bvw) in enumerate(combos):
                                q0 = 128 * g * d + r
                                for hp in range(2):
                                    pss, pssb = sbk[hp]
                                    op("pe", lambda e: e.matmul(pss[:, cj * 128:(cj + 1) * 128],
                                                                lhsT=kw[hp * 64:(hp + 1) * 64, col0:col0 + 127 * d + 1:d],
                                                                rhs=qc[hp * 64:(hp + 1) * 64, q0:q0 + 127 * d + 1:d],
                                                                start=False, stop=(cj == 3)), reads=[bkw, bqc], writes=[pssb], inc=(cj == 3))
                            for hp in range(2):
                                pss, pssb = sbk[hp]
                                op("act", lambda e: e.activation(out=pm[:, hp, :, :], in_=pss[:, :].rearrange("p (c n) -> p c n", c=4), func=AF.Exp),
                                   reads=[pssb], writes=[bpm])
                            job["combos"], job["pm"], job["bpm"] = combos, pm, bpm

                        def s2(job):
                            ci = job["ci"]
                            blk, c = chunks[ci]
                            P0 = blk * AB
                            st = cstate[ci]
                            acc, bacc = st["acc"]
                            combos, pm, bpm = job["combos"], job["pm"], job["bpm"]
                            d = job["pair"][0][0]
                            pso, psob = k.bank()
                            for it in range(2):
                                for hp in range(2):
                                    for part in range(2):
                                        r, g, i, col0, mt, vw, bvw = combos[it * 2 + part]
                                        op("pe", lambda e: e.matmul(pso[:, (it * 2 + hp) * 128:(it * 2 + hp + 1) * 128], lhsT=vw[:, i, hp, :],
                                                                    rhs=pm[:, hp, it * 2 + part, :], start=(part == 0), stop=(part == 1)),
                                           reads=[bvw, bpm], writes=[psob], inc=(it == 1 and hp == 1 and part == 1))
                            r0_, g0_ = combos[0][0], combos[0][1]
                            src_ap = pso[:, :].rearrange("p (i h q) -> p h i q", i=2, h=2)
                            if d < 16:
                                q0 = 128 * g0_ * d + r0_
                                acc_ap = acc[:, :, q0:q0 + 255 * d + 1:d].rearrange("p h (i q) -> p h i q", i=2)
                            else:
                                acc_ap = acc[:, :, :].rearrange("p h (q s) -> p h s q", s=16)[:, :, r0_:r0_ + 2, :]
                            if d == 1:
                                op("dve", lambda e: e.tensor_copy(out=acc_ap, in_=src_ap), reads=[psob], writes=[bacc])
                            else:
                                op("dve", lambda e: e.tensor_tensor(out=acc_ap, in0=acc_ap, in1=src_ap, op=ALU.add), reads=[psob, bacc], writes=[bacc])
                            if job["last"]:
                                oc, boc = OC.next()
                                op("act", lambda e: e.activation(out=acc[64:128, :, :], in_=acc[64:128, :, :], func=AF.Ln), reads=[bacc], writes=[bacc])
                                op("act", lambda e: e.activation(out=acc[64:128, :, :], in_=acc[64:128, :, :], func=AF.Exp, scale=-1.0), reads=[bacc], writes=[bacc])
                                for hp in range(2):
                                    op("dve", lambda e: e.tensor_copy(out=RD[0:64, :], in_=acc[64:128, hp, :]), reads=[bacc], writes=[bRD])
                                    if hp == 0:
                                        op("dve", lambda e: e.tensor_tensor(out=oc[0:64, :], in0=acc[0:64, 0, :], in1=RD[0:64, :], op=ALU.mult),
                                           reads=[bacc, bRD], writes=[boc])
                                    else:
                                        op("dve", lambda e: e.tensor_tensor(out=RD[0:64, :], in0=acc[0:64, 1, :], in1=RD[0:64, :], op=ALU.mult),
                                           reads=[bacc, bRD], writes=[bRD])
                                        op("dve", lambda e: e.tensor_copy(out=oc[64:128, :], in_=RD[0:64, :]), reads=[bRD], writes=[boc])
                                dma("pool", a_OT[c, :, P0:P0 + AB], oc[:], reads=[boc], writes=tb["OT"][P0 // N:(P0 + AB) // N])
                                del cstate[ci]["vws"]
                                load_vw16(ci + 1)

                        load_vw16(0)
                        s1(jobs[0])
                        s1(jobs[1])
                        for ji, job in enumerate(jobs):
                            if ji + 2 < len(jobs):
                                s1(jobs[ji + 2])
                            s2(job)
                        k.barrier()


            stop = False
            for stg in ("A", "R", "B", "C1", "C2"):
                (ffn_stage if stg in ("A", "B", "C2") else other_stage)(stg)
                if debug == stg:
                    stop = True
                    break
            if stop:
                break
        k_unused = None
        k.finish()
    return nc


def _consts(SMAX):
    ident = np.eye(128, dtype=np.float32)
    rot = np.zeros((128, 128), np.float32)
    for hp in range(2):
        for i in range(8):
            rot[hp * 64 + i + 8, hp * 64 + i] = -1.0
            rot[hp * 64 + i, hp * 64 + i + 8] = 1.0
    half = 8
    inv = np.power(np.float32(ROPE_THETA), -np.arange(half, dtype=np.float32) * np.float32(2.0 / 16)).astype(np.float32)
    pos = np.arange(SMAX, dtype=np.float32)
    ang = (pos[:, None] * inv[None, :]).astype(np.float32)
    cos = np.ones((128, SMAX), np.float32)
    sin = np.zeros((128, SMAX), np.float32)
    for hp in range(2):
        for i in range(16):
            cos[hp * 64 + i] = np.cos(ang[:, i % 8])
            sin[hp * 64 + i] = np.sin(ang[:, i % 8])
    kk = np.arange(128)[:, None]
    qq = np.arange(128)[None, :]
    mA = (kk >= qq)
    mB = (kk <= qq)
    masks = np.stack([mA, mB, mA & (kk >= 64), mB & (kk < 64)], axis=1).astype(np.float32)
    masks = np.repeat(masks[:, :, None, :], 2, axis=2)
    return dict(c_ident=ident, c_rot=rot, c_cos=cos, c_sin=sin, c_mask=np.ascontiguousarray(masks))


_WNAMES = ["ffn1_norm", "ffn1_w_in", "ffn1_w_out", "mix_norm", "ev_w_in", "ev_w_out", "gm_ln_g", "gm_ln_b", "gm_w_s", "gm_b_s",
           "rg_conv_w", "rg_conv_b", "rg_w_a", "rg_b_a", "rg_w_i", "rg_b_i", "rg_lam", "od_w_in", "od_w_out", "od_q_norm",
           "od_k_norm", "xa_norm", "xa_mem_norm", "xa_w_q", "xa_w_kv", "xa_w_o", "xa_q_norm", "xa_k_norm", "ffn2_norm",
           "ffn2_w_in", "ffn2_w_out"]


def kernel(x_prompt, x_sample, mem_prompt, mem_sample, **weights):
    n = 8
    S_list = (x_prompt.shape[1], x_sample.shape[1])
    nc = build(S_list)
    consts = _consts(max(S_list))
    wmap = {nm: np.ascontiguousarray(np.asarray(weights[nm], dtype=np.float32)) for nm in _WNAMES}
    in_maps = []
    for c in range(n):
        m = dict(wmap)
        m.update(consts)
        m["x0"] = np.ascontiguousarray(np.asarray(x_prompt[c], dtype=np.float32))
        m["x1"] = np.ascontiguousarray(np.asarray(x_sample[c], dtype=np.float32))
        m["m0"] = np.ascontiguousarray(np.asarray(mem_prompt[c], dtype=np.float32))
        m["m1"] = np.ascontiguousarray(np.asarray(mem_sample[c], dtype=np.float32))
        in_maps.append(m)
    res = run_bass_kernel_spmd(nc, in_maps, core_ids=list(range(n)))
    y0 = np.stack([np.asarray(r["y0"], dtype=np.float32) for r in res.results], axis=0)
    y1 = np.stack([np.asarray(r["y1"], dtype=np.float32) for r in res.results], axis=0)
    return (y0, y1)
```
